# Optimizing a Trainium2 kernel written in Bass

```python
import math
import jax, jax.numpy as jnp
from jax import lax
import numpy as np

D_MODEL = 4096
BATCH = 1
SEQ = 8192
DEPTH = 1

HEAD_DIM = 128
MIX_WIDTH = D_MODEL
MIX_HEADS = MIX_WIDTH // HEAD_DIM
A_Q_HEADS = MIX_HEADS // 2
A_KV_HEADS = 4
A_WIDTH = A_Q_HEADS * HEAD_DIM
A_WINDOW = 128
B_Q_HEADS = MIX_HEADS - A_Q_HEADS
B_KV_HEADS = 4
B_WIDTH = B_Q_HEADS * HEAD_DIM
CMP_BLOCK = 32
CMP_STRIDE = 16
CMP_HIDDEN = 256
SLC_BLOCK = 64
SLC_TOPK = 16
B_WINDOW = 512
N_BRANCH = 3
QBLOCK = 128
N_BUCKETS = 32
MAX_DISTANCE = 128
D_FF = 11008
CONV_WIDTH = 3
EPS = 1e-6
NEG = -1e30
FORCE_SCORE = 1e6
SCALE = HEAD_DIM ** -0.5
IN_SIZES = [A_WIDTH, A_KV_HEADS * HEAD_DIM, A_KV_HEADS * HEAD_DIM,
            B_WIDTH] + [B_KV_HEADS * HEAD_DIM] * 6 + [N_BRANCH * B_Q_HEADS]
N_IN = A_WIDTH + 2 * A_KV_HEADS * HEAD_DIM + B_WIDTH + 6 * B_KV_HEADS * HEAD_DIM + N_BRANCH * B_Q_HEADS

kernel_name = "hymba_style_swa_sinks_nsa_convffn"


def rms_norm(x, g):
    xf = x.astype(jnp.float32)
    y = xf * lax.rsqrt(jnp.mean(xf * xf, axis=-1, keepdims=True) + EPS)
    return (y * g.astype(jnp.float32)).astype(x.dtype)


def t5_bucket(dist):
    n = jnp.maximum(dist, 0)
    max_exact = N_BUCKETS // 2
    nf = jnp.maximum(n, 1).astype(jnp.float32)
    large = max_exact + (jnp.log(nf / max_exact) / math.log(MAX_DISTANCE / max_exact)
                         * (N_BUCKETS - max_exact)).astype(jnp.int32)
    large = jnp.minimum(large, N_BUCKETS - 1)
    return jnp.where(n < max_exact, n, large)


def banded_attention(q, k, v, tab, window, sinks):
    B, S, H, D = q.shape
    G = k.shape[2]
    R = H // G
    nb = S // QBLOCK
    nback = -(-(window - 1) // QBLOCK)
    K = (nback + 1) * QBLOCK
    qb = q.reshape(B, nb, QBLOCK, G, R, D)

    def band(t):
        tb = t.reshape(B, nb, QBLOCK, G, D)
        parts = [jnp.pad(tb, ((0, 0), (j, 0), (0, 0), (0, 0), (0, 0)))[:, :nb]
                 for j in range(nback, -1, -1)]
        return jnp.concatenate(parts, axis=2)

    kb, vb = band(k), band(v)
    rel = (jnp.arange(QBLOCK)[:, None] + nback * QBLOCK) - jnp.arange(K)[None, :]
    kglob = jnp.arange(nb)[:, None] * QBLOCK - nback * QBLOCK + jnp.arange(K)[None, :]
    mask = ((rel >= 0) & (rel < window))[None] & (kglob >= 0)[:, None, :]
    bias = jnp.moveaxis(tab[t5_bucket(rel)], -1, 0).reshape(G, R, QBLOCK, K).astype(jnp.float32)
    logits = jnp.einsum('bcqgrd,bckgd->bcgrqk', qb, kb).astype(jnp.float32) * SCALE + bias
    logits = jnp.where(mask[:, None, None], logits, NEG)
    if sinks is not None:
        s = sinks.astype(jnp.float32).reshape(G, R)[:, :, None, None]
        m = jnp.maximum(jnp.max(logits, axis=-1, keepdims=True), s)
        e = jnp.exp(logits - m)
        p = e / (jnp.sum(e, axis=-1, keepdims=True) + jnp.exp(s - m))
    else:
        p = jax.nn.softmax(logits, axis=-1)
    out = jnp.einsum('bcgrqk,bckgd->bcqgrd', p.astype(v.dtype), vb)
    return out.reshape(B, S, H, D)


def compress(t, pos_emb, w1, b1, w2, b2):
    B, S, G, D = t.shape
    n = (S - CMP_BLOCK) // CMP_STRIDE + 1
    idx = np.arange(n)[:, None] * CMP_STRIDE + np.arange(CMP_BLOCK)[None, :]
    blocks = t[:, idx] + pos_emb[:, None, :]
    blocks = jnp.moveaxis(blocks, 3, 2).reshape(B, n, G, CMP_BLOCK * D)
    return jax.nn.gelu(blocks @ w1 + b1) @ w2 + b2


def compressed_attention(q, kc, vc, tab):
    B, S, H, D = q.shape
    N, G = kc.shape[1], kc.shape[2]
    R = H // G
    qg = q.reshape(B, S, G, R, D)
    end = jnp.arange(N) * CMP_STRIDE + CMP_BLOCK - 1
    dist = jnp.arange(S)[:, None] - end[None, :]
    valid = dist >= 0
    bias = jnp.moveaxis(tab[t5_bucket(dist)], -1, 0).reshape(G, R, S, N).astype(jnp.float32)
    logits = jnp.einsum('bsgrd,bngd->bgrsn', qg, kc).astype(jnp.float32) * SCALE + bias
    logits = jnp.where(valid, logits, NEG)
    has_any = jnp.any(valid, axis=-1)[:, None].astype(jnp.float32)
    p = jax.nn.softmax(logits, axis=-1) * has_any
    out = jnp.einsum('bgrsn,bngd->bsgrd', p.astype(vc.dtype), vc).reshape(B, S, H, D)
    return out, jnp.sum(p, axis=2)


def select_blocks(p_grp, S):
    n_slc = S // SLC_BLOCK
    N = p_grp.shape[-1]
    i = np.arange(N)[:, None]
    j = np.arange(n_slc)[None, :]
    overlap = ((i * CMP_STRIDE <= j * SLC_BLOCK + SLC_BLOCK - 1)
               & (i * CMP_STRIDE + CMP_BLOCK - 1 >= j * SLC_BLOCK)).astype(np.float32)
    scores = jnp.einsum('bgsn,nj->bgsj', p_grp, jnp.asarray(overlap, p_grp.dtype))
    cur = jnp.arange(S)[:, None] // SLC_BLOCK
    blk = jnp.arange(n_slc)[None, :]
    forced = (blk == 0) | (blk == cur) | (blk == cur - 1)
    scores = jnp.where(forced, FORCE_SCORE, jnp.where(blk <= cur, scores, -1.0))
    _, idx = lax.top_k(scores, min(SLC_TOPK, n_slc))
    return idx


def selected_attention(q, ks, vs, idx, tab):
    B, S, H, D = q.shape
    G = ks.shape[2]
    R = H // G
    k_sel = idx.shape[-1]
    n_slc = S // SLC_BLOCK
    nc = S // QBLOCK
    kblk = ks.reshape(B, n_slc, SLC_BLOCK, G, D).transpose(0, 3, 1, 2, 4)
    vblk = vs.reshape(B, n_slc, SLC_BLOCK, G, D).transpose(0, 3, 1, 2, 4)
    tab_gr = tab.T.reshape(G, R, N_BUCKETS)
    q_c = jnp.moveaxis(q.reshape(B, nc, QBLOCK, G, R, D), 1, 0)
    idx_c = jnp.moveaxis(idx.reshape(B, G, nc, QBLOCK, k_sel), 2, 0)
    starts = jnp.arange(nc) * QBLOCK
    bi = jnp.arange(B)[:, None, None, None]
    gi = jnp.arange(G)[None, :, None, None]
    offs = jnp.arange(SLC_BLOCK)
    g5 = jnp.arange(G)[None, :, None, None, None]
    r5 = jnp.arange(R)[None, None, :, None, None]

    def one_block(args):
        qc, ic, s0 = args
        kg = kblk[bi, gi, ic].reshape(B, G, QBLOCK, k_sel * SLC_BLOCK, D)
        vg = vblk[bi, gi, ic].reshape(B, G, QBLOCK, k_sel * SLC_BLOCK, D)
        tok = (ic[..., None] * SLC_BLOCK + offs).reshape(B, G, QBLOCK, k_sel * SLC_BLOCK)
        dist = (s0 + jnp.arange(QBLOCK))[None, None, :, None] - tok
        bias = tab_gr[g5, r5, t5_bucket(dist)[:, :, None]].astype(jnp.float32)
        logits = jnp.einsum('bqgrd,bgqkd->bgrqk', qc, kg).astype(jnp.float32) * SCALE + bias
        logits = jnp.where((dist >= 0)[:, :, None], logits, NEG)
        p = jax.nn.softmax(logits, axis=-1)
        return jnp.einsum('bgrqk,bgqkd->bqgrd', p.astype(vg.dtype), vg)

    out = lax.map(one_block, (q_c, idx_c, starts))
    return jnp.moveaxis(out, 0, 1).reshape(B, S, H, D)


def causal_dwconv(u, w, b):
    S = u.shape[1]
    up = jnp.pad(u, ((0, 0), (CONV_WIDTH - 1, 0), (0, 0)))
    y = b
    for j in range(CONV_WIDTH):
        y = y + up[:, j:j + S] * w[j]
    return y


def hybrid_layer(x, rel_bias, norm_mix_g, w_in, a_q_norm_g, a_k_norm_g, a_sinks, b_q_norm_g,
                 b_k_norm_g, cmp_pos_emb, cmp_w1, cmp_b1, cmp_w2, cmp_b2, out_norm_g, w_out,
                 norm_ffn_g, w_gate, w_up, conv_w, conv_b, w_down):
    B, S, _ = x.shape
    hn = rms_norm(x, norm_mix_g)
    proj = hn @ w_in
    splits = np.cumsum(IN_SIZES)[:-1].tolist()
    aq, ak, av, bq, bkc, bvc, bks, bvs, bkw, bvw, bg = jnp.split(proj, splits, axis=-1)

    def heads(t, n):
        return t.reshape(B, S, n, HEAD_DIM)

    tab_a = rel_bias[:, :A_Q_HEADS]
    tab_b = rel_bias[:, A_Q_HEADS:]

    o_a = banded_attention(rms_norm(heads(aq, A_Q_HEADS), a_q_norm_g),
                           rms_norm(heads(ak, A_KV_HEADS), a_k_norm_g),
                           heads(av, A_KV_HEADS), tab_a, A_WINDOW, a_sinks)

    q = rms_norm(heads(bq, B_Q_HEADS), b_q_norm_g)
    kc = rms_norm(compress(heads(bkc, B_KV_HEADS), cmp_pos_emb[0], cmp_w1[0], cmp_b1[0],
                           cmp_w2[0], cmp_b2[0]), b_k_norm_g[0])
    vc = compress(heads(bvc, B_KV_HEADS), cmp_pos_emb[1], cmp_w1[1], cmp_b1[1], cmp_w2[1], cmp_b2[1])
    o_cmp, p_grp = compressed_attention(q, kc, vc, tab_b)
    sel_idx = select_blocks(p_grp, S)
    o_slc = selected_attention(q, rms_norm(heads(bks, B_KV_HEADS), b_k_norm_g[1]),
                               heads(bvs, B_KV_HEADS), sel_idx, tab_b)
    o_win = banded_attention(q, rms_norm(heads(bkw, B_KV_HEADS), b_k_norm_g[2]),
                             heads(bvw, B_KV_HEADS), tab_b, B_WINDOW, None)
    gates = jax.nn.sigmoid(bg.astype(jnp.float32)).reshape(B, S, B_Q_HEADS, N_BRANCH).astype(x.dtype)
    o_b = gates[..., 0:1] * o_cmp + gates[..., 1:2] * o_slc + gates[..., 2:3] * o_win

    o_a = rms_norm(o_a.reshape(B, S, A_WIDTH), out_norm_g[:A_WIDTH])
    o_b = rms_norm(o_b.reshape(B, S, B_WIDTH), out_norm_g[A_WIDTH:])
    x = x + jnp.concatenate([o_a, o_b], axis=-1) @ w_out

    hf = rms_norm(x, norm_ffn_g)
    g = causal_dwconv(hf @ w_gate, conv_w, conv_b)
    return x + (jax.nn.silu(g) * (hf @ w_up)) @ w_down


def setup_inputs(seed: int = 0) -> dict:
    key = jax.random.key(seed)
    ks = jax.random.split(key, 24)
    f32 = jnp.float32
    L = DEPTH

    def nrm(k, shape, scale):
        return jax.random.normal(k, shape, f32) * scale

    def gain(k, shape):
        return 1.0 + 0.05 * jax.random.normal(k, shape, f32)

    return {
        "x": nrm(ks[0], (BATCH, SEQ, D_MODEL), 1.0),
        "rel_bias": nrm(ks[1], (N_BUCKETS, MIX_HEADS), 0.5),
        "norm_mix_g": gain(ks[2], (L, D_MODEL)),
        "w_in": nrm(ks[3], (L, D_MODEL, N_IN), D_MODEL ** -0.5),
        "a_q_norm_g": gain(ks[4], (L, HEAD_DIM)),
        "a_k_norm_g": gain(ks[5], (L, HEAD_DIM)),
        "a_sinks": nrm(ks[6], (L, A_Q_HEADS), 0.5),
        "b_q_norm_g": gain(ks[7], (L, HEAD_DIM)),
        "b_k_norm_g": gain(ks[8], (L, N_BRANCH, HEAD_DIM)),
        "cmp_pos_emb": nrm(ks[9], (L, 2, CMP_BLOCK, HEAD_DIM), 0.1),
        "cmp_w1": nrm(ks[10], (L, 2, CMP_BLOCK * HEAD_DIM, CMP_HIDDEN), (CMP_BLOCK * HEAD_DIM) ** -0.5),
        "cmp_b1": nrm(ks[11], (L, 2, CMP_HIDDEN), 0.02),
        "cmp_w2": nrm(ks[12], (L, 2, CMP_HIDDEN, HEAD_DIM), CMP_HIDDEN ** -0.5),
        "cmp_b2": nrm(ks[13], (L, 2, HEAD_DIM), 0.02),
        "out_norm_g": gain(ks[14], (L, MIX_WIDTH)),
        "w_out": nrm(ks[15], (L, MIX_WIDTH, D_MODEL), MIX_WIDTH ** -0.5),
        "norm_ffn_g": gain(ks[16], (L, D_MODEL)),
        "w_gate": nrm(ks[17], (L, D_MODEL, D_FF), D_MODEL ** -0.5),
        "w_up": nrm(ks[18], (L, D_MODEL, D_FF), D_MODEL ** -0.5),
        "conv_w": nrm(ks[19], (L, CONV_WIDTH, D_FF), CONV_WIDTH ** -0.5),
        "conv_b": nrm(ks[20], (L, D_FF), 0.02),
        "w_down": nrm(ks[21], (L, D_FF, D_MODEL), D_FF ** -0.5),
    }


def reference(x, rel_bias, norm_mix_g, w_in, a_q_norm_g, a_k_norm_g, a_sinks, b_q_norm_g,
              b_k_norm_g, cmp_pos_emb, cmp_w1, cmp_b1, cmp_w2, cmp_b2, out_norm_g, w_out,
              norm_ffn_g, w_gate, w_up, conv_w, conv_b, w_down):
    h = x
    for l in range(DEPTH):
        h = hybrid_layer(h, rel_bias, norm_mix_g[l], w_in[l], a_q_norm_g[l], a_k_norm_g[l],
                         a_sinks[l], b_q_norm_g[l], b_k_norm_g[l], cmp_pos_emb[l], cmp_w1[l],
                         cmp_b1[l], cmp_w2[l], cmp_b2[l], out_norm_g[l], w_out[l], norm_ffn_g[l],
                         w_gate[l], w_up[l], conv_w[l], conv_b[l], w_down[l])
    return h
```

```python
import contextlib
import math
import types
import numpy as np
import ml_dtypes
import concourse.bass as bass
import concourse.mybir as mybir
from concourse.bass_utils import run_bass_kernel_spmd

F32 = mybir.dt.float32
BF16 = mybir.dt.bfloat16
AF = mybir.ActivationFunctionType
ALU = mybir.AluOpType
AX = mybir.AxisListType

NCORES = 8
D = 4096
S_ALL = 8192
TOK = 1024
DFF = 11008
NFC = 86
NIN = 8240
EPS = 1e-6
NEGM = -1.0e4
SCALE = 128 ** -0.5
NQT = 9

ENGS = ("pe", "act", "dve", "pool", "sp")
NDMASEM = 8


def _freeze(fn):
    if fn.__closure__ is None:
        return fn
    cells = []
    for c in fn.__closure__:
        try:
            cells.append(types.CellType(c.cell_contents))
        except ValueError:
            cells.append(c)
    return types.FunctionType(fn.__code__, fn.__globals__, fn.__name__, fn.__defaults__, tuple(cells))


class Op:
    __slots__ = ("eng", "fn", "reads", "writes", "dma", "idx", "deps", "signal",
                 "count", "dsem", "dcount", "dprev")

    def __init__(self, eng, fn, reads, writes, dma):
        self.eng, self.fn, self.reads, self.writes, self.dma = eng, fn, reads, writes, dma
        self.deps = []
        self.signal = False
        self.count = 0
        self.dsem = None
        self.dcount = 0
        self.dprev = 0


class Sched:
    def __init__(self, nc):
        self.nc = nc
        self.ops = {e: [] for e in ENGS}
        self.lastw = {}
        self.readers = {}
        self.ndma = {e: 0 for e in ENGS}
        self.dsem_n = {}
        self.dsem_last = {}
        self.fence_deps = {}

    def add(self, eng, fn, reads=(), writes=(), dma=False):
        op = Op(eng, _freeze(fn), tuple(reads), tuple(writes), dma)
        op.idx = len(self.ops[eng])
        deps = set()
        for k in op.reads:
            w = self.lastw.get(k)
            if w is not None:
                deps.add(w)
        for k in op.writes:
            w = self.lastw.get(k)
            if w is not None:
                deps.add(w)
            for r in self.readers.get(k, ()):
                deps.add(r)
        if eng in self.fence_deps:
            deps.update(self.fence_deps.pop(eng))
        deps.discard(op)
        op.deps = list(deps)
        for k in op.reads:
            self.readers.setdefault(k, []).append(op)
        for k in op.writes:
            self.lastw[k] = op
            self.readers[k] = []
        if dma:
            i = self.ndma[eng]
            self.ndma[eng] += 1
            op.dsem = (eng, i % NDMASEM)
            n = self.dsem_n.get(op.dsem, 0)
            op.dprev = n
            op.dcount = n + 1
            self.dsem_n[op.dsem] = n + 1
            self.dsem_last[op.dsem] = op
        self.ops[eng].append(op)
        return op

    def fence(self):
        last = []
        for e in ENGS:
            for op in reversed(self.ops[e]):
                if not op.dma:
                    last.append(op)
                    break
        last.extend(self.dsem_last.values())
        for e in ENGS:
            self.fence_deps.setdefault(e, set()).update(last)
        self.lastw = {}
        self.readers = {}

    def emit(self):
        nc = self.nc
        for e in ENGS:
            for op in self.ops[e]:
                for d in op.deps:
                    if d.dma or (d.eng == e and e == "pe" and not op.dma):
                        continue
                    d.signal = True
        for e in ENGS:
            c = 0
            for op in self.ops[e]:
                if op.signal and not op.dma:
                    c += 1
                    op.count = c
        with contextlib.ExitStack() as st:
            csem = {e: st.enter_context(nc.semaphore("c_" + e)) for e in ENGS}
            dsem = {}
            for e in ENGS:
                if self.ndma[e]:
                    for i in range(NDMASEM):
                        dsem[(e, i)] = st.enter_context(nc.semaphore("d_%s%d" % (e, i)))
            block = st.enter_context(nc.Block())

            def run(e, eng):
                seen = {}

                def wait(sem, key, val):
                    if seen.get(key, 0) >= val:
                        return
                    seen[key] = val
                    eng.wait_ge(sem, val)

                for op in self.ops[e]:
                    for d in op.deps:
                        if d.dma:
                            wait(dsem[d.dsem], d.dsem, 16 * d.dcount)
                        elif d.eng == e and e == "pe" and not op.dma:
                            continue
                        else:
                            wait(csem[d.eng], d.eng, d.count)
                    if op.dma and op.dprev:
                        wait(dsem[op.dsem], op.dsem, 16 * op.dprev)
                    ins = op.fn(eng)
                    if op.dma:
                        ins.then_inc(dsem[op.dsem], 16)
                    elif op.signal:
                        ins.then_inc(csem[e], 1)
                if e == "sp":
                    for k, n in self.dsem_n.items():
                        eng.wait_ge(dsem[k], 16 * n)
                    for e2 in ENGS:
                        ops2 = [o for o in self.ops[e2] if o.signal and not o.dma]
                        if ops2:
                            eng.wait_ge(csem[e2], ops2[-1].count)

            @block.tensor
            def _(eng):
                run("pe", eng)

            @block.scalar
            def _(eng):
                run("act", eng)

            @block.vector
            def _(eng):
                run("dve", eng)

            @block.gpsimd
            def _(eng):
                run("pool", eng)

            @block.sync
            def _(eng):
                run("sp", eng)


class Arena:
    def __init__(self, t32, tbf, nwords):
        self.t32, self.tbf, self.n = t32, tbf, nwords
        self.off = 0
        self.uid = 0

    def mark(self):
        return self.off

    def reset(self, m=0):
        self.off = m

    def _take(self, words):
        o = self.off
        self.off += (words + 31) // 32 * 32
        assert self.off <= self.n, ("arena overflow", self.off, self.n)
        self.uid += 1
        return o

    def f32(self, n):
        o = self._take(n)
        return self.t32[:, o:o + n]

    def bf(self, n):
        o = self._take((n + 1) // 2)
        return self.tbf[:, 2 * o:2 * o + n]


def build_nc(dbg=False, a1_only=False, st_list=None):
    nc = bass.Bass("TRN2", target_bir_lowering=False)

    def din(name, shape, dt=F32):
        return nc.dram_tensor(name, list(shape), dt, kind="ExternalInput").ap()

    def dscr(name, shape, dt):
        return nc.dram_tensor(name, list(shape), dt, kind="ExternalOutput" if dbg else "Internal").ap()

    x = din("x", [S_ALL, D])
    w_in = din("w_in", [D, NIN])
    w_out = din("w_out", [D, D])
    w_gate = din("w_gate", [D, DFF])
    w_up = din("w_up", [D, DFF])
    w_down = din("w_down", [DFF, D])
    gmix = din("gmix", [1, D])
    gout = din("gout", [1, D])
    gffn = din("gffn", [1, D])
    hg = din("hg", [128, 8])
    sinks = din("sinks", [1, 16])
    posT = din("posT", [2, 128, 32])
    cw1 = din("cw1", [2, 4096, 256])
    cb1 = din("cb1", [2, 128, 2])
    cw2 = din("cw2", [2, 256, 128])
    cb2c = din("cb2c", [128, 2])
    cb2r = din("cb2r", [1, 128])
    convw = din("convw", [128, NFC * 3])
    convb = din("convb", [128, NFC])
    tabA = din("tabA", [4, 2, 128, 512])
    tabB = din("tabB", [4, 4, 128, 512])
    tabC = din("tabC", [NQT, 4, 128, 512])
    cfar = din("cfar", [128, NQT * 16])
    selV = din("selV", [128, NQT * 128])
    selA = din("selA", [128, NQT * 128])
    selF = din("selF", [128, NQT * 128])
    padc = din("padc", [128, 4])
    wpad = din("wpad", [128, 13])
    hflag = din("hflag", [128, 2])
    emast = din("emast", [128, S_ALL])
    ovl = din("ovl", [128, 4 * 128])
    identf = din("identf", [128, 128])
    y = nc.dram_tensor("y", [TOK, D], F32, kind="ExternalOutput").ap()

    qT_d = dscr("qT_d", [32, 128, 1536], BF16)
    akT_d = dscr("akT_d", [4, 128, 1536], BF16)
    av_d = dscr("av_d", [1536, 512], BF16)
    kwT_d = dscr("kwT_d", [4, 128, 2048], BF16)
    vw_d = dscr("vw_d", [2048, 512], BF16)
    ksT_d = dscr("ksT_d", [4, 128, S_ALL], BF16)
    vs_d = dscr("vs_d", [S_ALL, 512], BF16)
    kcr_d = dscr("kcr_d", [4, 128, S_ALL + 16], BF16)
    vcr_d = dscr("vcr_d", [4, 128, S_ALL + 16], BF16)
    kcT_d = dscr("kcT_d", [4, 128, 512], BF16)
    vc_d = dscr("vc_d", [4, 512, 128], BF16)
    o_d = dscr("o_d", [NQT * 128, D], BF16)
    x1_d = dscr("x1_d", [NQT * 128, D], F32)
    hT_d = dscr("hT_d", [NFC, 128, TOK], BF16)

    S = Sched(nc)
    NW = 50500
    with contextlib.ExitStack() as st:
        ar32 = st.enter_context(nc.sbuf_tensor("arena", [128, NW], F32))
        AR = Arena(ar32, ar32.bitcast(BF16), NW)
        banks = [st.enter_context(nc.psum_tensor("pb%d" % i, [128, 512], F32)) for i in range(8)]
        banks_bf = [b.bitcast(BF16) for b in banks]
        uid = [0]

        def key(p):
            uid[0] += 1
            return "%s#%d" % (p, uid[0])

        ident = AR.bf(128)
        ones = AR.bf(128)
        onec = AR.bf(2)
        gates = AR.f32(12 * 48)
        ssqA = AR.f32(NQT * 4)
        ssqB = AR.f32(NQT * 4)
        ssqF = AR.f32(NQT * 8)
        hgc = AR.f32(8)
        hgq = AR.f32(2)
        small = AR.f32(64)
        epsc = AR.f32(2)
        base_mark = AR.mark()
        S.add("dve", lambda e: e.memset(epsc, EPS), writes=["epsc"])

        S.add("pool", lambda e: e.dma_start(out=ident, in_=identf[:, :]), writes=["ident"], dma=True)
        S.add("dve", lambda e: e.memset(ones, 1.0), writes=["ones"])
        S.add("dve", lambda e: e.memset(onec, 1.0), writes=["onec"])
        S.add("sp", lambda e: e.dma_start(out=hgc, in_=hg[:, :]), writes=["hgc"], dma=True)
        S.add("dve", lambda e: e.tensor_scalar(out=hgq[:, 0:1], in0=hgc[:, 0:1], scalar1=SCALE, scalar2=None,
                                               op0=ALU.mult), reads=["hgc"], writes=["hgq0"])
        S.add("dve", lambda e: e.tensor_scalar(out=hgq[:, 1:2], in0=hgc[:, 2:3], scalar1=SCALE, scalar2=None,
                                               op0=ALU.mult), reads=["hgc"], writes=["hgq1"])
        S.add("dve", lambda e: e.memset(ssqA, 0.0), writes=["ssqA"])
        S.add("dve", lambda e: e.memset(ssqB, 0.0), writes=["ssqB"])
        S.add("dve", lambda e: e.memset(ssqF, 0.0), writes=["ssqF"])

        evac_rr = [0]

        def evac_copy(out, in_, reads, writes):
            evac_rr[0] += 1
            if evac_rr[0] % 2:
                S.add("act", lambda e: e.copy(out=out, in_=in_), reads=reads, writes=writes)
            else:
                S.add("dve", lambda e: e.tensor_copy(out=out, in_=in_), reads=reads, writes=writes)

        def rstd_from(out, in_, inv_n, reads, writes, tmp):
            S.add("act", lambda e: e.activation(out=tmp, in_=in_, func=AF.Sqrt, scale=inv_n, bias=epsc[:, 0:1]),
                  reads=list(reads) + ["epsc"], writes=[writes[0] + "t"])
            S.add("dve", lambda e: e.reciprocal(out=out, in_=tmp), reads=[writes[0] + "t"], writes=writes)

        gbc = AR.f32(D)
        S.add("sp", lambda e: e.dma_start(out=gbc, in_=gmix[0].partition_broadcast(128)), writes=["gbc"], dma=True)
        xbuf = [AR.f32(D) for _ in range(2)]
        xs = AR.bf(D)
        hnT = AR.bf(32 * 512)
        hnT3 = hnT.rearrange("p (k t) -> p k t", t=512)
        wbuf = [AR.bf(32 * 512) for _ in range(2)]
        wbuf3 = [w.rearrange("p (k c) -> p k c", c=512) for w in wbuf]
        sqb = [AR.bf(512) for _ in range(2)]
        rsb = [AR.f32(512) for _ in range(2)]
        rtmp = [AR.f32(512) for _ in range(2)]
        stg = [AR.bf(512) for _ in range(3)]
        zer = AR.bf(16)
        stat = AR.f32(4)
        S.add("dve", lambda e: e.memset(zer, 0.0), writes=["zer"])
        for g in range(4):
            for dd in (kcr_d, vcr_d):
                S.add("sp", (lambda dd, g: lambda e: e.dma_start(out=dd[g][:, S_ALL:S_ALL + 16], in_=zer))(dd, g),
                      reads=["zer"], dma=True)

        chunks = []
        for i in range(4):
            chunks.append((i * 512, 512, "q", 13, (i * 4, 0)))
        chunks.append((2048, 512, "k", 13, ("ak",)))
        chunks.append((2560, 512, "v", 13, ("av",)))
        for i in range(4):
            chunks.append((3072 + i * 512, 512, "q", 13, (16 + i * 4, 1)))
        chunks.append((5120, 512, "raw", 0, (kcr_d,)))
        chunks.append((5632, 512, "raw", 0, (vcr_d,)))
        chunks.append((6144, 512, "k", 0, ("ks",)))
        chunks.append((6656, 512, "v", 0, ("vs",)))
        chunks.append((7168, 512, "k", 12, ("kw",)))
        chunks.append((7680, 512, "v", 12, ("vw",)))
        chunks.append((8192, 48, "g", 13, ()))

        wcnt = [0]
        pscnt = [0]
        st3 = [0]

        def proj_chunk(sti, ch):
            c0, ncol, kind, _, meta = ch
            wi = wcnt[0] % 2
            wcnt[0] += 1
            wk = "w%d" % wi
            w3 = wbuf3[wi]
            S.add("pool", lambda e: e.dma_start(out=w3[:, :, 0:ncol],
                                                in_=w_in[:, c0:c0 + ncol].rearrange("(k p) c -> p k c", p=128)),
                  writes=[wk], dma=True)
            if kind in ("q", "k", "raw"):
                for hh in range(4):
                    bi = 2 + pscnt[0] % 3
                    pscnt[0] += 1
                    pb = banks[bi]
                    pk = "bank%d" % bi
                    for kc in range(32):
                        S.add("pe", (lambda kc, hh, pb: lambda e: e.matmul(
                            pb[:, :], lhsT=w3[:, kc, hh * 128:(hh + 1) * 128], rhs=hnT3[:, kc, :],
                            start=(kc == 0), stop=(kc == 31)))(kc, hh, pb),
                            reads=[wk, "hnT"], writes=[pk])
                    si = st3[0] % 3
                    st3[0] += 1
                    sg = stg[si]
                    sk = "stg%d" % si
                    if kind == "raw":
                        evac_copy(sg, pb[:, :], [pk], [sk])
                        dd = meta[0]
                        S.add("sp", (lambda dd, hh, sg: lambda e: e.dma_start(
                            out=dd[hh][:, sti * 512:(sti + 1) * 512], in_=sg))(dd, hh, sg), reads=[sk], dma=True)
                        continue
                    j = si % 2
                    S.add("act", (lambda pb, j: lambda e: e.activation(out=sqb[j], in_=pb[:, :], func=AF.Square))(pb, j),
                          reads=[pk], writes=["sqb%d" % j])
                    b2 = 5 + j
                    S.add("pe", (lambda j, b2: lambda e: e.matmul(banks[b2][:, :], lhsT=ones, rhs=sqb[j],
                                                                  start=True, stop=True))(j, b2),
                          reads=["ones", "sqb%d" % j], writes=["bank%d" % b2])
                    rstd_from(rsb[j], banks[b2][:, :], 1.0 / 128, ["bank%d" % b2], ["rsb%d" % j], rtmp[j])
                    if kind == "q":
                        gcol = hgq[:, meta[1]:meta[1] + 1]
                        gk = "hgq%d" % meta[1]
                    else:
                        ci = {"ak": 1, "ks": 4, "kw": 5}[meta[0]]
                        gcol = hgc[:, ci:ci + 1]
                        gk = "hgc"
                    S.add("dve", (lambda pb, j, sg, gcol: lambda e: e.scalar_tensor_tensor(
                        out=sg, in0=pb[:, :], scalar=gcol, in1=rsb[j], op0=ALU.mult, op1=ALU.mult))(pb, j, sg, gcol),
                        reads=[pk, "rsb%d" % j, gk], writes=[sk])
                    if kind == "q":
                        dst = qT_d[meta[0] + hh][:, (sti - 13) * 512:(sti - 12) * 512]
                    elif meta[0] == "ak":
                        dst = akT_d[hh][:, (sti - 13) * 512:(sti - 12) * 512]
                    elif meta[0] == "ks":
                        dst = ksT_d[hh][:, sti * 512:(sti + 1) * 512]
                    else:
                        dst = kwT_d[hh][:, (sti - 12) * 512:(sti - 11) * 512]
                    S.add("sp", (lambda dst, sg: lambda e: e.dma_start(out=dst, in_=sg))(dst, sg), reads=[sk], dma=True)
            else:
                for tt in range(4):
                    bi = 2 + pscnt[0] % 3
                    pscnt[0] += 1
                    pb = banks[bi]
                    pk = "bank%d" % bi
                    for kc in range(32):
                        S.add("pe", (lambda kc, tt, pb: lambda e: e.matmul(
                            pb[:, 0:ncol], lhsT=hnT3[:, kc, tt * 128:(tt + 1) * 128], rhs=w3[:, kc, 0:ncol],
                            start=(kc == 0), stop=(kc == 31)))(kc, tt, pb),
                            reads=[wk, "hnT"], writes=[pk])
                    if kind == "g":
                        ti = (sti - 13) * 4 + tt
                        S.add("act", (lambda pb, ti: lambda e: e.activation(
                            out=gates[:, ti * 48:(ti + 1) * 48], in_=pb[:, 0:48], func=AF.Sigmoid))(pb, ti),
                            reads=[pk], writes=["gates%d" % ti])
                        continue
                    si = st3[0] % 3
                    st3[0] += 1
                    sg = stg[si]
                    sk = "stg%d" % si
                    evac_copy(sg, pb[:, :], [pk], [sk])
                    r0 = sti * 512 + tt * 128
                    if meta[0] == "av":
                        dst = av_d[r0 - 13 * 512:r0 - 13 * 512 + 128, :]
                    elif meta[0] == "vs":
                        dst = vs_d[r0:r0 + 128, :]
                    else:
                        dst = vw_d[r0 - 12 * 512:r0 - 12 * 512 + 128, :]
                    S.add("sp", (lambda dst, sg: lambda e: e.dma_start(out=dst, in_=sg))(dst, sg), reads=[sk], dma=True)

        def build_norm_T(src_rows, gb, dst3, tcol, xk_i, stat_ap, extra_scale=None):
            xb_ = xbuf[xk_i % 2]
            xk = "xbuf%d" % (xk_i % 2)
            S.add("sp", lambda e: e.dma_start(out=xb_, in_=src_rows), writes=[xk], dma=True)
            S.add("dve", lambda e: e.memset(stat_ap[:, 0:1], 0.0), writes=["stat0"])
            S.add("act", lambda e: e.activation(out=xs, in_=xb_, func=AF.Square, accum_out=stat_ap[:, 0:1]),
                  reads=[xk, "stat0"], writes=["xs", "stat0"])
            rstd_from(stat_ap[:, 1:2], stat_ap[:, 0:1], 1.0 / D, ["stat0"], ["stat1"], stat_ap[:, 2:3])
            S.add("dve", lambda e: e.scalar_tensor_tensor(out=xs, in0=xb_, scalar=stat_ap[:, 1:2], in1=gb,
                                                          op0=ALU.mult, op1=ALU.mult),
                  reads=[xk, "stat1", "gbc"], writes=["xs"])
            for q4 in range(4):
                bi = q4 % 2
                pbf = banks_bf[bi]
                pk = "bank%d" % bi
                for k8 in range(8):
                    kc = q4 * 8 + k8
                    S.add("pe", (lambda kc, k8, pbf: lambda e: e.transpose(
                        pbf[:, k8 * 128:(k8 + 1) * 128], xs[:, kc * 128:(kc + 1) * 128], ident))(kc, k8, pbf),
                        reads=["xs", "ident"], writes=[pk])
                yield q4, pbf, pk

        def norm_tile_to(src_rows, gb, dst3, dkey, tcol, xk_i, ncols=128, src_c0=0):
            for q4, pbf, pk in build_norm_T(src_rows, gb, dst3, tcol, xk_i, stat):
                evac_copy(dst3[:, q4 * 8:(q4 + 1) * 8, tcol:tcol + ncols],
                          pbf.rearrange("p (k t) -> p k t", t=128)[:, :, src_c0:src_c0 + ncols], [pk], [dkey])

        tile_i = 0
        hdbg = nc.dram_tensor("hdbg", [16, 128, 32 * 512], BF16, kind="ExternalOutput").ap() if a1_only else None
        for sti in (st_list if st_list is not None else range(16)):
            for tt in range(4):
                tl = sti * 4 + tt
                norm_tile_to(x[tl * 128:(tl + 1) * 128, :], gbc, hnT3, "hnT", tt * 128, tile_i)
                tile_i += 1
            if a1_only == 1:
                S.add("sp", (lambda sti: lambda e: e.dma_start(out=hdbg[sti], in_=hnT))(sti), reads=["hnT"], dma=True)
            for ch in chunks:
                if sti >= ch[3]:
                    proj_chunk(sti, ch)
        S.fence()
        if a1_only:
            S.emit()
            return nc

        AR.reset(base_mark)
        w1 = AR.bf(32 * 256)
        w13 = w1.rearrange("p (l h) -> p l h", h=256)
        w2 = AR.bf(2 * 128)
        w23 = w2.rearrange("p (c d) -> p c d", d=128)
        posb = AR.bf(32)
        b1c = AR.f32(2)
        c1 = AR.f32(2)
        b2c = AR.f32(2)
        b2r = AR.f32(128)
        rawT = [AR.bf(S_ALL + 16) for _ in range(2)]
        hid = AR.bf(2 * 512)
        hid3 = hid.rearrange("p (c n) -> p c n", n=512)
        u_ = AR.f32(512)
        t_ = AR.f32(512)
        sg_ = AR.f32(512)
        kraw = AR.f32(512)
        sq2 = AR.bf(512)
        rs2 = AR.f32(512)
        rt2 = AR.f32(512)
        kco = AR.bf(512)
        vco = AR.bf(128)
        S.add("sp", lambda e: e.dma_start(out=b2c, in_=cb2c[:, :]), writes=["b2c"], dma=True)
        S.add("sp", lambda e: e.dma_start(out=b2r, in_=cb2r[0].partition_broadcast(128)), writes=["b2r"], dma=True)
        GC = 2.0 * math.sqrt(2.0 / math.pi)
        ri = 0
        for kv in range(2):
            S.add("pool", (lambda kv: lambda e: e.dma_start(
                out=w13, in_=cw1[kv].rearrange("(l p) h -> p l h", p=128)))(kv), writes=["w1"], dma=True)
            S.add("pool", (lambda kv: lambda e: e.dma_start(
                out=w23, in_=cw2[kv].rearrange("(c p) d -> p c d", p=128)))(kv), writes=["w2"], dma=True)
            S.add("pool", (lambda kv: lambda e: e.dma_start(out=posb, in_=posT[kv]))(kv), writes=["posb"], dma=True)
            S.add("sp", (lambda kv: lambda e: e.dma_start(out=b1c, in_=cb1[kv]))(kv), writes=["b1c"], dma=True)
            for hc in range(2):
                for l in range(32):
                    S.add("pe", (lambda hc, l: lambda e: e.matmul(
                        banks[0][:, 0:1], lhsT=w13[:, l, hc * 128:(hc + 1) * 128], rhs=posb[:, l:l + 1],
                        start=(l == 0), stop=(l == 31)))(hc, l), reads=["w1", "posb"], writes=["bank0"])
                S.add("dve", (lambda hc: lambda e: e.tensor_tensor(
                    out=c1[:, hc:hc + 1], in0=banks[0][:, 0:1], in1=b1c[:, hc:hc + 1], op=ALU.add))(hc),
                    reads=["bank0", "b1c"], writes=["c1_%d" % hc])
            for g in range(4):
                rT = rawT[ri % 2]
                rk = "rawT%d" % (ri % 2)
                ri += 1
                src = (kcr_d if kv == 0 else vcr_d)[g]
                S.add("sp", (lambda rT, src: lambda e: e.dma_start(out=rT, in_=src))(rT, src), writes=[rk], dma=True)
                for hc in range(2):
                    pb = banks[1 + hc]
                    pk = "bank%d" % (1 + hc)
                    for l in range(32):
                        S.add("pe", (lambda hc, l, pb, rT: lambda e: e.matmul(
                            pb[:, :], lhsT=w13[:, l, hc * 128:(hc + 1) * 128], rhs=rT[:, l:l + 16 * 511 + 1:16],
                            start=(l == 0), stop=(l == 31)))(hc, l, pb, rT), reads=["w1", rk], writes=[pk])
                    S.add("act", (lambda hc, pb: lambda e: e.activation(
                        out=u_, in_=pb[:, :], func=AF.Identity, bias=c1[:, hc:hc + 1]))(hc, pb),
                        reads=[pk, "c1_%d" % hc], writes=["u_"])
                    S.add("dve", lambda e: e.tensor_tensor(out=t_, in0=u_, in1=u_, op=ALU.mult),
                          reads=["u_"], writes=["t_"])
                    S.add("dve", lambda e: e.tensor_scalar(out=t_, in0=t_, scalar1=0.044715, scalar2=1.0,
                                                           op0=ALU.mult, op1=ALU.add), reads=["t_"], writes=["t_"])
                    S.add("dve", lambda e: e.tensor_tensor(out=t_, in0=t_, in1=u_, op=ALU.mult),
                          reads=["t_", "u_"], writes=["t_"])
                    S.add("act", lambda e: e.activation(out=sg_, in_=t_, func=AF.Sigmoid, scale=GC),
                          reads=["t_"], writes=["sg_"])
                    S.add("dve", (lambda hc: lambda e: e.tensor_tensor(out=hid3[:, hc, :], in0=u_, in1=sg_,
                                                                       op=ALU.mult))(hc),
                          reads=["u_", "sg_"], writes=["hid%d" % hc])
                if kv == 0:
                    for hc in range(2):
                        S.add("pe", (lambda hc: lambda e: e.matmul(
                            banks[3][:, :], lhsT=w23[:, hc, :], rhs=hid3[:, hc, :], start=(hc == 0), stop=(hc == 1)))(hc),
                            reads=["w2", "hid%d" % hc], writes=["bank3"])
                    S.add("act", lambda e: e.activation(out=kraw, in_=banks[3][:, :], func=AF.Identity,
                                                        bias=b2c[:, 0:1]), reads=["bank3", "b2c"], writes=["kraw"])
                    S.add("act", lambda e: e.activation(out=sq2, in_=kraw, func=AF.Square),
                          reads=["kraw"], writes=["sq2"])
                    S.add("pe", lambda e: e.matmul(banks[4][:, :], lhsT=ones, rhs=sq2, start=True, stop=True),
                          reads=["ones", "sq2"], writes=["bank4"])
                    rstd_from(rs2, banks[4][:, :], 1.0 / 128, ["bank4"], ["rs2"], rt2)
                    S.add("dve", lambda e: e.scalar_tensor_tensor(out=kco, in0=kraw, scalar=hgc[:, 3:4], in1=rs2,
                                                                  op0=ALU.mult, op1=ALU.mult),
                          reads=["kraw", "rs2", "hgc"], writes=["kco"])
                    S.add("sp", (lambda g: lambda e: e.dma_start(out=kcT_d[g], in_=kco))(g), reads=["kco"], dma=True)
                else:
                    for t in range(4):
                        for hc in range(2):
                            S.add("pe", (lambda hc, t: lambda e: e.matmul(
                                banks[5][:, 0:128], lhsT=hid3[:, hc, t * 128:(t + 1) * 128], rhs=w23[:, hc, :],
                                start=(hc == 0), stop=(hc == 1)))(hc, t),
                                reads=["w2", "hid%d" % hc], writes=["bank5"])
                        S.add("dve", lambda e: e.tensor_tensor(out=vco, in0=banks[5][:, 0:128], in1=b2r, op=ALU.add),
                              reads=["bank5", "b2r"], writes=["vco"])
                        S.add("sp", (lambda g, t: lambda e: e.dma_start(out=vc_d[g][t * 128:(t + 1) * 128, :],
                                                                        in_=vco))(g, t), reads=["vco"], dma=True)
        S.fence()

        AR.reset(base_mark)
        tA = AR.bf(8 * 512)
        tA3 = tA.rearrange("p (t c) -> p t c", c=512)
        tB = AR.bf(16 * 512)
        tB3 = tB.rearrange("p (t c) -> p t c", c=512)
        em = AR.bf(S_ALL)
        ov = AR.bf(512)
        ov3 = ov.rearrange("p (t b) -> p t b", b=128)
        pdc = AR.f32(4)
        wpd = AR.f32(13)
        esk = AR.f32(16)
        cfp = AR.f32(NQT * 4 * 4)
        sV = AR.f32(NQT * 128)
        sA = AR.f32(NQT * 128)
        sF = AR.f32(NQT * 128)
        qT = AR.bf(4 * 1152)
        qT4 = qT.rearrange("p (q r t) -> p q r t", r=4, t=128)
        kT = AR.bf(S_ALL)
        vv = AR.bf(64 * 128)
        vv3 = vv.rearrange("p (t d) -> p t d", d=128)
        kw = AR.bf(13 * 128)
        vw = AR.bf(13 * 128)
        vw3 = vw.rearrange("p (t d) -> p t d", d=128)
        kc = AR.bf(512)
        vcs = AR.bf(512)
        vcs3 = vcs.rearrange("p (t d) -> p t d", d=128)
        tC = [AR.bf(512) for _ in range(2)]
        PT = [AR.bf(512) for _ in range(3)]
        addT = AR.bf(512)
        ob32 = AR.f32(512)
        obb = [AR.bf(512) for _ in range(2)]
        den = AR.f32(4)
        rden = AR.f32(4)
        sc = AR.f32(4)
        fin = AR.f32(128)
        fin2 = AR.f32(128)
        m8 = AR.f32(16)
        selb = AR.bf(128)
        junk = AR.bf(512)
        oTs = AR.f32(512)
        dns = AR.f32(512)
        idf = AR.f32(128)
        S.add("sp", lambda e: e.dma_start(out=idf, in_=identf[:, :]), writes=["idf"], dma=True)

        S.add("pool", lambda e: e.dma_start(out=tA3, in_=tabA.rearrange("g j k c -> k (g j) c")),
              writes=["tA"], dma=True)
        S.add("pool", lambda e: e.dma_start(out=tB3, in_=tabB.rearrange("g j k c -> k (g j) c")),
              writes=["tB"], dma=True)
        S.add("pool", lambda e: e.dma_start(out=em, in_=emast[:, :]), writes=["em"], dma=True)
        S.add("pool", lambda e: e.dma_start(out=ov, in_=ovl[:, :]), writes=["ov"], dma=True)
        S.add("sp", lambda e: e.dma_start(out=pdc, in_=padc[:, :]), writes=["pdc"], dma=True)
        S.add("sp", lambda e: e.dma_start(out=esk, in_=sinks[0].partition_broadcast(128)), writes=["esk"], dma=True)
        S.add("act", lambda e: e.activation(out=esk, in_=esk, func=AF.Exp), reads=["esk"], writes=["esk"])
        S.add("sp", lambda e: e.dma_start(out=cfp, in_=cfar[:, :]), writes=["cfp"], dma=True)
        S.add("sp", lambda e: e.dma_start(out=wpd, in_=wpad[:, :]), writes=["pdc"], dma=True)
        S.add("dve", lambda e: e.tensor_scalar(out=cfp, in0=cfp, scalar1=-NEGM, scalar2=None, op0=ALU.add),
              reads=["cfp"], writes=["cfp"])
        for nm, dst, src in (("sV", sV, selV), ("sA", sA, selA), ("sF", sF, selF)):
            S.add("sp", (lambda dst, src: lambda e: e.dma_start(out=dst, in_=src[:, :]))(dst, src),
                  writes=[nm], dma=True)

        ps_rr = [0]
        acc_rr = [0]
        pt_rr = [0]
        dcol = [0]
        ob_rr = [0]

        def attend(rhs_q, tiles, pad_bias=None):
            ao = 2 + acc_rr[0] % 2
            acc_rr[0] += 1
            dc_ = (dcol[0] % 16) * 4
            dcol[0] += 1
            aok = "bank%d" % ao
            dk = "bank5_%d" % dc_
            n = len(tiles)
            for ti, (kap, kkeys, terms, vap, vkeys, xrhs) in enumerate(tiles):
                bi = ps_rr[0] % 2
                ps_rr[0] += 1
                pb = banks[bi]
                pk = "bank%d" % bi
                S.add("pe", (lambda pb, kap, nt: lambda e: e.matmul(pb[:, :], lhsT=kap, rhs=rhs_q, start=True,
                                                                    stop=(nt == 0)))(pb, kap, len(terms)),
                      reads=list(kkeys) + ["qT"], writes=[pk])
                for i2, (tl, tr, tk) in enumerate(terms):
                    S.add("pe", (lambda pb, tl, tr, last: lambda e: e.matmul(pb[:, :], lhsT=tl, rhs=tr, start=False,
                                                                             stop=last))(pb, tl, tr, i2 == len(terms) - 1),
                          reads=list(tk), writes=[pk])
                pi = pt_rr[0] % 3
                pt_rr[0] += 1
                P = PT[pi]
                ptk = "PT%d" % pi
                if pad_bias is not None and pad_bias[ti] is not None:
                    S.add("act", (lambda P, pb, bb: lambda e: e.activation(out=P, in_=pb[:, :], func=AF.Exp, bias=bb))(
                        P, pb, pad_bias[ti]), reads=[pk, "pdc"], writes=[ptk])
                else:
                    S.add("act", (lambda P, pb: lambda e: e.activation(out=P, in_=pb[:, :], func=AF.Exp))(P, pb),
                          reads=[pk], writes=[ptk])
                for r in range(4):
                    Pr = P[:, r * 128:(r + 1) * 128]
                    S.add("pe", (lambda Pr, r, vap, ti: lambda e: e.matmul(
                        banks[ao][:, r * 128:(r + 1) * 128], lhsT=Pr, rhs=vap, start=(ti == 0 and r == 0),
                        stop=(ti == n - 1)))(
                        Pr, r, vap, ti), reads=[ptk] + list(vkeys), writes=[aok])
                    S.add("pe", (lambda Pr, r, ti: lambda e: e.matmul(
                        banks[5][:, dc_ + r:dc_ + r + 1], lhsT=Pr, rhs=onec[:, 0:1], start=(ti == 0 and r == 0),
                        stop=(ti == n - 1)))(Pr, r, ti), reads=[ptk, "onec"], writes=[dk])
                    if xrhs is not None:
                        S.add("pe", (lambda Pr, r, ti, xr: lambda e: e.matmul(
                            banks[4][:, r * 128:(r + 1) * 128], lhsT=Pr, rhs=xr, start=(ti == 0 and r == 0),
                            stop=(ti == n - 1)))(
                            Pr, r, ti, xrhs), reads=[ptk, "ov"], writes=["bank4"])
            return ao, dc_

        def attend_vs(rhs_q, tiles):
            ao = 2 + acc_rr[0] % 2
            acc_rr[0] += 1
            dc_ = (dcol[0] % 16) * 4
            dcol[0] += 1
            aok = "bank%d" % ao
            dk = "bank5_%d" % dc_
            n = len(tiles)
            for ti, (kap, kkeys, terms, vap, vkeys, xrhs) in enumerate(tiles):
                bi = ps_rr[0] % 2
                ps_rr[0] += 1
                pb = banks[bi]
                pk = "bank%d" % bi
                S.add("pe", (lambda pb, kap, nt: lambda e: e.matmul(pb[:, :], lhsT=kap, rhs=rhs_q, start=True,
                                                                    stop=(nt == 0)))(pb, kap, len(terms)),
                      reads=list(kkeys) + ["qT"], writes=[pk])
                for i2, (tl, tr, tk) in enumerate(terms):
                    S.add("pe", (lambda pb, tl, tr, last: lambda e: e.matmul(pb[:, :], lhsT=tl, rhs=tr, start=False,
                                                                             stop=last))(pb, tl, tr, i2 == len(terms) - 1),
                          reads=list(tk), writes=[pk])
                pi = pt_rr[0] % 3
                pt_rr[0] += 1
                P = PT[pi]
                ptk = "PT%d" % pi
                S.add("act", (lambda P, pb: lambda e: e.activation(out=P, in_=pb[:, :], func=AF.Exp))(P, pb),
                      reads=[pk], writes=[ptk])
                S.add("pe", (lambda P, vap, ti: lambda e: e.matmul(banks[7][:, :], lhsT=vap, rhs=P, start=(ti == 0),
                                                                   stop=(ti == n - 1)))(P, vap, ti),
                      reads=[ptk] + list(vkeys), writes=["bank7"])
                S.add("pe", (lambda P, ti: lambda e: e.matmul(banks[6][0:1, :], lhsT=onec[:, 0:1], rhs=P, start=(ti == 0),
                                                              stop=(ti == n - 1)))(P, ti),
                      reads=[ptk, "onec"], writes=["bank6"])
            S.add("act", lambda e: e.copy(out=oTs, in_=banks[7][:, :]), reads=["bank7"], writes=["oTs"])
            S.add("dve", lambda e: e.tensor_copy(out=dns[0:1, :], in_=banks[6][0:1, :]), reads=["bank6"], writes=["dns"])
            for r in range(4):
                S.add("pe", (lambda r: lambda e: e.transpose(banks[ao][:, r * 128:(r + 1) * 128],
                                                             oTs[:, r * 128:(r + 1) * 128], idf))(r),
                      reads=["oTs", "idf"], writes=[aok])
            for r in range(4):
                S.add("pe", (lambda r: lambda e: e.matmul(banks[5][:, dc_ + r:dc_ + r + 1],
                                                          lhsT=dns[0:1, r * 128:(r + 1) * 128], rhs=idf[0:1, 0:1],
                                                          start=True, stop=True))(r),
                      reads=["dns", "idf"], writes=[dk])
            return ao, dc_

        def finish(ao, dc_, gate_cols, first, add_sink=None):
            aok = "bank%d" % ao
            dk = "bank5_%d" % dc_
            if add_sink is not None:
                S.add("dve", lambda e: e.tensor_tensor(out=den, in0=banks[5][:, dc_:dc_ + 4], in1=add_sink, op=ALU.add),
                      reads=[dk, "esk"], writes=["den"])
            else:
                S.add("dve", lambda e: e.tensor_scalar(out=den, in0=banks[5][:, dc_:dc_ + 4], scalar1=1e-30,
                                                       scalar2=None, op0=ALU.max), reads=[dk], writes=["den"])
            S.add("dve", lambda e: e.reciprocal(out=rden, in_=den), reads=["den"], writes=["rden"])
            if gate_cols is not None:
                S.add("dve", lambda e: e.tensor_tensor(out=sc, in0=rden, in1=gate_cols, op=ALU.mult),
                      reads=["rden", "gatesall"], writes=["sc"])
                scal, sck = sc, "sc"
            else:
                scal, sck = rden, "rden"
            for r in range(4):
                o_ = ob32[:, r * 128:(r + 1) * 128]
                a_ = banks[ao][:, r * 128:(r + 1) * 128]
                if first:
                    S.add("dve", (lambda o_, a_, r: lambda e: e.tensor_scalar(
                        out=o_, in0=a_, scalar1=scal[:, r:r + 1], scalar2=None, op0=ALU.mult))(o_, a_, r),
                        reads=[aok, sck], writes=["ob32_%d" % r])
                else:
                    S.add("dve", (lambda o_, a_, r: lambda e: e.scalar_tensor_tensor(
                        out=o_, in0=a_, scalar=scal[:, r:r + 1], in1=o_, op0=ALU.mult, op1=ALU.add))(o_, a_, r),
                        reads=[aok, sck, "ob32_%d" % r], writes=["ob32_%d" % r])

        def emit_o(qidx, colblk, ssq_ap, ssq_key):
            oi = ob_rr[0] % 2
            ob_rr[0] += 1
            ob_ = obb[oi]
            okk = "obb%d" % oi
            S.add("pool", lambda e: e.tensor_copy(out=ob_, in_=ob32), reads=["ob32_%d" % r for r in range(4)],
                  writes=[okk])
            S.add("act", lambda e: e.activation(out=junk, in_=ob_, func=AF.Square, accum_out=ssq_ap),
                  reads=[okk], writes=["junk", ssq_key])
            S.add("sp", lambda e: e.dma_start(out=o_d[qidx * 128:(qidx + 1) * 128, colblk * 512:(colblk + 1) * 512],
                                              in_=ob_), reads=[okk], dma=True)

        for grp in range(8):
            isA = grp < 4
            g = grp % 4
            h0 = (0 if isA else 16) + 4 * g
            for r in range(4):
                S.add("sp", (lambda h0, r: lambda e: e.dma_start(
                    out=qT4[:, :, r, :], in_=qT_d[h0 + r][:, 384:1536].rearrange("d (q t) -> d q t", t=128)))(h0, r),
                    writes=["qT"], dma=True)
            if isA:
                S.add("sp", (lambda g: lambda e: e.dma_start(out=kT[:, 0:1280], in_=akT_d[g][:, 256:1536]))(g),
                      writes=["kT"], dma=True)
                S.add("sp", (lambda g: lambda e: e.dma_start(
                    out=vv3[:, 0:10, :], in_=av_d[256:1536, g * 128:(g + 1) * 128].rearrange("(t p) d -> p t d", p=128)))(g),
                    writes=["vv"], dma=True)
            else:
                S.add("sp", (lambda g: lambda e: e.dma_start(out=kT, in_=ksT_d[g]))(g), writes=["kT"], dma=True)
                S.add("sp", (lambda g: lambda e: e.dma_start(
                    out=vv3, in_=vs_d[:, g * 128:(g + 1) * 128].rearrange("(t p) d -> p t d", p=128)))(g),
                    writes=["vv"], dma=True)
                S.add("sp", (lambda g: lambda e: e.dma_start(out=kw, in_=kwT_d[g][:, 384:2048]))(g),
                      writes=["kw"], dma=True)
                S.add("sp", (lambda g: lambda e: e.dma_start(
                    out=vw3, in_=vw_d[384:2048, g * 128:(g + 1) * 128].rearrange("(t p) d -> p t d", p=128)))(g),
                    writes=["vw"], dma=True)
                S.add("sp", (lambda g: lambda e: e.dma_start(out=kc, in_=kcT_d[g]))(g), writes=["kc"], dma=True)
                S.add("sp", (lambda g: lambda e: e.dma_start(
                    out=vcs3, in_=vc_d[g].rearrange("(t p) d -> p t d", p=128)))(g), writes=["vcs"], dma=True)
            for qx in range(NQT):
                Q = 55 + qx
                rhs_q = qT[:, qx * 512:(qx + 1) * 512]
                if isA:
                    tiles = []
                    for j in (1, 0):
                        kt = Q - j - 54
                        tiles.append((kT[:, kt * 128:(kt + 1) * 128], ["kT"],
                                      [(ident, tA3[:, g * 2 + j, :], ["ident", "tA"])],
                                      vv3[:, kt, :], ["vv"], None))
                    ao, dc_ = attend(rhs_q, tiles, pad_bias=[wpd[:, Q - j - 51:Q - j - 50] for j in (1, 0)])
                    finish(ao, dc_, None, True, add_sink=esk[:, 4 * g:4 * g + 4])
                    emit_o(qx, g, ssqA[:, qx * 4 + g:qx * 4 + g + 1], key("ssqA"))
                    continue
                gt = gates[:, (3 + qx) * 48:(4 + qx) * 48].rearrange("p (h b) -> p h b", b=3)
                tci = (grp * NQT + qx) % 2
                tCc = tC[tci]
                S.add("pool", (lambda tCc, qx, g: lambda e: e.dma_start(out=tCc, in_=tabC[qx][g]))(tCc, qx, g),
                      writes=["tC%d" % tci], dma=True)
                tiles = []
                for t in range(4):
                    if t < 3:
                        term = (ident, tB3[:, g * 4 + 2, :], ["ident", "tB"])
                    else:
                        term = (ident, tCc, ["ident", "tC%d" % tci])
                    tiles.append((kc[:, t * 128:(t + 1) * 128], ["kc"], [term], vcs3[:, t, :], ["vcs"], ov3[:, t, :]))
                ao, dc_ = attend(rhs_q, tiles, pad_bias=[pdc[:, t:t + 1] for t in range(4)])
                finish(ao, dc_, gt[:, 4 * g:4 * g + 4, 0], True)
                for r in range(4):
                    a_ = banks[4][:, r * 128:(r + 1) * 128]
                    if r == 0:
                        S.add("dve", (lambda a_: lambda e: e.tensor_scalar(out=fin, in0=a_, scalar1=rden[:, 0:1],
                                                                           scalar2=None, op0=ALU.mult))(a_),
                              reads=["bank4", "rden"], writes=["fin"])
                    else:
                        S.add("dve", (lambda a_, r: lambda e: e.scalar_tensor_tensor(
                            out=fin, in0=a_, scalar=rden[:, r:r + 1], in1=fin, op0=ALU.mult, op1=ALU.add))(a_, r),
                            reads=["bank4", "rden", "fin"], writes=["fin"])
                S.add("dve", (lambda qx: lambda e: e.tensor_tensor(out=fin, in0=fin, in1=sV[:, qx * 128:(qx + 1) * 128],
                                                                   op=ALU.mult))(qx), reads=["fin", "sV"], writes=["fin"])
                S.add("dve", (lambda qx: lambda e: e.tensor_tensor(out=fin, in0=fin, in1=sA[:, qx * 128:(qx + 1) * 128],
                                                                   op=ALU.add))(qx), reads=["fin", "sA"], writes=["fin"])
                S.add("dve", lambda e: e.max(out=m8[:, 0:8], in_=fin), reads=["fin"], writes=["m8a"])
                S.add("dve", lambda e: e.match_replace(out=fin2, in_to_replace=m8[:, 0:8], in_values=fin,
                                                       imm_value=-1e30), reads=["fin", "m8a"], writes=["fin2"])
                S.add("dve", lambda e: e.max(out=m8[:, 8:16], in_=fin2), reads=["fin2"], writes=["m8b"])
                S.add("dve", lambda e: e.tensor_scalar(out=fin2, in0=fin, scalar1=m8[:, 15:16], scalar2=None,
                                                       op0=ALU.is_ge), reads=["fin", "m8b"], writes=["fin2"])
                S.add("dve", (lambda qx: lambda e: e.tensor_tensor(out=selb, in0=fin2, in1=sF[:, qx * 128:(qx + 1) * 128],
                                                                   op=ALU.mult))(qx), reads=["fin2", "sF"], writes=["selb"])
                S.add("pe", lambda e: e.transpose(banks_bf[6][:, 0:128], selb, ident), reads=["selb", "ident"],
                      writes=["bank6"])
                for r in range(4):
                    cc = (qx * 4 + g) * 4 + r
                    S.add("dve", (lambda r, cc: lambda e: e.tensor_scalar(
                        out=addT[:, r * 128:(r + 1) * 128], in0=banks_bf[6][:, 0:128], scalar1=cfp[:, cc:cc + 1],
                        scalar2=NEGM, op0=ALU.mult, op1=ALU.add))(r, cc), reads=["bank6", "cfp"], writes=["addT"])
                tiles = []
                for kt in range(0, Q + 1):
                    j = Q - kt
                    ek = em[:, kt * 128:(kt + 1) * 128]
                    if j == 0:
                        terms = [(ident, tB3[:, g * 4 + 0, :], ["ident", "tB"])]
                    elif j == 1:
                        terms = [(ek, addT, ["em", "addT"]), (ident, tB3[:, g * 4 + 1, :], ["ident", "tB"])]
                    else:
                        terms = [(ek, addT, ["em", "addT"])]
                    tiles.append((kT[:, kt * 128:(kt + 1) * 128], ["kT"], terms, vv3[:, kt, :], ["vv"], None))
                ao, dc_ = attend_vs(rhs_q, tiles)
                finish(ao, dc_, gt[:, 4 * g:4 * g + 4, 1], False)
                tiles = []
                for j in (4, 3, 2, 1, 0):
                    kt = Q - j - 51
                    tb = {0: 0, 1: 1, 2: 2, 3: 2, 4: 3}[j]
                    tiles.append((kw[:, kt * 128:(kt + 1) * 128], ["kw"],
                                  [(ident, tB3[:, g * 4 + tb, :], ["ident", "tB"])], vw3[:, kt, :], ["vw"], None))
                ao, dc_ = attend(rhs_q, tiles, pad_bias=[wpd[:, Q - j - 51:Q - j - 50] for j in (4, 3, 2, 1, 0)])
                finish(ao, dc_, gt[:, 4 * g:4 * g + 4, 2], False)
                emit_o(qx, 4 + g, ssqB[:, qx * 4 + g:qx * 4 + g + 1], key("ssqB"))
        S.fence()

        AR.reset(base_mark)
        gob = AR.f32(D)
        S.add("sp", lambda e: e.dma_start(out=gob, in_=gout[0].partition_broadcast(128)), writes=["gbc"], dma=True)
        onT = AR.bf(32 * 1152)
        onT3 = onT.rearrange("p (k t) -> p k t", t=1152)
        obf = [AR.bf(D) for _ in range(2)]
        xs2 = AR.bf(D)
        wo = [AR.bf(32 * 512) for _ in range(2)]
        wo3 = [w.rearrange("p (k c) -> p k c", c=512) for w in wo]
        xc = [AR.f32(512) for _ in range(3)]
        x1c = [AR.f32(512) for _ in range(3)]
        rA = AR.f32(NQT)
        rB = AR.f32(NQT)
        rtm = AR.f32(NQT)
        rtmB = AR.f32(NQT)
        junk2 = AR.bf(512)
        S.add("dve", lambda e: e.tensor_reduce(out=rA, in_=ssqA.rearrange("p (q g) -> p q g", g=4), axis=AX.X,
                                               op=ALU.add), reads=[], writes=["rA0"])
        S.add("dve", lambda e: e.tensor_reduce(out=rB, in_=ssqB.rearrange("p (q g) -> p q g", g=4), axis=AX.X,
                                               op=ALU.add), reads=[], writes=["rB0"])
        rstd_from(rA, rA, 1.0 / 2048, ["rA0"], ["rA"], rtm)
        rstd_from(rB, rB, 1.0 / 2048, ["rB0"], ["rB"], rtmB)
        for qx in range(NQT):
            ob_ = obf[qx % 2]
            okk = "obf%d" % (qx % 2)
            S.add("sp", (lambda ob_, qx: lambda e: e.dma_start(out=ob_, in_=o_d[qx * 128:(qx + 1) * 128, :]))(ob_, qx),
                  writes=[okk], dma=True)
            S.add("dve", (lambda ob_, qx: lambda e: e.scalar_tensor_tensor(
                out=xs2[:, 0:2048], in0=ob_[:, 0:2048], scalar=rA[:, qx:qx + 1], in1=gob[:, 0:2048],
                op0=ALU.mult, op1=ALU.mult))(ob_, qx), reads=[okk, "rA", "gbc"], writes=["xs2a"])
            S.add("dve", (lambda ob_, qx: lambda e: e.scalar_tensor_tensor(
                out=xs2[:, 2048:4096], in0=ob_[:, 2048:4096], scalar=rB[:, qx:qx + 1], in1=gob[:, 2048:4096],
                op0=ALU.mult, op1=ALU.mult))(ob_, qx), reads=[okk, "rB", "gbc"], writes=["xs2b"])
            for q4 in range(4):
                bi = q4 % 2
                pbf = banks_bf[bi]
                pk = "bank%d" % bi
                for k8 in range(8):
                    kc_ = q4 * 8 + k8
                    S.add("pe", (lambda kc_, k8, pbf: lambda e: e.transpose(
                        pbf[:, k8 * 128:(k8 + 1) * 128], xs2[:, kc_ * 128:(kc_ + 1) * 128], ident))(kc_, k8, pbf),
                        reads=["xs2a", "xs2b", "ident"], writes=[pk])
                evac_copy(onT3[:, q4 * 8:(q4 + 1) * 8, qx * 128:(qx + 1) * 128],
                          pbf.rearrange("p (k t) -> p k t", t=128), [pk], ["onT"])
        xi = 0
        for dc in range(8):
            wi = dc % 2
            S.add("pool", (lambda wi, dc: lambda e: e.dma_start(
                out=wo3[wi], in_=w_out[:, dc * 512:(dc + 1) * 512].rearrange("(k p) c -> p k c", p=128)))(wi, dc),
                writes=["wo%d" % wi], dma=True)
            for qx in range(NQT):
                bi = 2 + (dc * NQT + qx) % 3
                pb = banks[bi]
                pk = "bank%d" % bi
                for kc_ in range(32):
                    S.add("pe", (lambda kc_, pb, wi, qx: lambda e: e.matmul(
                        pb[:, :], lhsT=onT3[:, kc_, qx * 128:(qx + 1) * 128], rhs=wo3[wi][:, kc_, :],
                        start=(kc_ == 0), stop=(kc_ == 31)))(kc_, pb, wi, qx), reads=["onT", "wo%d" % wi], writes=[pk])
                j = xi % 3
                xi += 1
                S.add("sp", (lambda j, qx, dc: lambda e: e.dma_start(
                    out=xc[j], in_=x[(55 + qx) * 128:(56 + qx) * 128, dc * 512:(dc + 1) * 512]))(j, qx, dc),
                    writes=["xc%d" % j], dma=True)
                S.add("dve", (lambda j, pb: lambda e: e.tensor_tensor(out=x1c[j], in0=pb[:, :], in1=xc[j],
                                                                      op=ALU.add))(j, pb),
                      reads=[pk, "xc%d" % j], writes=["x1c%d" % j])
                S.add("act", (lambda j, qx, dc: lambda e: e.activation(
                    out=junk2, in_=x1c[j], func=AF.Square, accum_out=ssqF[:, qx * 8 + dc:qx * 8 + dc + 1]))(j, qx, dc),
                    reads=["x1c%d" % j], writes=["junk2", key("ssqF")])
                S.add("sp", (lambda j, qx, dc: lambda e: e.dma_start(
                    out=x1_d[qx * 128:(qx + 1) * 128, dc * 512:(dc + 1) * 512], in_=x1c[j]))(j, qx, dc),
                    reads=["x1c%d" % j], dma=True)
        S.fence()

        AR.reset(base_mark)
        gfb = AR.f32(D)
        S.add("sp", lambda e: e.dma_start(out=gfb, in_=gffn[0].partition_broadcast(128)), writes=["gbc"], dma=True)
        hfT = AR.bf(32 * 1026)
        hfT3 = hfT.rearrange("p (k t) -> p k t", t=1026)
        ffn_mark = AR.mark()
        xbuf = [AR.f32(D) for _ in range(2)]
        xs = AR.bf(D)
        rF = AR.f32(NQT)
        rtm2 = AR.f32(NQT)
        S.add("dve", lambda e: e.tensor_reduce(out=rF, in_=ssqF.rearrange("p (q g) -> p q g", g=8), axis=AX.X,
                                               op=ALU.add), reads=[], writes=["rF0"])
        rstd_from(rF, rF, 1.0 / D, ["rF0"], ["rF"], rtm2)
        for qx in range(NQT):
            xb_ = xbuf[qx % 2]
            xk = "xbuf%d" % (qx % 2)
            S.add("sp", (lambda xb_, qx: lambda e: e.dma_start(out=xb_, in_=x1_d[qx * 128:(qx + 1) * 128, :]))(xb_, qx),
                  writes=[xk], dma=True)
            S.add("dve", (lambda xb_, qx: lambda e: e.scalar_tensor_tensor(
                out=xs, in0=xb_, scalar=rF[:, qx:qx + 1], in1=gfb, op0=ALU.mult, op1=ALU.mult))(xb_, qx),
                reads=[xk, "rF", "gbc"], writes=["xs"])
            for q4 in range(4):
                bi = q4 % 2
                pbf = banks_bf[bi]
                pk = "bank%d" % bi
                for k8 in range(8):
                    kc_ = q4 * 8 + k8
                    S.add("pe", (lambda kc_, k8, pbf: lambda e: e.transpose(
                        pbf[:, k8 * 128:(k8 + 1) * 128], xs[:, kc_ * 128:(kc_ + 1) * 128], ident))(kc_, k8, pbf),
                        reads=["xs", "ident"], writes=[pk])
                src = pbf.rearrange("p (k t) -> p k t", t=128)
                if qx == 0:
                    evac_copy(hfT3[:, q4 * 8:(q4 + 1) * 8, 0:2], src[:, :, 126:128], [pk], ["hfT"])
                else:
                    c0 = 2 + (qx - 1) * 128
                    evac_copy(hfT3[:, q4 * 8:(q4 + 1) * 8, c0:c0 + 128], src, [pk], ["hfT"])
        S.fence()

        AR.reset(ffn_mark)
        cwc = AR.f32(NFC * 3)
        cbc = AR.f32(NFC)
        hfl = AR.f32(2)
        S.add("sp", lambda e: e.dma_start(out=hfl, in_=hflag[:, :]), writes=["hfl"], dma=True)
        S.add("sp", lambda e: e.dma_start(out=cwc, in_=convw[:, :]), writes=["cwc"], dma=True)
        S.add("sp", lambda e: e.dma_start(out=cbc, in_=convb[:, :]), writes=["cbc"], dma=True)
        wg = [AR.bf(32 * 256) for _ in range(2)]
        wu = [AR.bf(32 * 256) for _ in range(2)]
        wg3 = [w.rearrange("p (k c) -> p k c", c=256) for w in wg]
        wu3 = [w.rearrange("p (k c) -> p k c", c=256) for w in wu]
        gsb = [AR.f32(514) for _ in range(2)]
        tcv = [AR.f32(512) for _ in range(2)]
        ssb = [AR.f32(512) for _ in range(2)]
        hst = [AR.bf(1024) for _ in range(2)]
        u2 = 0
        for c2 in range(43):
            wi = c2 % 2
            S.add("pool", (lambda wi, c2: lambda e: e.dma_start(
                out=wg3[wi], in_=w_gate[:, c2 * 256:(c2 + 1) * 256].rearrange("(k p) c -> p k c", p=128)))(wi, c2),
                writes=["wg%d" % wi], dma=True)
            S.add("pool", (lambda wi, c2: lambda e: e.dma_start(
                out=wu3[wi], in_=w_up[:, c2 * 256:(c2 + 1) * 256].rearrange("(k p) c -> p k c", p=128)))(wi, c2),
                writes=["wu%d" % wi], dma=True)
            for sub in range(2):
                fc = c2 * 2 + sub
                hs = hst[fc % 2]
                hk = "hst%d" % (fc % 2)
                for half in range(2):
                    b0 = half * 512
                    u = u2 % 2
                    u2 += 1
                    bG, bU = banks[u * 3], banks[u * 3 + 1]
                    bH = banks[6 + u]
                    kG, kU, kH = "bank%d" % (u * 3), "bank%d" % (u * 3 + 1), "bank%d" % (6 + u)
                    for kc_ in range(32):
                        lg = wg3[wi][:, kc_, sub * 128:(sub + 1) * 128]
                        S.add("pe", (lambda kc_, lg, bG, b0: lambda e: e.matmul(
                            bG[:, :], lhsT=lg, rhs=hfT3[:, kc_, b0 + 2:b0 + 514], start=(kc_ == 0), stop=(kc_ == 31)))(
                            kc_, lg, bG, b0), reads=["wg%d" % wi, "hfT"], writes=[kG])
                        S.add("pe", (lambda kc_, lg, bH, b0: lambda e: e.matmul(
                            bH[:, 0:2], lhsT=lg, rhs=hfT3[:, kc_, b0:b0 + 2], start=(kc_ == 0), stop=(kc_ == 31)))(
                            kc_, lg, bH, b0), reads=["wg%d" % wi, "hfT"], writes=[kH])
                    for kc_ in range(32):
                        lu = wu3[wi][:, kc_, sub * 128:(sub + 1) * 128]
                        S.add("pe", (lambda kc_, lu, bU, b0: lambda e: e.matmul(
                            bU[:, :], lhsT=lu, rhs=hfT3[:, kc_, b0 + 2:b0 + 514], start=(kc_ == 0), stop=(kc_ == 31)))(
                            kc_, lu, bU, b0), reads=["wu%d" % wi, "hfT"], writes=[kU])
                    gs, tc_, ss = gsb[u], tcv[u], ssb[u]
                    S.add("act", (lambda gs, bH, half: lambda e: e.mul(out=gs[:, 0:2], in_=bH[:, 0:2],
                                                                       mul=hfl[:, half:half + 1]))(gs, bH, half),
                          reads=[kH, "hfl"], writes=["gsbh%d" % u])
                    S.add("act", (lambda gs, bG: lambda e: e.copy(out=gs[:, 2:514], in_=bG[:, :]))(gs, bG),
                          reads=[kG], writes=["gsbm%d" % u])
                    gk = ["gsbh%d" % u, "gsbm%d" % u]
                    S.add("dve", (lambda gs, tc_, fc: lambda e: e.tensor_scalar(
                        out=tc_, in0=gs[:, 2:514], scalar1=cwc[:, fc * 3 + 2:fc * 3 + 3], scalar2=None, op0=ALU.mult))(
                        gs, tc_, fc), reads=gk + ["cwc"], writes=["tcv%d" % u])
                    S.add("dve", (lambda gs, tc_, fc: lambda e: e.scalar_tensor_tensor(
                        out=tc_, in0=gs[:, 1:513], scalar=cwc[:, fc * 3 + 1:fc * 3 + 2], in1=tc_, op0=ALU.mult,
                        op1=ALU.add))(gs, tc_, fc), reads=gk + ["cwc", "tcv%d" % u], writes=["tcv%d" % u])
                    S.add("dve", (lambda gs, tc_, fc: lambda e: e.scalar_tensor_tensor(
                        out=tc_, in0=gs[:, 0:512], scalar=cwc[:, fc * 3:fc * 3 + 1], in1=tc_, op0=ALU.mult,
                        op1=ALU.add))(gs, tc_, fc), reads=gk + ["cwc", "tcv%d" % u], writes=["tcv%d" % u])
                    S.add("act", (lambda tc_, ss, fc: lambda e: e.activation(
                        out=ss, in_=tc_, func=AF.Silu, bias=cbc[:, fc:fc + 1]))(tc_, ss, fc),
                        reads=["tcv%d" % u, "cbc"], writes=["ssb%d" % u])
                    S.add("dve", (lambda ss, bU, hs, b0: lambda e: e.tensor_tensor(
                        out=hs[:, b0:b0 + 512], in0=bU[:, :], in1=ss, op=ALU.mult))(ss, bU, hs, b0),
                        reads=["ssb%d" % u, kU], writes=[hk + "_%d" % half])
                S.add("sp", (lambda hs, fc: lambda e: e.dma_start(out=hT_d[fc], in_=hs))(hs, fc),
                      reads=[hk + "_0", hk + "_1"], dma=True)
        S.fence()

        AR.reset(base_mark)
        wd = [AR.bf(4 * 512) for _ in range(2)]
        wd3 = [w.rearrange("p (f c) -> p f c", c=512) for w in wd]
        hb = [AR.bf(4 * 1024) for _ in range(2)]
        hb3 = [h.rearrange("p (f t) -> p f t", t=1024) for h in hb]
        x1b = [AR.f32(512) for _ in range(3)]
        ob = [AR.f32(512) for _ in range(3)]
        li = 0
        oi = 0
        for dc in range(8):
            for fg in range(22):
                nf = 4 if fg < 21 else 2
                i2 = li % 2
                li += 1
                S.add("pool", (lambda i2, fg, nf, dc: lambda e: e.dma_start(
                    out=wd3[i2][:, 0:nf, :],
                    in_=w_down[fg * 512:fg * 512 + nf * 128, dc * 512:(dc + 1) * 512].rearrange("(f p) c -> p f c", p=128)))(
                    i2, fg, nf, dc), writes=["wd%d" % i2], dma=True)
                S.add("sp", (lambda i2, fg, nf: lambda e: e.dma_start(
                    out=hb3[i2][:, 0:nf, :], in_=hT_d[fg * 4:fg * 4 + nf].rearrange("f p t -> p f t")))(i2, fg, nf),
                    writes=["hb%d" % i2], dma=True)
                for f in range(nf):
                    fc = fg * 4 + f
                    for tl in range(8):
                        S.add("pe", (lambda i2, f, tl, fc: lambda e: e.matmul(
                            banks[tl][:, :], lhsT=hb3[i2][:, f, tl * 128:(tl + 1) * 128], rhs=wd3[i2][:, f, :],
                            start=(fc == 0), stop=(fc == NFC - 1)))(i2, f, tl, fc),
                            reads=["wd%d" % i2, "hb%d" % i2], writes=["bank%d" % tl])
            for tl in range(8):
                j = oi % 3
                oi += 1
                S.add("sp", (lambda j, tl, dc: lambda e: e.dma_start(
                    out=x1b[j], in_=x1_d[(tl + 1) * 128:(tl + 2) * 128, dc * 512:(dc + 1) * 512]))(j, tl, dc),
                    writes=["x1b%d" % j], dma=True)
                S.add("dve", (lambda j, tl: lambda e: e.tensor_tensor(out=ob[j], in0=banks[tl][:, :], in1=x1b[j],
                                                                      op=ALU.add))(j, tl),
                      reads=["bank%d" % tl, "x1b%d" % j], writes=["ob%d" % j])
                S.add("sp", (lambda j, tl, dc: lambda e: e.dma_start(
                    out=y[tl * 128:(tl + 1) * 128, dc * 512:(dc + 1) * 512], in_=ob[j]))(j, tl, dc),
                    reads=["ob%d" % j], dma=True)
        S.emit()
    return nc


def _bucket(dist):
    n = np.maximum(dist, 0)
    nf = np.maximum(n, 1).astype(np.float32)
    large = 16 + (np.log(nf / np.float32(16)) / np.float32(math.log(128 / 16)) * np.float32(16)).astype(np.int32)
    large = np.minimum(large, 31)
    return np.where(n < 16, n, large)


def _host_tables(rel_bias, c):
    rb = np.asarray(rel_bias, np.float32)
    k = np.arange(128)[:, None]
    q = np.arange(128)[None, :]
    tabA = np.full((4, 2, 128, 4, 128), NEGM, np.float32)
    tabB = np.full((4, 4, 128, 4, 128), NEGM, np.float32)
    for g in range(4):
        for r in range(4):
            hA = 4 * g + r
            hB = 16 + 4 * g + r
            for j in range(2):
                dist = q - k + 128 * j
                ok = (dist >= 0) & (dist < 128)
                tabA[g, j, :, r, :] = np.where(ok, rb[_bucket(dist), hA], NEGM)
            dist = q - k
            tabB[g, 0, :, r, :] = np.where(dist >= 0, rb[_bucket(dist), hB], NEGM)
            dist = q - k + 128
            tabB[g, 1, :, r, :] = rb[_bucket(dist), hB]
            tabB[g, 2, :, r, :] = rb[31, hB]
            tabB[g, 3, :, r, :] = np.where(k > q, rb[31, hB], NEGM)
    tabC = np.full((NQT, 4, 128, 4, 128), NEGM, np.float32)
    cfar = np.zeros((NQT, 4, 128, 4), np.float32)
    npad_blk = (7 - c) * 16
    selV = np.zeros((NQT, 128, 128), np.float32)
    selA = np.zeros((NQT, 128, 128), np.float32)
    selF = np.zeros((NQT, 128, 128), np.float32)
    blk = np.arange(128)[None, :]
    for qx in range(NQT):
        Q = 55 + qx
        qs = Q * 128 + np.arange(128)
        nn = 384 + np.arange(128)
        dist = qs[None, :] - (16 * nn[:, None] + 31)
        for g in range(4):
            for r in range(4):
                hB = 16 + 4 * g + r
                tabC[qx, g, :, r, :] = np.where(dist >= 0, rb[_bucket(dist), hB], NEGM)
                cfar[qx, g, :, r] = np.where(np.arange(128) < 2 * Q - 2, rb[31, hB], 0.0)
        cur = (qs // 64)[:, None]
        valid = (blk <= cur) & (blk >= npad_blk)
        forced = ((blk == npad_blk) | (blk == cur) | (blk == cur - 1)) & valid
        selF[qx] = valid
        selV[qx] = valid & ~forced
        selA[qx] = np.where(forced, 1e6 + blk * 16.0, np.where(valid, 0.0, -1.0))
    padc = np.zeros((128, 4), np.float32)
    for t in range(4):
        n_ = t * 128 + np.arange(128)
        padc[:, t] = np.where(n_ >= (7 - c) * 64, 0.0, NEGM)
    wpad = np.zeros((128, 13), np.float32)
    for i in range(13):
        wpad[:, i] = 0.0 if (51 + i) * 128 >= (7 - c) * TOK else NEGM
    hflag = np.ones((128, 2), np.float32)
    if c == 0:
        hflag[:, 0] = 0.0
    lay = lambda a: np.ascontiguousarray(a.transpose(1, 0, 2).reshape(128, -1))
    cf = np.ascontiguousarray(cfar.transpose(2, 0, 1, 3).reshape(128, NQT * 16))
    return dict(tabA=tabA.reshape(4, 2, 128, 512), tabB=tabB.reshape(4, 4, 128, 512),
                tabC=tabC.reshape(NQT, 4, 128, 512), cfar=cf, selV=lay(selV), selA=lay(selA), selF=lay(selF),
                padc=padc, wpad=wpad, hflag=hflag)


_NC_CACHE = {}


def kernel(x, rel_bias, norm_mix_g, w_in, a_q_norm_g, a_k_norm_g, a_sinks, b_q_norm_g, b_k_norm_g,
           cmp_pos_emb, cmp_w1, cmp_b1, cmp_w2, cmp_b2, out_norm_g, w_out, norm_ffn_g, w_gate, w_up,
           conv_w, conv_b, w_down):
    if "nc" not in _NC_CACHE:
        _NC_CACHE["nc"] = build_nc()
    nc = _NC_CACHE["nc"]
    in_maps = _prep(x, rel_bias, norm_mix_g, w_in, a_q_norm_g, a_k_norm_g, a_sinks, b_q_norm_g, b_k_norm_g,
                    cmp_pos_emb, cmp_w1, cmp_b1, cmp_w2, cmp_b2, out_norm_g, w_out, norm_ffn_g, w_gate, w_up,
                    conv_w, conv_b, w_down)
    res = run_bass_kernel_spmd(nc, in_maps, core_ids=list(range(NCORES)))
    out = np.concatenate([np.asarray(r["y"], np.float32) for r in res.results], axis=0)
    return out[None]


def _prep(x, rel_bias, norm_mix_g, w_in, a_q_norm_g, a_k_norm_g, a_sinks, b_q_norm_g, b_k_norm_g,
          cmp_pos_emb, cmp_w1, cmp_b1, cmp_w2, cmp_b2, out_norm_g, w_out, norm_ffn_g, w_gate, w_up,
          conv_w, conv_b, w_down):
    f = lambda a: np.ascontiguousarray(np.asarray(a, np.float32))
    x = f(x)[0]
    hg = np.zeros((128, 8), np.float32)
    hg[:, 0] = f(a_q_norm_g)[0]
    hg[:, 1] = f(a_k_norm_g)[0]
    hg[:, 2] = f(b_q_norm_g)[0]
    hg[:, 3:6] = f(b_k_norm_g)[0].T
    kk = np.arange(S_ALL)
    emast = (kk[None, :] // 64 == np.arange(128)[:, None]).astype(np.float32)
    nn = np.arange(512)[:, None]
    bb = np.arange(128)[None, :]
    ovl = ((nn * 16 <= bb * 64 + 63) & (nn * 16 + 31 >= bb * 64)).astype(np.float32)
    ovl = ovl.reshape(4, 128, 128).transpose(1, 0, 2).reshape(128, 512)
    shared = dict(
        w_in=f(w_in)[0], w_out=f(w_out)[0], w_gate=f(w_gate)[0], w_up=f(w_up)[0], w_down=f(w_down)[0],
        gmix=f(norm_mix_g), gout=f(out_norm_g), gffn=f(norm_ffn_g), hg=hg, sinks=f(a_sinks),
        posT=np.ascontiguousarray(f(cmp_pos_emb)[0].transpose(0, 2, 1)),
        cw1=f(cmp_w1)[0], cb1=np.ascontiguousarray(f(cmp_b1)[0].reshape(2, 2, 128).transpose(0, 2, 1)),
        cw2=f(cmp_w2)[0],
        cb2c=np.ascontiguousarray(np.stack([f(cmp_b2)[0, 0], f(cmp_b2)[0, 0]], axis=1)),
        cb2r=f(cmp_b2)[0, 1:2],
        convw=np.ascontiguousarray(f(conv_w)[0].T.reshape(NFC, 128, 3).transpose(1, 0, 2).reshape(128, NFC * 3)),
        convb=np.ascontiguousarray(f(conv_b)[0].reshape(NFC, 128).T),
        emast=emast, ovl=np.ascontiguousarray(ovl), identf=np.eye(128, dtype=np.float32),
    )
    in_maps = []
    for c in range(NCORES):
        pad = (7 - c) * TOK
        xr = np.zeros((S_ALL, D), np.float32)
        xr[pad:] = x[:S_ALL - pad]
        m = dict(shared)
        m["x"] = xr
        m.update(_host_tables(rel_bias, c))
        in_maps.append(m)
    return in_maps
```

```python
import contextlib
import math
import types
import numpy as np
import ml_dtypes
import concourse.bass as bass
import concourse.mybir as mybir
from concourse.bass_utils import run_bass_kernel_spmd

F32 = mybir.dt.float32
BF16 = mybir.dt.bfloat16
AF = mybir.ActivationFunctionType
ALU = mybir.AluOpType
AX = mybir.AxisListType

NCORES = 8
D = 4096
S_ALL = 8192
TOK = 1024
DFF = 11008
NFC = 86
NIN = 8240
EPS = 1e-6
NEGM = -1.0e4
SCALE = 128 ** -0.5
NQT = 9

ENGS = ("pe", "act", "dve", "pool", "sp")
NDMASEM = 8


def _freeze(fn):
    if fn.__closure__ is None:
        return fn
    cells = []
    for c in fn.__closure__:
        try:
            cells.append(types.CellType(c.cell_contents))
        except ValueError:
            cells.append(c)
    return types.FunctionType(fn.__code__, fn.__globals__, fn.__name__, fn.__defaults__, tuple(cells))


class Op:
    __slots__ = ("eng", "fn", "reads", "writes", "dma", "idx", "deps", "signal",
                 "count", "dsem", "dcount", "dprev")

    def __init__(self, eng, fn, reads, writes, dma):
        self.eng, self.fn, self.reads, self.writes, self.dma = eng, fn, reads, writes, dma
        self.deps = []
        self.signal = False
        self.count = 0
        self.dsem = None
        self.dcount = 0
        self.dprev = 0


class Sched:
    def __init__(self, nc):
        self.nc = nc
        self.ops = {e: [] for e in ENGS}
        self.lastw = {}
        self.readers = {}
        self.ndma = {e: 0 for e in ENGS}
        self.dsem_n = {}
        self.dsem_last = {}
        self.fence_deps = {}

    def add(self, eng, fn, reads=(), writes=(), dma=False):
        op = Op(eng, _freeze(fn), tuple(reads), tuple(writes), dma)
        op.idx = len(self.ops[eng])
        deps = set()
        for k in op.reads:
            w = self.lastw.get(k)
            if w is not None:
                deps.add(w)
        for k in op.writes:
            w = self.lastw.get(k)
            if w is not None:
                deps.add(w)
            for r in self.readers.get(k, ()):
                deps.add(r)
        if eng in self.fence_deps:
            deps.update(self.fence_deps.pop(eng))
        deps.discard(op)
        op.deps = list(deps)
        for k in op.reads:
            self.readers.setdefault(k, []).append(op)
        for k in op.writes:
            self.lastw[k] = op
            self.readers[k] = []
        if dma:
            i = self.ndma[eng]
            self.ndma[eng] += 1
            op.dsem = (eng, i % NDMASEM)
            n = self.dsem_n.get(op.dsem, 0)
            op.dprev = n
            op.dcount = n + 1
            self.dsem_n[op.dsem] = n + 1
            self.dsem_last[op.dsem] = op
        self.ops[eng].append(op)
        return op

    def fence(self):
        last = []
        for e in ENGS:
            for op in reversed(self.ops[e]):
                if not op.dma:
                    last.append(op)
                    break
        last.extend(self.dsem_last.values())
        for e in ENGS:
            self.fence_deps.setdefault(e, set()).update(last)
        self.lastw = {}
        self.readers = {}

    def emit(self):
        nc = self.nc
        for e in ENGS:
            for op in self.ops[e]:
                for d in op.deps:
                    if d.dma or (d.eng == e and e == "pe" and not op.dma):
                        continue
                    d.signal = True
        for e in ENGS:
            c = 0
            for op in self.ops[e]:
                if op.signal and not op.dma:
                    c += 1
                    op.count = c
        with contextlib.ExitStack() as st:
            csem = {e: st.enter_context(nc.semaphore("c_" + e)) for e in ENGS}
            dsem = {}
            for e in ENGS:
                if self.ndma[e]:
                    for i in range(NDMASEM):
                        dsem[(e, i)] = st.enter_context(nc.semaphore("d_%s%d" % (e, i)))
            block = st.enter_context(nc.Block())

            def run(e, eng):
                seen = {}

                def wait(sem, key, val):
                    if seen.get(key, 0) >= val:
                        return
                    seen[key] = val
                    eng.wait_ge(sem, val)

                for op in self.ops[e]:
                    for d in op.deps:
                        if d.dma:
                            wait(dsem[d.dsem], d.dsem, 16 * d.dcount)
                        elif d.eng == e and e == "pe" and not op.dma:
                            continue
                        else:
                            wait(csem[d.eng], d.eng, d.count)
                    if op.dma and op.dprev:
                        wait(dsem[op.dsem], op.dsem, 16 * op.dprev)
                    ins = op.fn(eng)
                    if op.dma:
                        ins.then_inc(dsem[op.dsem], 16)
                    elif op.signal:
                        ins.then_inc(csem[e], 1)
                if e == "sp":
                    for k, n in self.dsem_n.items():
                        eng.wait_ge(dsem[k], 16 * n)
                    for e2 in ENGS:
                        ops2 = [o for o in self.ops[e2] if o.signal and not o.dma]
                        if ops2:
                            eng.wait_ge(csem[e2], ops2[-1].count)

            @block.tensor
            def _(eng):
                run("pe", eng)

            @block.scalar
            def _(eng):
                run("act", eng)

            @block.vector
            def _(eng):
                run("dve", eng)

            @block.gpsimd
            def _(eng):
                run("pool", eng)

            @block.sync
            def _(eng):
                run("sp", eng)


class Arena:
    def __init__(self, t32, tbf, nwords):
        self.t32, self.tbf, self.n = t32, tbf, nwords
        self.off = 0
        self.uid = 0

    def mark(self):
        return self.off

    def reset(self, m=0):
        self.off = m

    def _take(self, words):
        o = self.off
        self.off += (words + 31) // 32 * 32
        assert self.off <= self.n, ("arena overflow", self.off, self.n)
        self.uid += 1
        return o

    def f32(self, n):
        o = self._take(n)
        return self.t32[:, o:o + n]

    def bf(self, n):
        o = self._take((n + 1) // 2)
        return self.tbf[:, 2 * o:2 * o + n]


def build_nc(dbg=False, a1_only=False, st_list=None):
    nc = bass.Bass("TRN2", target_bir_lowering=False)

    def din(name, shape, dt=F32):
        return nc.dram_tensor(name, list(shape), dt, kind="ExternalInput").ap()

    def dscr(name, shape, dt):
        return nc.dram_tensor(name, list(shape), dt, kind="ExternalOutput" if dbg else "Internal").ap()

    x = din("x", [S_ALL, D])
    w_in = din("w_in", [D, NIN])
    w_out = din("w_out", [D, D])
    w_gate = din("w_gate", [D, DFF])
    w_up = din("w_up", [D, DFF])
    w_down = din("w_down", [DFF, D])
    gmix = din("gmix", [1, D])
    gout = din("gout", [1, D])
    gffn = din("gffn", [1, D])
    hg = din("hg", [128, 8])
    sinks = din("sinks", [1, 16])
    posT = din("posT", [2, 128, 32])
    cw1 = din("cw1", [2, 4096, 256])
    cb1 = din("cb1", [2, 128, 2])
    cw2 = din("cw2", [2, 256, 128])
    cb2c = din("cb2c", [128, 2])
    cb2r = din("cb2r", [1, 128])
    convw = din("convw", [128, NFC * 3])
    convb = din("convb", [128, NFC])
    tabA = din("tabA", [4, 2, 128, 512])
    tabB = din("tabB", [4, 4, 128, 512])
    tabC = din("tabC", [NQT, 4, 128, 512])
    cfar = din("cfar", [128, NQT * 16])
    selV = din("selV", [128, NQT * 128])
    selA = din("selA", [128, NQT * 128])
    selF = din("selF", [128, NQT * 128])
    padc = din("padc", [128, 4])
    wpad = din("wpad", [128, 13])
    hflag = din("hflag", [128, 2])
    emast = din("emast", [128, S_ALL])
    ovl = din("ovl", [128, 4 * 128])
    identf = din("identf", [128, 128])
    y = nc.dram_tensor("y", [TOK, D], F32, kind="ExternalOutput").ap()

    qT_d = dscr("qT_d", [32, 128, 1536], BF16)
    akT_d = dscr("akT_d", [4, 128, 1536], BF16)
    av_d = dscr("av_d", [1536, 512], BF16)
    kwT_d = dscr("kwT_d", [4, 128, 2048], BF16)
    vw_d = dscr("vw_d", [2048, 512], BF16)
    ksT_d = dscr("ksT_d", [4, 128, S_ALL], BF16)
    vs_d = dscr("vs_d", [S_ALL, 512], BF16)
    kcr_d = dscr("kcr_d", [4, 128, S_ALL + 16], BF16)
    vcr_d = dscr("vcr_d", [4, 128, S_ALL + 16], BF16)
    kcT_d = dscr("kcT_d", [4, 128, 512], BF16)
    vc_d = dscr("vc_d", [4, 512, 128], BF16)
    o_d = dscr("o_d", [NQT * 128, D], BF16)
    x1_d = dscr("x1_d", [NQT * 128, D], F32)
    hT_d = dscr("hT_d", [NFC, 128, TOK], BF16)

    S = Sched(nc)
    NW = 50500
    with contextlib.ExitStack() as st:
        ar32 = st.enter_context(nc.sbuf_tensor("arena", [128, NW], F32))
        AR = Arena(ar32, ar32.bitcast(BF16), NW)
        banks = [st.enter_context(nc.psum_tensor("pb%d" % i, [128, 512], F32)) for i in range(8)]
        banks_bf = [b.bitcast(BF16) for b in banks]
        uid = [0]

        def key(p):
            uid[0] += 1
            return "%s#%d" % (p, uid[0])

        ident = AR.bf(128)
        ones = AR.bf(128)
        onec = AR.bf(2)
        gates = AR.f32(12 * 48)
        ssqA = AR.f32(NQT * 4)
        ssqB = AR.f32(NQT * 4)
        ssqF = AR.f32(NQT * 8)
        hgc = AR.f32(8)
        hgq = AR.f32(2)
        small = AR.f32(64)
        epsc = AR.f32(2)
        base_mark = AR.mark()
        S.add("dve", lambda e: e.memset(epsc, EPS), writes=["epsc"])

        S.add("pool", lambda e: e.dma_start(out=ident, in_=identf[:, :]), writes=["ident"], dma=True)
        S.add("dve", lambda e: e.memset(ones, 1.0), writes=["ones"])
        S.add("dve", lambda e: e.memset(onec, 1.0), writes=["onec"])
        S.add("sp", lambda e: e.dma_start(out=hgc, in_=hg[:, :]), writes=["hgc"], dma=True)
        S.add("dve", lambda e: e.tensor_scalar(out=hgq[:, 0:1], in0=hgc[:, 0:1], scalar1=SCALE, scalar2=None,
                                               op0=ALU.mult), reads=["hgc"], writes=["hgq0"])
        S.add("dve", lambda e: e.tensor_scalar(out=hgq[:, 1:2], in0=hgc[:, 2:3], scalar1=SCALE, scalar2=None,
                                               op0=ALU.mult), reads=["hgc"], writes=["hgq1"])
        S.add("dve", lambda e: e.memset(ssqA, 0.0), writes=["ssqA"])
        S.add("dve", lambda e: e.memset(ssqB, 0.0), writes=["ssqB"])
        S.add("dve", lambda e: e.memset(ssqF, 0.0), writes=["ssqF"])

        evac_rr = [0]

        def evac_copy(out, in_, reads, writes):
            evac_rr[0] += 1
            if evac_rr[0] % 2:
                S.add("act", lambda e: e.copy(out=out, in_=in_), reads=reads, writes=writes)
            else:
                S.add("dve", lambda e: e.tensor_copy(out=out, in_=in_), reads=reads, writes=writes)

        def rstd_from(out, in_, inv_n, reads, writes, tmp):
            S.add("act", lambda e: e.activation(out=tmp, in_=in_, func=AF.Sqrt, scale=inv_n, bias=epsc[:, 0:1]),
                  reads=list(reads) + ["epsc"], writes=[writes[0] + "t"])
            S.add("dve", lambda e: e.reciprocal(out=out, in_=tmp), reads=[writes[0] + "t"], writes=writes)

        gbc = AR.f32(D)
        S.add("sp", lambda e: e.dma_start(out=gbc, in_=gmix[0].partition_broadcast(128)), writes=["gbc"], dma=True)
        xbuf = [AR.f32(D) for _ in range(2)]
        xs = AR.bf(D)
        hnT = AR.bf(32 * 512)
        hnT3 = hnT.rearrange("p (k t) -> p k t", t=512)
        wbuf = [AR.bf(32 * 512) for _ in range(2)]
        wbuf3 = [w.rearrange("p (k c) -> p k c", c=512) for w in wbuf]
        sqb = [AR.bf(512) for _ in range(2)]
        rsb = [AR.f32(512) for _ in range(2)]
        rtmp = [AR.f32(512) for _ in range(2)]
        stg = [AR.bf(512) for _ in range(3)]
        zer = AR.bf(16)
        stat = AR.f32(4)
        S.add("dve", lambda e: e.memset(zer, 0.0), writes=["zer"])
        for g in range(4):
            for dd in (kcr_d, vcr_d):
                S.add("sp", (lambda dd, g: lambda e: e.dma_start(out=dd[g][:, S_ALL:S_ALL + 16], in_=zer))(dd, g),
                      reads=["zer"], dma=True)

        chunks = []
        for i in range(4):
            chunks.append((i * 512, 512, "q", 13, (i * 4, 0)))
        chunks.append((2048, 512, "k", 13, ("ak",)))
        chunks.append((2560, 512, "v", 13, ("av",)))
        for i in range(4):
            chunks.append((3072 + i * 512, 512, "q", 13, (16 + i * 4, 1)))
        chunks.append((5120, 512, "raw", 0, (kcr_d,)))
        chunks.append((5632, 512, "raw", 0, (vcr_d,)))
        chunks.append((6144, 512, "k", 0, ("ks",)))
        chunks.append((6656, 512, "v", 0, ("vs",)))
        chunks.append((7168, 512, "k", 12, ("kw",)))
        chunks.append((7680, 512, "v", 12, ("vw",)))
        chunks.append((8192, 48, "g", 13, ()))

        wcnt = [0]
        pscnt = [0]
        st3 = [0]

        def proj_chunk(sti, ch):
            c0, ncol, kind, _, meta = ch
            wi = wcnt[0] % 2
            wcnt[0] += 1
            wk = "w%d" % wi
            w3 = wbuf3[wi]
            S.add("pool", lambda e: e.dma_start(out=w3[:, :, 0:ncol],
                                                in_=w_in[:, c0:c0 + ncol].rearrange("(k p) c -> p k c", p=128)),
                  writes=[wk], dma=True)
            if kind in ("q", "k", "raw"):
                for hh in range(4):
                    bi = 2 + pscnt[0] % 3
                    pscnt[0] += 1
                    pb = banks[bi]
                    pk = "bank%d" % bi
                    for kc in range(32):
                        S.add("pe", (lambda kc, hh, pb: lambda e: e.matmul(
                            pb[:, :], lhsT=w3[:, kc, hh * 128:(hh + 1) * 128], rhs=hnT3[:, kc, :],
                            start=(kc == 0), stop=(kc == 31)))(kc, hh, pb),
                            reads=[wk, "hnT"], writes=[pk])
                    si = st3[0] % 3
                    st3[0] += 1
                    sg = stg[si]
                    sk = "stg%d" % si
                    if kind == "raw":
                        evac_copy(sg, pb[:, :], [pk], [sk])
                        dd = meta[0]
                        S.add("sp", (lambda dd, hh, sg: lambda e: e.dma_start(
                            out=dd[hh][:, sti * 512:(sti + 1) * 512], in_=sg))(dd, hh, sg), reads=[sk], dma=True)
                        continue
                    j = si % 2
                    S.add("act", (lambda pb, j: lambda e: e.activation(out=sqb[j], in_=pb[:, :], func=AF.Square))(pb, j),
                          reads=[pk], writes=["sqb%d" % j])
                    b2 = 5 + j
                    S.add("pe", (lambda j, b2: lambda e: e.matmul(banks[b2][:, :], lhsT=ones, rhs=sqb[j],
                                                                  start=True, stop=True))(j, b2),
                          reads=["ones", "sqb%d" % j], writes=["bank%d" % b2])
                    rstd_from(rsb[j], banks[b2][:, :], 1.0 / 128, ["bank%d" % b2], ["rsb%d" % j], rtmp[j])
                    if kind == "q":
                        gcol = hgq[:, meta[1]:meta[1] + 1]
                        gk = "hgq%d" % meta[1]
                    else:
                        ci = {"ak": 1, "ks": 4, "kw": 5}[meta[0]]
                        gcol = hgc[:, ci:ci + 1]
                        gk = "hgc"
                    S.add("dve", (lambda pb, j, sg, gcol: lambda e: e.scalar_tensor_tensor(
                        out=sg, in0=pb[:, :], scalar=gcol, in1=rsb[j], op0=ALU.mult, op1=ALU.mult))(pb, j, sg, gcol),
                        reads=[pk, "rsb%d" % j, gk], writes=[sk])
                    if kind == "q":
                        dst = qT_d[meta[0] + hh][:, (sti - 13) * 512:(sti - 12) * 512]
                    elif meta[0] == "ak":
                        dst = akT_d[hh][:, (sti - 13) * 512:(sti - 12) * 512]
                    elif meta[0] == "ks":
                        dst = ksT_d[hh][:, sti * 512:(sti + 1) * 512]
                    else:
                        dst = kwT_d[hh][:, (sti - 12) * 512:(sti - 11) * 512]
                    S.add("sp", (lambda dst, sg: lambda e: e.dma_start(out=dst, in_=sg))(dst, sg), reads=[sk], dma=True)
            else:
                for tt in range(4):
                    bi = 2 + pscnt[0] % 3
                    pscnt[0] += 1
                    pb = banks[bi]
                    pk = "bank%d" % bi
                    for kc in range(32):
                        S.add("pe", (lambda kc, tt, pb: lambda e: e.matmul(
                            pb[:, 0:ncol], lhsT=hnT3[:, kc, tt * 128:(tt + 1) * 128], rhs=w3[:, kc, 0:ncol],
                            start=(kc == 0), stop=(kc == 31)))(kc, tt, pb),
                            reads=[wk, "hnT"], writes=[pk])
                    if kind == "g":
                        ti = (sti - 13) * 4 + tt
                        S.add("act", (lambda pb, ti: lambda e: e.activation(
                            out=gates[:, ti * 48:(ti + 1) * 48], in_=pb[:, 0:48], func=AF.Sigmoid))(pb, ti),
                            reads=[pk], writes=["gates%d" % ti])
                        continue
                    si = st3[0] % 3
                    st3[0] += 1
                    sg = stg[si]
                    sk = "stg%d" % si
                    evac_copy(sg, pb[:, :], [pk], [sk])
                    r0 = sti * 512 + tt * 128
                    if meta[0] == "av":
                        dst = av_d[r0 - 13 * 512:r0 - 13 * 512 + 128, :]
                    elif meta[0] == "vs":
                        dst = vs_d[r0:r0 + 128, :]
                    else:
                        dst = vw_d[r0 - 12 * 512:r0 - 12 * 512 + 128, :]
                    S.add("sp", (lambda dst, sg: lambda e: e.dma_start(out=dst, in_=sg))(dst, sg), reads=[sk], dma=True)

        def build_norm_T(src_rows, gb, dst3, tcol, xk_i, stat_ap, extra_scale=None):
            xb_ = xbuf[xk_i % 2]
            xk = "xbuf%d" % (xk_i % 2)
            S.add("sp", lambda e: e.dma_start(out=xb_, in_=src_rows), writes=[xk], dma=True)
            S.add("dve", lambda e: e.memset(stat_ap[:, 0:1], 0.0), writes=["stat0"])
            S.add("act", lambda e: e.activation(out=xs, in_=xb_, func=AF.Square, accum_out=stat_ap[:, 0:1]),
                  reads=[xk, "stat0"], writes=["xs", "stat0"])
            rstd_from(stat_ap[:, 1:2], stat_ap[:, 0:1], 1.0 / D, ["stat0"], ["stat1"], stat_ap[:, 2:3])
            S.add("dve", lambda e: e.scalar_tensor_tensor(out=xs, in0=xb_, scalar=stat_ap[:, 1:2], in1=gb,
                                                          op0=ALU.mult, op1=ALU.mult),
                  reads=[xk, "stat1", "gbc"], writes=["xs"])
            for q4 in range(4):
                bi = q4 % 2
                pbf = banks_bf[bi]
                pk = "bank%d" % bi
                for k8 in range(8):
                    kc = q4 * 8 + k8
                    S.add("pe", (lambda kc, k8, pbf: lambda e: e.transpose(
                        pbf[:, k8 * 128:(k8 + 1) * 128], xs[:, kc * 128:(kc + 1) * 128], ident))(kc, k8, pbf),
                        reads=["xs", "ident"], writes=[pk])
                yield q4, pbf, pk

        def norm_tile_to(src_rows, gb, dst3, dkey, tcol, xk_i, ncols=128, src_c0=0):
            for q4, pbf, pk in build_norm_T(src_rows, gb, dst3, tcol, xk_i, stat):
                evac_copy(dst3[:, q4 * 8:(q4 + 1) * 8, tcol:tcol + ncols],
                          pbf.rearrange("p (k t) -> p k t", t=128)[:, :, src_c0:src_c0 + ncols], [pk], [dkey])

        tile_i = 0
        hdbg = nc.dram_tensor("hdbg", [16, 128, 32 * 512], BF16, kind="ExternalOutput").ap() if a1_only else None
        for sti in (st_list if st_list is not None else range(16)):
            for tt in range(4):
                tl = sti * 4 + tt
                norm_tile_to(x[tl * 128:(tl + 1) * 128, :], gbc, hnT3, "hnT", tt * 128, tile_i)
                tile_i += 1
            if a1_only == 1:
                S.add("sp", (lambda sti: lambda e: e.dma_start(out=hdbg[sti], in_=hnT))(sti), reads=["hnT"], dma=True)
            for ch in chunks:
                if sti >= ch[3]:
                    proj_chunk(sti, ch)
        S.fence()
        if a1_only:
            S.emit()
            return nc

        AR.reset(base_mark)
        w1 = AR.bf(32 * 256)
        w13 = w1.rearrange("p (l h) -> p l h", h=256)
        w2 = AR.bf(2 * 128)
        w23 = w2.rearrange("p (c d) -> p c d", d=128)
        posb = AR.bf(32)
        b1c = AR.f32(2)
        c1 = AR.f32(2)
        b2c = AR.f32(2)
        b2r = AR.f32(128)
        rawT = [AR.bf(S_ALL + 16) for _ in range(2)]
        hid = AR.bf(2 * 512)
        hid3 = hid.rearrange("p (c n) -> p c n", n=512)
        u_ = AR.f32(512)
        t_ = AR.f32(512)
        sg_ = AR.f32(512)
        kraw = AR.f32(512)
        sq2 = AR.bf(512)
        rs2 = AR.f32(512)
        rt2 = AR.f32(512)
        kco = AR.bf(512)
        vco = AR.bf(128)
        S.add("sp", lambda e: e.dma_start(out=b2c, in_=cb2c[:, :]), writes=["b2c"], dma=True)
        S.add("sp", lambda e: e.dma_start(out=b2r, in_=cb2r[0].partition_broadcast(128)), writes=["b2r"], dma=True)
        GC = 2.0 * math.sqrt(2.0 / math.pi)
        ri = 0
        for kv in range(2):
            S.add("pool", (lambda kv: lambda e: e.dma_start(
                out=w13, in_=cw1[kv].rearrange("(l p) h -> p l h", p=128)))(kv), writes=["w1"], dma=True)
            S.add("pool", (lambda kv: lambda e: e.dma_start(
                out=w23, in_=cw2[kv].rearrange("(c p) d -> p c d", p=128)))(kv), writes=["w2"], dma=True)
            S.add("pool", (lambda kv: lambda e: e.dma_start(out=posb, in_=posT[kv]))(kv), writes=["posb"], dma=True)
            S.add("sp", (lambda kv: lambda e: e.dma_start(out=b1c, in_=cb1[kv]))(kv), writes=["b1c"], dma=True)
            for hc in range(2):
                for l in range(32):
                    S.add("pe", (lambda hc, l: lambda e: e.matmul(
                        banks[0][:, 0:1], lhsT=w13[:, l, hc * 128:(hc + 1) * 128], rhs=posb[:, l:l + 1],
                        start=(l == 0), stop=(l == 31)))(hc, l), reads=["w1", "posb"], writes=["bank0"])
                S.add("dve", (lambda hc: lambda e: e.tensor_tensor(
                    out=c1[:, hc:hc + 1], in0=banks[0][:, 0:1], in1=b1c[:, hc:hc + 1], op=ALU.add))(hc),
                    reads=["bank0", "b1c"], writes=["c1_%d" % hc])
            for g in range(4):
                rT = rawT[ri % 2]
                rk = "rawT%d" % (ri % 2)
                ri += 1
                src = (kcr_d if kv == 0 else vcr_d)[g]
                S.add("sp", (lambda rT, src: lambda e: e.dma_start(out=rT, in_=src))(rT, src), writes=[rk], dma=True)
                for hc in range(2):
                    pb = banks[1 + hc]
                    pk = "bank%d" % (1 + hc)
                    for l in range(32):
                        S.add("pe", (lambda hc, l, pb, rT: lambda e: e.matmul(
                            pb[:, :], lhsT=w13[:, l, hc * 128:(hc + 1) * 128], rhs=rT[:, l:l + 16 * 511 + 1:16],
                            start=(l == 0), stop=(l == 31)))(hc, l, pb, rT), reads=["w1", rk], writes=[pk])
                    S.add("act", (lambda hc, pb: lambda e: e.activation(
                        out=u_, in_=pb[:, :], func=AF.Identity, bias=c1[:, hc:hc + 1]))(hc, pb),
                        reads=[pk, "c1_%d" % hc], writes=["u_"])
                    S.add("dve", lambda e: e.tensor_tensor(out=t_, in0=u_, in1=u_, op=ALU.mult),
                          reads=["u_"], writes=["t_"])
                    S.add("dve", lambda e: e.tensor_scalar(out=t_, in0=t_, scalar1=0.044715, scalar2=1.0,
                                                           op0=ALU.mult, op1=ALU.add), reads=["t_"], writes=["t_"])
                    S.add("dve", lambda e: e.tensor_tensor(out=t_, in0=t_, in1=u_, op=ALU.mult),
                          reads=["t_", "u_"], writes=["t_"])
                    S.add("act", lambda e: e.activation(out=sg_, in_=t_, func=AF.Sigmoid, scale=GC),
                          reads=["t_"], writes=["sg_"])
                    S.add("dve", (lambda hc: lambda e: e.tensor_tensor(out=hid3[:, hc, :], in0=u_, in1=sg_,
                                                                       op=ALU.mult))(hc),
                          reads=["u_", "sg_"], writes=["hid%d" % hc])
                if kv == 0:
                    for hc in range(2):
                        S.add("pe", (lambda hc: lambda e: e.matmul(
                            banks[3][:, :], lhsT=w23[:, hc, :], rhs=hid3[:, hc, :], start=(hc == 0), stop=(hc == 1)))(hc),
                            reads=["w2", "hid%d" % hc], writes=["bank3"])
                    S.add("act", lambda e: e.activation(out=kraw, in_=banks[3][:, :], func=AF.Identity,
                                                        bias=b2c[:, 0:1]), reads=["bank3", "b2c"], writes=["kraw"])
                    S.add("act", lambda e: e.activation(out=sq2, in_=kraw, func=AF.Square),
                          reads=["kraw"], writes=["sq2"])
                    S.add("pe", lambda e: e.matmul(banks[4][:, :], lhsT=ones, rhs=sq2, start=True, stop=True),
                          reads=["ones", "sq2"], writes=["bank4"])
                    rstd_from(rs2, banks[4][:, :], 1.0 / 128, ["bank4"], ["rs2"], rt2)
                    S.add("dve", lambda e: e.scalar_tensor_tensor(out=kco, in0=kraw, scalar=hgc[:, 3:4], in1=rs2,
                                                                  op0=ALU.mult, op1=ALU.mult),
                          reads=["kraw", "rs2", "hgc"], writes=["kco"])
                    S.add("sp", (lambda g: lambda e: e.dma_start(out=kcT_d[g], in_=kco))(g), reads=["kco"], dma=True)
                else:
                    for t in range(4):
                        for hc in range(2):
                            S.add("pe", (lambda hc, t: lambda e: e.matmul(
                                banks[5][:, 0:128], lhsT=hid3[:, hc, t * 128:(t + 1) * 128], rhs=w23[:, hc, :],
                                start=(hc == 0), stop=(hc == 1)))(hc, t),
                                reads=["w2", "hid%d" % hc], writes=["bank5"])
                        S.add("dve", lambda e: e.tensor_tensor(out=vco, in0=banks[5][:, 0:128], in1=b2r, op=ALU.add),
                              reads=["bank5", "b2r"], writes=["vco"])
                        S.add("sp", (lambda g, t: lambda e: e.dma_start(out=vc_d[g][t * 128:(t + 1) * 128, :],
                                                                        in_=vco))(g, t), reads=["vco"], dma=True)
        S.fence()

        AR.reset(base_mark)
        tA = AR.bf(8 * 512)
        tA3 = tA.rearrange("p (t c) -> p t c", c=512)
        tB = AR.bf(16 * 512)
        tB3 = tB.rearrange("p (t c) -> p t c", c=512)
        em = AR.bf(S_ALL)
        ov = AR.bf(512)
        ov3 = ov.rearrange("p (t b) -> p t b", b=128)
        pdc = AR.f32(4)
        wpd = AR.f32(13)
        esk = AR.f32(16)
        cfp = AR.f32(NQT * 4 * 4)
        sV = AR.f32(NQT * 128)
        sA = AR.f32(NQT * 128)
        sF = AR.f32(NQT * 128)
        qT = AR.bf(4 * 1152)
        qT4 = qT.rearrange("p (q r t) -> p q r t", r=4, t=128)
        kT = AR.bf(S_ALL)
        vv = AR.bf(64 * 128)
        vv3 = vv.rearrange("p (t d) -> p t d", d=128)
        kw = AR.bf(13 * 128)
        vw = AR.bf(13 * 128)
        vw3 = vw.rearrange("p (t d) -> p t d", d=128)
        kc = AR.bf(512)
        vcs = AR.bf(512)
        vcs3 = vcs.rearrange("p (t d) -> p t d", d=128)
        tC = [AR.bf(512) for _ in range(2)]
        PT = [AR.bf(512) for _ in range(3)]
        addT = AR.bf(512)
        ob32 = AR.f32(512)
        obb = [AR.bf(512) for _ in range(2)]
        den = AR.f32(4)
        rden = AR.f32(4)
        sc = AR.f32(4)
        fin = AR.f32(128)
        fin2 = AR.f32(128)
        m8 = AR.f32(16)
        selb = AR.bf(128)
        junk = AR.bf(512)
        oTs = AR.f32(512)
        dns = AR.f32(512)
        idf = AR.f32(128)
        S.add("sp", lambda e: e.dma_start(out=idf, in_=identf[:, :]), writes=["idf"], dma=True)

        S.add("pool", lambda e: e.dma_start(out=tA3, in_=tabA.rearrange("g j k c -> k (g j) c")),
              writes=["tA"], dma=True)
        S.add("pool", lambda e: e.dma_start(out=tB3, in_=tabB.rearrange("g j k c -> k (g j) c")),
              writes=["tB"], dma=True)
        S.add("pool", lambda e: e.dma_start(out=em, in_=emast[:, :]), writes=["em"], dma=True)
        S.add("pool", lambda e: e.dma_start(out=ov, in_=ovl[:, :]), writes=["ov"], dma=True)
        S.add("sp", lambda e: e.dma_start(out=pdc, in_=padc[:, :]), writes=["pdc"], dma=True)
        S.add("sp", lambda e: e.dma_start(out=esk, in_=sinks[0].partition_broadcast(128)), writes=["esk"], dma=True)
        S.add("act", lambda e: e.activation(out=esk, in_=esk, func=AF.Exp), reads=["esk"], writes=["esk"])
        S.add("sp", lambda e: e.dma_start(out=cfp, in_=cfar[:, :]), writes=["cfp"], dma=True)
        S.add("sp", lambda e: e.dma_start(out=wpd, in_=wpad[:, :]), writes=["pdc"], dma=True)
        S.add("dve", lambda e: e.tensor_scalar(out=cfp, in0=cfp, scalar1=-NEGM, scalar2=None, op0=ALU.add),
              reads=["cfp"], writes=["cfp"])
        for nm, dst, src in (("sV", sV, selV), ("sA", sA, selA), ("sF", sF, selF)):
            S.add("sp", (lambda dst, src: lambda e: e.dma_start(out=dst, in_=src[:, :]))(dst, src),
                  writes=[nm], dma=True)

        ps_rr = [0]
        acc_rr = [0]
        pt_rr = [0]
        dcol = [0]
        ob_rr = [0]

        def attend(rhs_q, tiles, pad_bias=None, vs=False):
            ao = 2 + acc_rr[0] % 2
            acc_rr[0] += 1
            dc_ = (dcol[0] % 16) * 4
            dcol[0] += 1
            aok = "bank%d" % ao
            dk = "bank5_%d" % dc_
            n = len(tiles)
            stt = {}

            def s_part(ti):
                kap, kkeys, terms, vap, vkeys, xrhs = tiles[ti]
                bi = ps_rr[0] % 2
                ps_rr[0] += 1
                pb = banks[bi]
                pk = "bank%d" % bi
                S.add("pe", (lambda pb, kap, nt: lambda e: e.matmul(pb[:, :], lhsT=kap, rhs=rhs_q, start=True,
                                                                    stop=(nt == 0)))(pb, kap, len(terms)),
                      reads=list(kkeys) + ["qT"], writes=[pk])
                for i2, (tl, tr, tk) in enumerate(terms):
                    S.add("pe", (lambda pb, tl, tr, last: lambda e: e.matmul(pb[:, :], lhsT=tl, rhs=tr, start=False,
                                                                             stop=last))(pb, tl, tr, i2 == len(terms) - 1),
                          reads=list(tk), writes=[pk])
                pi = pt_rr[0] % 3
                pt_rr[0] += 1
                P = PT[pi]
                ptk = "PT%d" % pi
                if pad_bias is not None and pad_bias[ti] is not None:
                    S.add("act", (lambda P, pb, bb: lambda e: e.activation(out=P, in_=pb[:, :], func=AF.Exp, bias=bb))(
                        P, pb, pad_bias[ti]), reads=[pk, "pdc"], writes=[ptk])
                else:
                    S.add("act", (lambda P, pb: lambda e: e.activation(out=P, in_=pb[:, :], func=AF.Exp))(P, pb),
                          reads=[pk], writes=[ptk])
                stt[ti] = (P, ptk)

            def pv_part(ti):
                kap, kkeys, terms, vap, vkeys, xrhs = tiles[ti]
                P, ptk = stt.pop(ti)
                if vs:
                    S.add("pe", (lambda P, vap, ti: lambda e: e.matmul(banks[7][:, :], lhsT=vap, rhs=P, start=(ti == 0),
                                                                       stop=(ti == n - 1)))(P, vap, ti),
                          reads=[ptk] + list(vkeys), writes=["bank7"])
                    S.add("pe", (lambda P, ti: lambda e: e.matmul(banks[6][0:1, :], lhsT=onec[:, 0:1], rhs=P,
                                                                  start=(ti == 0), stop=(ti == n - 1)))(P, ti),
                          reads=[ptk, "onec"], writes=["bank6"])
                    return
                for r in range(4):
                    Pr = P[:, r * 128:(r + 1) * 128]
                    S.add("pe", (lambda Pr, r, vap, ti: lambda e: e.matmul(
                        banks[ao][:, r * 128:(r + 1) * 128], lhsT=Pr, rhs=vap, start=(ti == 0 and r == 0),
                        stop=(ti == n - 1)))(Pr, r, vap, ti), reads=[ptk] + list(vkeys), writes=[aok])
                    S.add("pe", (lambda Pr, r, ti: lambda e: e.matmul(
                        banks[5][:, dc_ + r:dc_ + r + 1], lhsT=Pr, rhs=onec[:, 0:1], start=(ti == 0 and r == 0),
                        stop=(ti == n - 1)))(Pr, r, ti), reads=[ptk, "onec"], writes=[dk])
                    if xrhs is not None:
                        S.add("pe", (lambda Pr, r, ti, xr: lambda e: e.matmul(
                            banks[4][:, r * 128:(r + 1) * 128], lhsT=Pr, rhs=xr, start=(ti == 0 and r == 0),
                            stop=(ti == n - 1)))(Pr, r, ti, xrhs), reads=[ptk, "ov"], writes=["bank4"])

            s_part(0)
            for ti in range(n):
                if ti + 1 < n:
                    s_part(ti + 1)
                pv_part(ti)
            if vs:
                S.add("act", lambda e: e.copy(out=oTs, in_=banks[7][:, :]), reads=["bank7"], writes=["oTs"])
                S.add("dve", lambda e: e.tensor_copy(out=dns[0:1, :], in_=banks[6][0:1, :]), reads=["bank6"],
                      writes=["dns"])
                for r in range(4):
                    S.add("pe", (lambda r: lambda e: e.transpose(banks[ao][:, r * 128:(r + 1) * 128],
                                                                 oTs[:, r * 128:(r + 1) * 128], idf))(r),
                          reads=["oTs", "idf"], writes=[aok])
                for r in range(4):
                    S.add("pe", (lambda r: lambda e: e.matmul(banks[5][:, dc_ + r:dc_ + r + 1],
                                                              lhsT=dns[0:1, r * 128:(r + 1) * 128], rhs=idf[0:1, 0:1],
                                                              start=True, stop=True))(r),
                          reads=["dns", "idf"], writes=[dk])
            return ao, dc_

        def finish(ao, dc_, gate_cols, first, add_sink=None):
            aok = "bank%d" % ao
            dk = "bank5_%d" % dc_
            if add_sink is not None:
                S.add("dve", lambda e: e.tensor_tensor(out=den, in0=banks[5][:, dc_:dc_ + 4], in1=add_sink, op=ALU.add),
                      reads=[dk, "esk"], writes=["den"])
            else:
                S.add("dve", lambda e: e.tensor_scalar(out=den, in0=banks[5][:, dc_:dc_ + 4], scalar1=1e-30,
                                                       scalar2=None, op0=ALU.max), reads=[dk], writes=["den"])
            S.add("dve", lambda e: e.reciprocal(out=rden, in_=den), reads=["den"], writes=["rden"])
            if gate_cols is not None:
                S.add("dve", lambda e: e.tensor_tensor(out=sc, in0=rden, in1=gate_cols, op=ALU.mult),
                      reads=["rden", "gatesall"], writes=["sc"])
                scal, sck = sc, "sc"
            else:
                scal, sck = rden, "rden"
            for r in range(4):
                o_ = ob32[:, r * 128:(r + 1) * 128]
                a_ = banks[ao][:, r * 128:(r + 1) * 128]
                if first:
                    S.add("dve", (lambda o_, a_, r: lambda e: e.tensor_scalar(
                        out=o_, in0=a_, scalar1=scal[:, r:r + 1], scalar2=None, op0=ALU.mult))(o_, a_, r),
                        reads=[aok, sck], writes=["ob32_%d" % r])
                else:
                    S.add("dve", (lambda o_, a_, r: lambda e: e.scalar_tensor_tensor(
                        out=o_, in0=a_, scalar=scal[:, r:r + 1], in1=o_, op0=ALU.mult, op1=ALU.add))(o_, a_, r),
                        reads=[aok, sck, "ob32_%d" % r], writes=["ob32_%d" % r])

        def emit_o(qidx, colblk, ssq_ap, ssq_key):
            oi = ob_rr[0] % 2
            ob_rr[0] += 1
            ob_ = obb[oi]
            okk = "obb%d" % oi
            S.add("pool", lambda e: e.tensor_copy(out=ob_, in_=ob32), reads=["ob32_%d" % r for r in range(4)],
                  writes=[okk])
            S.add("act", lambda e: e.activation(out=junk, in_=ob_, func=AF.Square, accum_out=ssq_ap),
                  reads=[okk], writes=["junk", ssq_key])
            S.add("sp", lambda e: e.dma_start(out=o_d[qidx * 128:(qidx + 1) * 128, colblk * 512:(colblk + 1) * 512],
                                              in_=ob_), reads=[okk], dma=True)

        for grp in range(8):
            isA = grp < 4
            g = grp % 4
            h0 = (0 if isA else 16) + 4 * g
            for r in range(4):
                S.add("sp", (lambda h0, r: lambda e: e.dma_start(
                    out=qT4[:, :, r, :], in_=qT_d[h0 + r][:, 384:1536].rearrange("d (q t) -> d q t", t=128)))(h0, r),
                    writes=["qT"], dma=True)
            if isA:
                S.add("sp", (lambda g: lambda e: e.dma_start(out=kT[:, 0:1280], in_=akT_d[g][:, 256:1536]))(g),
                      writes=["kT"], dma=True)
                S.add("sp", (lambda g: lambda e: e.dma_start(
                    out=vv3[:, 0:10, :], in_=av_d[256:1536, g * 128:(g + 1) * 128].rearrange("(t p) d -> p t d", p=128)))(g),
                    writes=["vv"], dma=True)
            else:
                S.add("sp", (lambda g: lambda e: e.dma_start(out=kT, in_=ksT_d[g]))(g), writes=["kT"], dma=True)
                S.add("sp", (lambda g: lambda e: e.dma_start(
                    out=vv3, in_=vs_d[:, g * 128:(g + 1) * 128].rearrange("(t p) d -> p t d", p=128)))(g),
                    writes=["vv"], dma=True)
                S.add("sp", (lambda g: lambda e: e.dma_start(out=kw, in_=kwT_d[g][:, 384:2048]))(g),
                      writes=["kw"], dma=True)
                S.add("sp", (lambda g: lambda e: e.dma_start(
                    out=vw3, in_=vw_d[384:2048, g * 128:(g + 1) * 128].rearrange("(t p) d -> p t d", p=128)))(g),
                    writes=["vw"], dma=True)
                S.add("sp", (lambda g: lambda e: e.dma_start(out=kc, in_=kcT_d[g]))(g), writes=["kc"], dma=True)
                S.add("sp", (lambda g: lambda e: e.dma_start(
                    out=vcs3, in_=vc_d[g].rearrange("(t p) d -> p t d", p=128)))(g), writes=["vcs"], dma=True)
            for qx in range(NQT):
                Q = 55 + qx
                rhs_q = qT[:, qx * 512:(qx + 1) * 512]
                if isA:
                    tiles = []
                    for j in (1, 0):
                        kt = Q - j - 54
                        tiles.append((kT[:, kt * 128:(kt + 1) * 128], ["kT"],
                                      [(ident, tA3[:, g * 2 + j, :], ["ident", "tA"])],
                                      vv3[:, kt, :], ["vv"], None))
                    ao, dc_ = attend(rhs_q, tiles, pad_bias=[wpd[:, Q - j - 51:Q - j - 50] for j in (1, 0)])
                    finish(ao, dc_, None, True, add_sink=esk[:, 4 * g:4 * g + 4])
                    emit_o(qx, g, ssqA[:, qx * 4 + g:qx * 4 + g + 1], key("ssqA"))
                    continue
                gt = gates[:, (3 + qx) * 48:(4 + qx) * 48].rearrange("p (h b) -> p h b", b=3)
                tci = (grp * NQT + qx) % 2
                tCc = tC[tci]
                S.add("pool", (lambda tCc, qx, g: lambda e: e.dma_start(out=tCc, in_=tabC[qx][g]))(tCc, qx, g),
                      writes=["tC%d" % tci], dma=True)
                tiles = []
                for t in range(4):
                    if t < 3:
                        term = (ident, tB3[:, g * 4 + 2, :], ["ident", "tB"])
                    else:
                        term = (ident, tCc, ["ident", "tC%d" % tci])
                    tiles.append((kc[:, t * 128:(t + 1) * 128], ["kc"], [term], vcs3[:, t, :], ["vcs"], ov3[:, t, :]))
                ao, dc_ = attend(rhs_q, tiles, pad_bias=[pdc[:, t:t + 1] for t in range(4)])
                finish(ao, dc_, gt[:, 4 * g:4 * g + 4, 0], True)
                for r in range(4):
                    a_ = banks[4][:, r * 128:(r + 1) * 128]
                    if r == 0:
                        S.add("dve", (lambda a_: lambda e: e.tensor_scalar(out=fin, in0=a_, scalar1=rden[:, 0:1],
                                                                           scalar2=None, op0=ALU.mult))(a_),
                              reads=["bank4", "rden"], writes=["fin"])
                    else:
                        S.add("dve", (lambda a_, r: lambda e: e.scalar_tensor_tensor(
                            out=fin, in0=a_, scalar=rden[:, r:r + 1], in1=fin, op0=ALU.mult, op1=ALU.add))(a_, r),
                            reads=["bank4", "rden", "fin"], writes=["fin"])
                S.add("dve", (lambda qx: lambda e: e.tensor_tensor(out=fin, in0=fin, in1=sV[:, qx * 128:(qx + 1) * 128],
                                                                   op=ALU.mult))(qx), reads=["fin", "sV"], writes=["fin"])
                S.add("dve", (lambda qx: lambda e: e.tensor_tensor(out=fin, in0=fin, in1=sA[:, qx * 128:(qx + 1) * 128],
                                                                   op=ALU.add))(qx), reads=["fin", "sA"], writes=["fin"])
                S.add("dve", lambda e: e.max(out=m8[:, 0:8], in_=fin), reads=["fin"], writes=["m8a"])
                S.add("dve", lambda e: e.match_replace(out=fin2, in_to_replace=m8[:, 0:8], in_values=fin,
                                                       imm_value=-1e30), reads=["fin", "m8a"], writes=["fin2"])
                S.add("dve", lambda e: e.max(out=m8[:, 8:16], in_=fin2), reads=["fin2"], writes=["m8b"])
                S.add("dve", lambda e: e.tensor_scalar(out=fin2, in0=fin, scalar1=m8[:, 15:16], scalar2=None,
                                                       op0=ALU.is_ge), reads=["fin", "m8b"], writes=["fin2"])
                S.add("dve", (lambda qx: lambda e: e.tensor_tensor(out=selb, in0=fin2, in1=sF[:, qx * 128:(qx + 1) * 128],
                                                                   op=ALU.mult))(qx), reads=["fin2", "sF"], writes=["selb"])
                S.add("pe", lambda e: e.transpose(banks_bf[6][:, 0:128], selb, ident), reads=["selb", "ident"],
                      writes=["bank6"])
                for r in range(4):
                    cc = (qx * 4 + g) * 4 + r
                    S.add("dve", (lambda r, cc: lambda e: e.tensor_scalar(
                        out=addT[:, r * 128:(r + 1) * 128], in0=banks_bf[6][:, 0:128], scalar1=cfp[:, cc:cc + 1],
                        scalar2=NEGM, op0=ALU.mult, op1=ALU.add))(r, cc), reads=["bank6", "cfp"], writes=["addT"])
                tiles = []
                for kt in range(0, Q + 1):
                    j = Q - kt
                    ek = em[:, kt * 128:(kt + 1) * 128]
                    if j == 0:
                        terms = [(ident, tB3[:, g * 4 + 0, :], ["ident", "tB"])]
                    elif j == 1:
                        terms = [(ek, addT, ["em", "addT"]), (ident, tB3[:, g * 4 + 1, :], ["ident", "tB"])]
                    else:
                        terms = [(ek, addT, ["em", "addT"])]
                    tiles.append((kT[:, kt * 128:(kt + 1) * 128], ["kT"], terms, vv3[:, kt, :], ["vv"], None))
                ao, dc_ = attend(rhs_q, tiles, vs=True)
                finish(ao, dc_, gt[:, 4 * g:4 * g + 4, 1], False)
                tiles = []
                for j in (4, 3, 2, 1, 0):
                    kt = Q - j - 51
                    tb = {0: 0, 1: 1, 2: 2, 3: 2, 4: 3}[j]
                    tiles.append((kw[:, kt * 128:(kt + 1) * 128], ["kw"],
                                  [(ident, tB3[:, g * 4 + tb, :], ["ident", "tB"])], vw3[:, kt, :], ["vw"], None))
                ao, dc_ = attend(rhs_q, tiles, pad_bias=[wpd[:, Q - j - 51:Q - j - 50] for j in (4, 3, 2, 1, 0)])
                finish(ao, dc_, gt[:, 4 * g:4 * g + 4, 2], False)
                emit_o(qx, 4 + g, ssqB[:, qx * 4 + g:qx * 4 + g + 1], key("ssqB"))
        S.fence()

        AR.reset(base_mark)
        gob = AR.f32(D)
        S.add("sp", lambda e: e.dma_start(out=gob, in_=gout[0].partition_broadcast(128)), writes=["gbc"], dma=True)
        onT = AR.bf(32 * 1152)
        onT3 = onT.rearrange("p (k t) -> p k t", t=1152)
        obf = [AR.bf(D) for _ in range(2)]
        xs2 = AR.bf(D)
        wo = [AR.bf(32 * 512) for _ in range(2)]
        wo3 = [w.rearrange("p (k c) -> p k c", c=512) for w in wo]
        xc = [AR.f32(512) for _ in range(3)]
        x1c = [AR.f32(512) for _ in range(3)]
        rA = AR.f32(NQT)
        rB = AR.f32(NQT)
        rtm = AR.f32(NQT)
        rtmB = AR.f32(NQT)
        junk2 = AR.bf(512)
        S.add("dve", lambda e: e.tensor_reduce(out=rA, in_=ssqA.rearrange("p (q g) -> p q g", g=4), axis=AX.X,
                                               op=ALU.add), reads=[], writes=["rA0"])
        S.add("dve", lambda e: e.tensor_reduce(out=rB, in_=ssqB.rearrange("p (q g) -> p q g", g=4), axis=AX.X,
                                               op=ALU.add), reads=[], writes=["rB0"])
        rstd_from(rA, rA, 1.0 / 2048, ["rA0"], ["rA"], rtm)
        rstd_from(rB, rB, 1.0 / 2048, ["rB0"], ["rB"], rtmB)
        for qx in range(NQT):
            ob_ = obf[qx % 2]
            okk = "obf%d" % (qx % 2)
            S.add("sp", (lambda ob_, qx: lambda e: e.dma_start(out=ob_, in_=o_d[qx * 128:(qx + 1) * 128, :]))(ob_, qx),
                  writes=[okk], dma=True)
            S.add("dve", (lambda ob_, qx: lambda e: e.scalar_tensor_tensor(
                out=xs2[:, 0:2048], in0=ob_[:, 0:2048], scalar=rA[:, qx:qx + 1], in1=gob[:, 0:2048],
                op0=ALU.mult, op1=ALU.mult))(ob_, qx), reads=[okk, "rA", "gbc"], writes=["xs2a"])
            S.add("dve", (lambda ob_, qx: lambda e: e.scalar_tensor_tensor(
                out=xs2[:, 2048:4096], in0=ob_[:, 2048:4096], scalar=rB[:, qx:qx + 1], in1=gob[:, 2048:4096],
                op0=ALU.mult, op1=ALU.mult))(ob_, qx), reads=[okk, "rB", "gbc"], writes=["xs2b"])
            for q4 in range(4):
                bi = q4 % 2
                pbf = banks_bf[bi]
                pk = "bank%d" % bi
                for k8 in range(8):
                    kc_ = q4 * 8 + k8
                    S.add("pe", (lambda kc_, k8, pbf: lambda e: e.transpose(
                        pbf[:, k8 * 128:(k8 + 1) * 128], xs2[:, kc_ * 128:(kc_ + 1) * 128], ident))(kc_, k8, pbf),
                        reads=["xs2a", "xs2b", "ident"], writes=[pk])
                evac_copy(onT3[:, q4 * 8:(q4 + 1) * 8, qx * 128:(qx + 1) * 128],
                          pbf.rearrange("p (k t) -> p k t", t=128), [pk], ["onT"])
        xi = 0
        for dc in range(8):
            wi = dc % 2
            S.add("pool", (lambda wi, dc: lambda e: e.dma_start(
                out=wo3[wi], in_=w_out[:, dc * 512:(dc + 1) * 512].rearrange("(k p) c -> p k c", p=128)))(wi, dc),
                writes=["wo%d" % wi], dma=True)
            for qx in range(NQT):
                bi = 2 + (dc * NQT + qx) % 3
                pb = banks[bi]
                pk = "bank%d" % bi
                for kc_ in range(32):
                    S.add("pe", (lambda kc_, pb, wi, qx: lambda e: e.matmul(
                        pb[:, :], lhsT=onT3[:, kc_, qx * 128:(qx + 1) * 128], rhs=wo3[wi][:, kc_, :],
                        start=(kc_ == 0), stop=(kc_ == 31)))(kc_, pb, wi, qx), reads=["onT", "wo%d" % wi], writes=[pk])
                j = xi % 3
                xi += 1
                S.add("sp", (lambda j, qx, dc: lambda e: e.dma_start(
                    out=xc[j], in_=x[(55 + qx) * 128:(56 + qx) * 128, dc * 512:(dc + 1) * 512]))(j, qx, dc),
                    writes=["xc%d" % j], dma=True)
                S.add("dve", (lambda j, pb: lambda e: e.tensor_tensor(out=x1c[j], in0=pb[:, :], in1=xc[j],
                                                                      op=ALU.add))(j, pb),
                      reads=[pk, "xc%d" % j], writes=["x1c%d" % j])
                S.add("act", (lambda j, qx, dc: lambda e: e.activation(
                    out=junk2, in_=x1c[j], func=AF.Square, accum_out=ssqF[:, qx * 8 + dc:qx * 8 + dc + 1]))(j, qx, dc),
                    reads=["x1c%d" % j], writes=["junk2", key("ssqF")])
                S.add("sp", (lambda j, qx, dc: lambda e: e.dma_start(
                    out=x1_d[qx * 128:(qx + 1) * 128, dc * 512:(dc + 1) * 512], in_=x1c[j]))(j, qx, dc),
                    reads=["x1c%d" % j], dma=True)
        S.fence()

        AR.reset(base_mark)
        gfb = AR.f32(D)
        S.add("sp", lambda e: e.dma_start(out=gfb, in_=gffn[0].partition_broadcast(128)), writes=["gbc"], dma=True)
        hfT = AR.bf(32 * 1026)
        hfT3 = hfT.rearrange("p (k t) -> p k t", t=1026)
        ffn_mark = AR.mark()
        xbuf = [AR.f32(D) for _ in range(2)]
        xs = AR.bf(D)
        rF = AR.f32(NQT)
        rtm2 = AR.f32(NQT)
        S.add("dve", lambda e: e.tensor_reduce(out=rF, in_=ssqF.rearrange("p (q g) -> p q g", g=8), axis=AX.X,
                                               op=ALU.add), reads=[], writes=["rF0"])
        rstd_from(rF, rF, 1.0 / D, ["rF0"], ["rF"], rtm2)
        for qx in range(NQT):
            xb_ = xbuf[qx % 2]
            xk = "xbuf%d" % (qx % 2)
            S.add("sp", (lambda xb_, qx: lambda e: e.dma_start(out=xb_, in_=x1_d[qx * 128:(qx + 1) * 128, :]))(xb_, qx),
                  writes=[xk], dma=True)
            S.add("dve", (lambda xb_, qx: lambda e: e.scalar_tensor_tensor(
                out=xs, in0=xb_, scalar=rF[:, qx:qx + 1], in1=gfb, op0=ALU.mult, op1=ALU.mult))(xb_, qx),
                reads=[xk, "rF", "gbc"], writes=["xs"])
            for q4 in range(4):
                bi = q4 % 2
                pbf = banks_bf[bi]
                pk = "bank%d" % bi
                for k8 in range(8):
                    kc_ = q4 * 8 + k8
                    S.add("pe", (lambda kc_, k8, pbf: lambda e: e.transpose(
                        pbf[:, k8 * 128:(k8 + 1) * 128], xs[:, kc_ * 128:(kc_ + 1) * 128], ident))(kc_, k8, pbf),
                        reads=["xs", "ident"], writes=[pk])
                src = pbf.rearrange("p (k t) -> p k t", t=128)
                if qx == 0:
                    evac_copy(hfT3[:, q4 * 8:(q4 + 1) * 8, 0:2], src[:, :, 126:128], [pk], ["hfT"])
                else:
                    c0 = 2 + (qx - 1) * 128
                    evac_copy(hfT3[:, q4 * 8:(q4 + 1) * 8, c0:c0 + 128], src, [pk], ["hfT"])
        S.fence()

        AR.reset(ffn_mark)
        cwc = AR.f32(NFC * 3)
        cbc = AR.f32(NFC)
        hfl = AR.f32(2)
        S.add("sp", lambda e: e.dma_start(out=hfl, in_=hflag[:, :]), writes=["hfl"], dma=True)
        S.add("sp", lambda e: e.dma_start(out=cwc, in_=convw[:, :]), writes=["cwc"], dma=True)
        S.add("sp", lambda e: e.dma_start(out=cbc, in_=convb[:, :]), writes=["cbc"], dma=True)
        wg = [AR.bf(32 * 256) for _ in range(2)]
        wu = [AR.bf(32 * 256) for _ in range(2)]
        wg3 = [w.rearrange("p (k c) -> p k c", c=256) for w in wg]
        wu3 = [w.rearrange("p (k c) -> p k c", c=256) for w in wu]
        gsb = [AR.f32(514) for _ in range(2)]
        tcv = [AR.f32(512) for _ in range(2)]
        ssb = [AR.f32(512) for _ in range(2)]
        hst = [AR.bf(1024) for _ in range(2)]
        u2 = 0
        for c2 in range(43):
            wi = c2 % 2
            S.add("pool", (lambda wi, c2: lambda e: e.dma_start(
                out=wg3[wi], in_=w_gate[:, c2 * 256:(c2 + 1) * 256].rearrange("(k p) c -> p k c", p=128)))(wi, c2),
                writes=["wg%d" % wi], dma=True)
            S.add("pool", (lambda wi, c2: lambda e: e.dma_start(
                out=wu3[wi], in_=w_up[:, c2 * 256:(c2 + 1) * 256].rearrange("(k p) c -> p k c", p=128)))(wi, c2),
                writes=["wu%d" % wi], dma=True)
            for sub in range(2):
                fc = c2 * 2 + sub
                hs = hst[fc % 2]
                hk = "hst%d" % (fc % 2)
                for half in range(2):
                    b0 = half * 512
                    u = u2 % 2
                    u2 += 1
                    bG, bU = banks[u * 3], banks[u * 3 + 1]
                    bH = banks[6 + u]
                    kG, kU, kH = "bank%d" % (u * 3), "bank%d" % (u * 3 + 1), "bank%d" % (6 + u)
                    for kc_ in range(32):
                        lg = wg3[wi][:, kc_, sub * 128:(sub + 1) * 128]
                        S.add("pe", (lambda kc_, lg, bG, b0: lambda e: e.matmul(
                            bG[:, :], lhsT=lg, rhs=hfT3[:, kc_, b0 + 2:b0 + 514], start=(kc_ == 0), stop=(kc_ == 31)))(
                            kc_, lg, bG, b0), reads=["wg%d" % wi, "hfT"], writes=[kG])
                        S.add("pe", (lambda kc_, lg, bH, b0: lambda e: e.matmul(
                            bH[:, 0:2], lhsT=lg, rhs=hfT3[:, kc_, b0:b0 + 2], start=(kc_ == 0), stop=(kc_ == 31)))(
                            kc_, lg, bH, b0), reads=["wg%d" % wi, "hfT"], writes=[kH])
                    for kc_ in range(32):
                        lu = wu3[wi][:, kc_, sub * 128:(sub + 1) * 128]
                        S.add("pe", (lambda kc_, lu, bU, b0: lambda e: e.matmul(
                            bU[:, :], lhsT=lu, rhs=hfT3[:, kc_, b0 + 2:b0 + 514], start=(kc_ == 0), stop=(kc_ == 31)))(
                            kc_, lu, bU, b0), reads=["wu%d" % wi, "hfT"], writes=[kU])
                    gs, tc_, ss = gsb[u], tcv[u], ssb[u]
                    S.add("act", (lambda gs, bH, half: lambda e: e.mul(out=gs[:, 0:2], in_=bH[:, 0:2],
                                                                       mul=hfl[:, half:half + 1]))(gs, bH, half),
                          reads=[kH, "hfl"], writes=["gsbh%d" % u])
                    S.add("act", (lambda gs, bG: lambda e: e.copy(out=gs[:, 2:514], in_=bG[:, :]))(gs, bG),
                          reads=[kG], writes=["gsbm%d" % u])
                    gk = ["gsbh%d" % u, "gsbm%d" % u]
                    S.add("dve", (lambda gs, tc_, fc: lambda e: e.tensor_scalar(
                        out=tc_, in0=gs[:, 2:514], scalar1=cwc[:, fc * 3 + 2:fc * 3 + 3], scalar2=None, op0=ALU.mult))(
                        gs, tc_, fc), reads=gk + ["cwc"], writes=["tcv%d" % u])
                    S.add("dve", (lambda gs, tc_, fc: lambda e: e.scalar_tensor_tensor(
                        out=tc_, in0=gs[:, 1:513], scalar=cwc[:, fc * 3 + 1:fc * 3 + 2], in1=tc_, op0=ALU.mult,
                        op1=ALU.add))(gs, tc_, fc), reads=gk + ["cwc", "tcv%d" % u], writes=["tcv%d" % u])
                    S.add("dve", (lambda gs, tc_, fc: lambda e: e.scalar_tensor_tensor(
                        out=tc_, in0=gs[:, 0:512], scalar=cwc[:, fc * 3:fc * 3 + 1], in1=tc_, op0=ALU.mult,
                        op1=ALU.add))(gs, tc_, fc), reads=gk + ["cwc", "tcv%d" % u], writes=["tcv%d" % u])
                    S.add("act", (lambda tc_, ss, fc: lambda e: e.activation(
                        out=ss, in_=tc_, func=AF.Silu, bias=cbc[:, fc:fc + 1]))(tc_, ss, fc),
                        reads=["tcv%d" % u, "cbc"], writes=["ssb%d" % u])
                    S.add("dve", (lambda ss, bU, hs, b0: lambda e: e.tensor_tensor(
                        out=hs[:, b0:b0 + 512], in0=bU[:, :], in1=ss, op=ALU.mult))(ss, bU, hs, b0),
                        reads=["ssb%d" % u, kU], writes=[hk + "_%d" % half])
                S.add("sp", (lambda hs, fc: lambda e: e.dma_start(out=hT_d[fc], in_=hs))(hs, fc),
                      reads=[hk + "_0", hk + "_1"], dma=True)
        S.fence()

        AR.reset(base_mark)
        wd = [AR.bf(4 * 512) for _ in range(2)]
        wd3 = [w.rearrange("p (f c) -> p f c", c=512) for w in wd]
        hb = [AR.bf(4 * 1024) for _ in range(2)]
        hb3 = [h.rearrange("p (f t) -> p f t", t=1024) for h in hb]
        x1b = [AR.f32(512) for _ in range(3)]
        ob = [AR.f32(512) for _ in range(3)]
        li = 0
        oi = 0
        for dc in range(8):
            for fg in range(22):
                nf = 4 if fg < 21 else 2
                i2 = li % 2
                li += 1
                S.add("pool", (lambda i2, fg, nf, dc: lambda e: e.dma_start(
                    out=wd3[i2][:, 0:nf, :],
                    in_=w_down[fg * 512:fg * 512 + nf * 128, dc * 512:(dc + 1) * 512].rearrange("(f p) c -> p f c", p=128)))(
                    i2, fg, nf, dc), writes=["wd%d" % i2], dma=True)
                S.add("sp", (lambda i2, fg, nf: lambda e: e.dma_start(
                    out=hb3[i2][:, 0:nf, :], in_=hT_d[fg * 4:fg * 4 + nf].rearrange("f p t -> p f t")))(i2, fg, nf),
                    writes=["hb%d" % i2], dma=True)
                for f in range(nf):
                    fc = fg * 4 + f
                    for tl in range(8):
                        S.add("pe", (lambda i2, f, tl, fc: lambda e: e.matmul(
                            banks[tl][:, :], lhsT=hb3[i2][:, f, tl * 128:(tl + 1) * 128], rhs=wd3[i2][:, f, :],
                            start=(fc == 0), stop=(fc == NFC - 1)))(i2, f, tl, fc),
                            reads=["wd%d" % i2, "hb%d" % i2], writes=["bank%d" % tl])
            for tl in range(8):
                j = oi % 3
                oi += 1
                S.add("sp", (lambda j, tl, dc: lambda e: e.dma_start(
                    out=x1b[j], in_=x1_d[(tl + 1) * 128:(tl + 2) * 128, dc * 512:(dc + 1) * 512]))(j, tl, dc),
                    writes=["x1b%d" % j], dma=True)
                S.add("dve", (lambda j, tl: lambda e: e.tensor_tensor(out=ob[j], in0=banks[tl][:, :], in1=x1b[j],
                                                                      op=ALU.add))(j, tl),
                      reads=["bank%d" % tl, "x1b%d" % j], writes=["ob%d" % j])
                S.add("sp", (lambda j, tl, dc: lambda e: e.dma_start(
                    out=y[tl * 128:(tl + 1) * 128, dc * 512:(dc + 1) * 512], in_=ob[j]))(j, tl, dc),
                    reads=["ob%d" % j], dma=True)
        S.emit()
    return nc


def _bucket(dist):
    n = np.maximum(dist, 0)
    nf = np.maximum(n, 1).astype(np.float32)
    large = 16 + (np.log(nf / np.float32(16)) / np.float32(math.log(128 / 16)) * np.float32(16)).astype(np.int32)
    large = np.minimum(large, 31)
    return np.where(n < 16, n, large)


def _host_tables(rel_bias, c):
    rb = np.asarray(rel_bias, np.float32)
    k = np.arange(128)[:, None]
    q = np.arange(128)[None, :]
    tabA = np.full((4, 2, 128, 4, 128), NEGM, np.float32)
    tabB = np.full((4, 4, 128, 4, 128), NEGM, np.float32)
    for g in range(4):
        for r in range(4):
            hA = 4 * g + r
            hB = 16 + 4 * g + r
            for j in range(2):
                dist = q - k + 128 * j
                ok = (dist >= 0) & (dist < 128)
                tabA[g, j, :, r, :] = np.where(ok, rb[_bucket(dist), hA], NEGM)
            dist = q - k
            tabB[g, 0, :, r, :] = np.where(dist >= 0, rb[_bucket(dist), hB], NEGM)
            dist = q - k + 128
            tabB[g, 1, :, r, :] = rb[_bucket(dist), hB]
            tabB[g, 2, :, r, :] = rb[31, hB]
            tabB[g, 3, :, r, :] = np.where(k > q, rb[31, hB], NEGM)
    tabC = np.full((NQT, 4, 128, 4, 128), NEGM, np.float32)
    cfar = np.zeros((NQT, 4, 128, 4), np.float32)
    npad_blk = (7 - c) * 16
    selV = np.zeros((NQT, 128, 128), np.float32)
    selA = np.zeros((NQT, 128, 128), np.float32)
    selF = np.zeros((NQT, 128, 128), np.float32)
    blk = np.arange(128)[None, :]
    for qx in range(NQT):
        Q = 55 + qx
        qs = Q * 128 + np.arange(128)
        nn = 384 + np.arange(128)
        dist = qs[None, :] - (16 * nn[:, None] + 31)
        for g in range(4):
            for r in range(4):
                hB = 16 + 4 * g + r
                tabC[qx, g, :, r, :] = np.where(dist >= 0, rb[_bucket(dist), hB], NEGM)
                cfar[qx, g, :, r] = np.where(np.arange(128) < 2 * Q - 2, rb[31, hB], 0.0)
        cur = (qs // 64)[:, None]
        valid = (blk <= cur) & (blk >= npad_blk)
        forced = ((blk == npad_blk) | (blk == cur) | (blk == cur - 1)) & valid
        selF[qx] = valid
        selV[qx] = valid & ~forced
        selA[qx] = np.where(forced, 1e6 + blk * 16.0, np.where(valid, 0.0, -1.0))
    padc = np.zeros((128, 4), np.float32)
    for t in range(4):
        n_ = t * 128 + np.arange(128)
        padc[:, t] = np.where(n_ >= (7 - c) * 64, 0.0, NEGM)
    wpad = np.zeros((128, 13), np.float32)
    for i in range(13):
        wpad[:, i] = 0.0 if (51 + i) * 128 >= (7 - c) * TOK else NEGM
    hflag = np.ones((128, 2), np.float32)
    if c == 0:
        hflag[:, 0] = 0.0
    lay = lambda a: np.ascontiguousarray(a.transpose(1, 0, 2).reshape(128, -1))
    cf = np.ascontiguousarray(cfar.transpose(2, 0, 1, 3).reshape(128, NQT * 16))
    return dict(tabA=tabA.reshape(4, 2, 128, 512), tabB=tabB.reshape(4, 4, 128, 512),
                tabC=tabC.reshape(NQT, 4, 128, 512), cfar=cf, selV=lay(selV), selA=lay(selA), selF=lay(selF),
                padc=padc, wpad=wpad, hflag=hflag)


_NC_CACHE = {}


def kernel(x, rel_bias, norm_mix_g, w_in, a_q_norm_g, a_k_norm_g, a_sinks, b_q_norm_g, b_k_norm_g,
           cmp_pos_emb, cmp_w1, cmp_b1, cmp_w2, cmp_b2, out_norm_g, w_out, norm_ffn_g, w_gate, w_up,
           conv_w, conv_b, w_down):
    if "nc" not in _NC_CACHE:
        _NC_CACHE["nc"] = build_nc()
    nc = _NC_CACHE["nc"]
    in_maps = _prep(x, rel_bias, norm_mix_g, w_in, a_q_norm_g, a_k_norm_g, a_sinks, b_q_norm_g, b_k_norm_g,
                    cmp_pos_emb, cmp_w1, cmp_b1, cmp_w2, cmp_b2, out_norm_g, w_out, norm_ffn_g, w_gate, w_up,
                    conv_w, conv_b, w_down)
    res = run_bass_kernel_spmd(nc, in_maps, core_ids=list(range(NCORES)))
    out = np.concatenate([np.asarray(r["y"], np.float32) for r in res.results], axis=0)
    return out[None]


def _prep(x, rel_bias, norm_mix_g, w_in, a_q_norm_g, a_k_norm_g, a_sinks, b_q_norm_g, b_k_norm_g,
          cmp_pos_emb, cmp_w1, cmp_b1, cmp_w2, cmp_b2, out_norm_g, w_out, norm_ffn_g, w_gate, w_up,
          conv_w, conv_b, w_down):
    f = lambda a: np.ascontiguousarray(np.asarray(a, np.float32))
    x = f(x)[0]
    hg = np.zeros((128, 8), np.float32)
    hg[:, 0] = f(a_q_norm_g)[0]
    hg[:, 1] = f(a_k_norm_g)[0]
    hg[:, 2] = f(b_q_norm_g)[0]
    hg[:, 3:6] = f(b_k_norm_g)[0].T
    kk = np.arange(S_ALL)
    emast = (kk[None, :] // 64 == np.arange(128)[:, None]).astype(np.float32)
    nn = np.arange(512)[:, None]
    bb = np.arange(128)[None, :]
    ovl = ((nn * 16 <= bb * 64 + 63) & (nn * 16 + 31 >= bb * 64)).astype(np.float32)
    ovl = ovl.reshape(4, 128, 128).transpose(1, 0, 2).reshape(128, 512)
    shared = dict(
        w_in=f(w_in)[0], w_out=f(w_out)[0], w_gate=f(w_gate)[0], w_up=f(w_up)[0], w_down=f(w_down)[0],
        gmix=f(norm_mix_g), gout=f(out_norm_g), gffn=f(norm_ffn_g), hg=hg, sinks=f(a_sinks),
        posT=np.ascontiguousarray(f(cmp_pos_emb)[0].transpose(0, 2, 1)),
        cw1=f(cmp_w1)[0], cb1=np.ascontiguousarray(f(cmp_b1)[0].reshape(2, 2, 128).transpose(0, 2, 1)),
        cw2=f(cmp_w2)[0],
        cb2c=np.ascontiguousarray(np.stack([f(cmp_b2)[0, 0], f(cmp_b2)[0, 0]], axis=1)),
        cb2r=f(cmp_b2)[0, 1:2],
        convw=np.ascontiguousarray(f(conv_w)[0].T.reshape(NFC, 128, 3).transpose(1, 0, 2).reshape(128, NFC * 3)),
        convb=np.ascontiguousarray(f(conv_b)[0].reshape(NFC, 128).T),
        emast=emast, ovl=np.ascontiguousarray(ovl), identf=np.eye(128, dtype=np.float32),
    )
    in_maps = []
    for c in range(NCORES):
        pad = (7 - c) * TOK
        xr = np.zeros((S_ALL, D), np.float32)
        xr[pad:] = x[:S_ALL - pad]
        m = dict(shared)
        m["x"] = xr
        m.update(_host_tables(rel_bias, c))
        in_maps.append(m)
    return in_maps
```

```python
import contextlib
import math
import types
import numpy as np
import ml_dtypes
import concourse.bass as bass
import concourse.mybir as mybir
from concourse.bass_utils import run_bass_kernel_spmd

F32 = mybir.dt.float32
BF16 = mybir.dt.bfloat16
AF = mybir.ActivationFunctionType
ALU = mybir.AluOpType
AX = mybir.AxisListType

NCORES = 8
D = 4096
S_ALL = 8192
TOK = 1024
DFF = 11008
NFC = 86
NIN = 8240
EPS = 1e-6
NEGM = -1.0e4
SCALE = 128 ** -0.5
NQT = 9

ENGS = ("pe", "act", "dve", "pool", "sp")
NDMASEM = 8


def _freeze(fn):
    if fn.__closure__ is None:
        return fn
    cells = []
    for c in fn.__closure__:
        try:
            cells.append(types.CellType(c.cell_contents))
        except ValueError:
            cells.append(c)
    return types.FunctionType(fn.__code__, fn.__globals__, fn.__name__, fn.__defaults__, tuple(cells))


class Op:
    __slots__ = ("eng", "fn", "reads", "writes", "dma", "idx", "deps", "signal",
                 "count", "dsem", "dcount", "dprev")

    def __init__(self, eng, fn, reads, writes, dma):
        self.eng, self.fn, self.reads, self.writes, self.dma = eng, fn, reads, writes, dma
        self.deps = []
        self.signal = False
        self.count = 0
        self.dsem = None
        self.dcount = 0
        self.dprev = 0


class Sched:
    def __init__(self, nc):
        self.nc = nc
        self.ops = {e: [] for e in ENGS}
        self.lastw = {}
        self.readers = {}
        self.ndma = {e: 0 for e in ENGS}
        self.dsem_n = {}
        self.dsem_last = {}
        self.fence_deps = {}

    def add(self, eng, fn, reads=(), writes=(), dma=False):
        op = Op(eng, _freeze(fn), tuple(reads), tuple(writes), dma)
        op.idx = len(self.ops[eng])
        deps = set()
        for k in op.reads:
            w = self.lastw.get(k)
            if w is not None:
                deps.add(w)
        for k in op.writes:
            w = self.lastw.get(k)
            if w is not None:
                deps.add(w)
            for r in self.readers.get(k, ()):
                deps.add(r)
        if eng in self.fence_deps:
            deps.update(self.fence_deps.pop(eng))
        deps.discard(op)
        op.deps = list(deps)
        for k in op.reads:
            self.readers.setdefault(k, []).append(op)
        for k in op.writes:
            self.lastw[k] = op
            self.readers[k] = []
        if dma:
            i = self.ndma[eng]
            self.ndma[eng] += 1
            op.dsem = (eng, i % NDMASEM)
            n = self.dsem_n.get(op.dsem, 0)
            op.dprev = n
            op.dcount = n + 1
            self.dsem_n[op.dsem] = n + 1
            self.dsem_last[op.dsem] = op
        self.ops[eng].append(op)
        return op

    def fence(self):
        last = []
        for e in ENGS:
            for op in reversed(self.ops[e]):
                if not op.dma:
                    last.append(op)
                    break
        last.extend(self.dsem_last.values())
        for e in ENGS:
            self.fence_deps.setdefault(e, set()).update(last)
        self.lastw = {}
        self.readers = {}

    def emit(self):
        nc = self.nc
        for e in ENGS:
            for op in self.ops[e]:
                for d in op.deps:
                    if d.dma or (d.eng == e and e == "pe" and not op.dma):
                        continue
                    d.signal = True
        for e in ENGS:
            c = 0
            for op in self.ops[e]:
                if op.signal and not op.dma:
                    c += 1
                    op.count = c
        with contextlib.ExitStack() as st:
            csem = {e: st.enter_context(nc.semaphore("c_" + e)) for e in ENGS}
            dsem = {}
            for e in ENGS:
                if self.ndma[e]:
                    for i in range(NDMASEM):
                        dsem[(e, i)] = st.enter_context(nc.semaphore("d_%s%d" % (e, i)))
            block = st.enter_context(nc.Block())

            def run(e, eng):
                seen = {}

                def wait(sem, key, val):
                    if seen.get(key, 0) >= val:
                        return
                    seen[key] = val
                    eng.wait_ge(sem, val)

                for op in self.ops[e]:
                    for d in op.deps:
                        if d.dma:
                            wait(dsem[d.dsem], d.dsem, 16 * d.dcount)
                        elif d.eng == e and e == "pe" and not op.dma:
                            continue
                        else:
                            wait(csem[d.eng], d.eng, d.count)
                    if op.dma and op.dprev:
                        wait(dsem[op.dsem], op.dsem, 16 * op.dprev)
                    ins = op.fn(eng)
                    if op.dma:
                        ins.then_inc(dsem[op.dsem], 16)
                    elif op.signal:
                        ins.then_inc(csem[e], 1)
                if e == "sp":
                    for k, n in self.dsem_n.items():
                        eng.wait_ge(dsem[k], 16 * n)
                    for e2 in ENGS:
                        ops2 = [o for o in self.ops[e2] if o.signal and not o.dma]
                        if ops2:
                            eng.wait_ge(csem[e2], ops2[-1].count)

            @block.tensor
            def _(eng):
                run("pe", eng)

            @block.scalar
            def _(eng):
                run("act", eng)

            @block.vector
            def _(eng):
                run("dve", eng)

            @block.gpsimd
            def _(eng):
                run("pool", eng)

            @block.sync
            def _(eng):
                run("sp", eng)


class Arena:
    def __init__(self, t32, tbf, nwords):
        self.t32, self.tbf, self.n = t32, tbf, nwords
        self.off = 0
        self.uid = 0

    def mark(self):
        return self.off

    def reset(self, m=0):
        self.off = m

    def _take(self, words):
        o = self.off
        self.off += (words + 31) // 32 * 32
        assert self.off <= self.n, ("arena overflow", self.off, self.n)
        self.uid += 1
        return o

    def f32(self, n):
        o = self._take(n)
        return self.t32[:, o:o + n]

    def bf(self, n):
        o = self._take((n + 1) // 2)
        return self.tbf[:, 2 * o:2 * o + n]


def build_nc(dbg=False, a1_only=False, st_list=None):
    nc = bass.Bass("TRN2", target_bir_lowering=False)

    def din(name, shape, dt=F32):
        return nc.dram_tensor(name, list(shape), dt, kind="ExternalInput").ap()

    def dscr(name, shape, dt):
        return nc.dram_tensor(name, list(shape), dt, kind="ExternalOutput" if dbg else "Internal").ap()

    x = din("x", [S_ALL, D])
    w_in = din("w_in", [D, NIN])
    w_out = din("w_out", [D, D])
    w_gate = din("w_gate", [D, DFF])
    w_up = din("w_up", [D, DFF])
    w_down = din("w_down", [DFF, D])
    gmix = din("gmix", [1, D])
    gout = din("gout", [1, D])
    gffn = din("gffn", [1, D])
    hg = din("hg", [128, 8])
    sinks = din("sinks", [1, 16])
    posT = din("posT", [2, 128, 32])
    cw1 = din("cw1", [2, 4096, 256])
    cb1 = din("cb1", [2, 128, 2])
    cw2 = din("cw2", [2, 256, 128])
    cb2c = din("cb2c", [128, 2])
    cb2r = din("cb2r", [1, 128])
    convw = din("convw", [128, NFC * 3])
    convb = din("convb", [128, NFC])
    tabA = din("tabA", [4, 2, 128, 512])
    tabB = din("tabB", [4, 4, 128, 512])
    tabC = din("tabC", [NQT, 4, 128, 512])
    cfar = din("cfar", [128, NQT * 16])
    selV = din("selV", [128, NQT * 128])
    selA = din("selA", [128, NQT * 128])
    selF = din("selF", [128, NQT * 128])
    padc = din("padc", [128, 4])
    wpad = din("wpad", [128, 13])
    hflag = din("hflag", [128, 2])
    emast = din("emast", [128, S_ALL])
    ovl = din("ovl", [128, 4 * 128])
    identf = din("identf", [128, 128])
    y = nc.dram_tensor("y", [TOK, D], F32, kind="ExternalOutput").ap()

    qT_d = dscr("qT_d", [32, 128, 1536], BF16)
    akT_d = dscr("akT_d", [4, 128, 1536], BF16)
    av_d = dscr("av_d", [1536, 512], BF16)
    kwT_d = dscr("kwT_d", [4, 128, 2048], BF16)
    vw_d = dscr("vw_d", [2048, 512], BF16)
    ksT_d = dscr("ksT_d", [4, 128, S_ALL], BF16)
    vs_d = dscr("vs_d", [S_ALL, 512], BF16)
    kcr_d = dscr("kcr_d", [4, 128, S_ALL + 16], BF16)
    vcr_d = dscr("vcr_d", [4, 128, S_ALL + 16], BF16)
    kcT_d = dscr("kcT_d", [4, 128, 512], BF16)
    vc_d = dscr("vc_d", [4, 512, 128], BF16)
    o_d = dscr("o_d", [NQT * 128, D], BF16)
    x1_d = dscr("x1_d", [NQT * 128, D], F32)
    hT_d = dscr("hT_d", [NFC, 128, TOK], BF16)
    wkv_d = nc.dram_tensor("wkv_d", [4, 128, 32 * 512], BF16, kind="Internal").ap()

    S = Sched(nc)
    NW = 50500
    with contextlib.ExitStack() as st:
        ar32 = st.enter_context(nc.sbuf_tensor("arena", [128, NW], F32))
        AR = Arena(ar32, ar32.bitcast(BF16), NW)
        banks = [st.enter_context(nc.psum_tensor("pb%d" % i, [128, 512], F32)) for i in range(8)]
        banks_bf = [b.bitcast(BF16) for b in banks]
        uid = [0]

        def key(p):
            uid[0] += 1
            return "%s#%d" % (p, uid[0])

        ident = AR.bf(128)
        ones = AR.bf(128)
        onec = AR.bf(2)
        gates = AR.f32(12 * 48)
        ssqA = AR.f32(NQT * 4)
        ssqB = AR.f32(NQT * 4)
        ssqF = AR.f32(NQT * 8)
        hgc = AR.f32(8)
        hgq = AR.f32(2)
        small = AR.f32(64)
        epsc = AR.f32(2)
        base_mark = AR.mark()
        S.add("dve", lambda e: e.memset(epsc, EPS), writes=["epsc"])

        S.add("pool", lambda e: e.dma_start(out=ident, in_=identf[:, :]), writes=["ident"], dma=True)
        S.add("dve", lambda e: e.memset(ones, 1.0), writes=["ones"])
        S.add("dve", lambda e: e.memset(onec, 1.0), writes=["onec"])
        S.add("sp", lambda e: e.dma_start(out=hgc, in_=hg[:, :]), writes=["hgc"], dma=True)
        S.add("dve", lambda e: e.tensor_scalar(out=hgq[:, 0:1], in0=hgc[:, 0:1], scalar1=SCALE, scalar2=None,
                                               op0=ALU.mult), reads=["hgc"], writes=["hgq0"])
        S.add("dve", lambda e: e.tensor_scalar(out=hgq[:, 1:2], in0=hgc[:, 2:3], scalar1=SCALE, scalar2=None,
                                               op0=ALU.mult), reads=["hgc"], writes=["hgq1"])
        S.add("dve", lambda e: e.memset(ssqA, 0.0), writes=["ssqA"])
        S.add("dve", lambda e: e.memset(ssqB, 0.0), writes=["ssqB"])
        S.add("dve", lambda e: e.memset(ssqF, 0.0), writes=["ssqF"])

        evac_rr = [0]

        def evac_copy(out, in_, reads, writes):
            evac_rr[0] += 1
            if evac_rr[0] % 2:
                S.add("act", lambda e: e.copy(out=out, in_=in_), reads=reads, writes=writes)
            else:
                S.add("dve", lambda e: e.tensor_copy(out=out, in_=in_), reads=reads, writes=writes)

        def rstd_from(out, in_, inv_n, reads, writes, tmp):
            S.add("act", lambda e: e.activation(out=tmp, in_=in_, func=AF.Sqrt, scale=inv_n, bias=epsc[:, 0:1]),
                  reads=list(reads) + ["epsc"], writes=[writes[0] + "t"])
            S.add("dve", lambda e: e.reciprocal(out=out, in_=tmp), reads=[writes[0] + "t"], writes=writes)

        gbc = AR.f32(D)
        S.add("sp", lambda e: e.dma_start(out=gbc, in_=gmix[0].partition_broadcast(128)), writes=["gbc"], dma=True)
        xbuf = [AR.f32(D) for _ in range(2)]
        xs = AR.bf(D)
        hnT = AR.bf(32 * 512)
        hnT3 = hnT.rearrange("p (k t) -> p k t", t=512)
        wbuf = [AR.bf(32 * 512) for _ in range(2)]
        wbuf3 = [w.rearrange("p (k c) -> p k c", c=512) for w in wbuf]
        sqb = [AR.bf(512) for _ in range(2)]
        rsb = [AR.f32(512) for _ in range(2)]
        rtmp = [AR.f32(512) for _ in range(2)]
        stg = [AR.bf(512) for _ in range(3)]
        zer = AR.bf(16)
        stat = AR.f32(4)
        S.add("dve", lambda e: e.memset(zer, 0.0), writes=["zer"])
        for g in range(4):
            for dd in (kcr_d, vcr_d):
                S.add("sp", (lambda dd, g: lambda e: e.dma_start(out=dd[g][:, S_ALL:S_ALL + 16], in_=zer))(dd, g),
                      reads=["zer"], dma=True)

        for c4 in range(4):
            S.add("pool", (lambda c4: lambda e: e.dma_start(
                out=wkv_d[c4].rearrange("p (k c) -> p k c", c=512),
                in_=w_in[:, 5120 + c4 * 512:5120 + (c4 + 1) * 512].rearrange("(k p) c -> p k c", p=128)))(c4),
                writes=["wkv%d" % c4], dma=True)

        chunks = []
        for i in range(4):
            chunks.append((i * 512, 512, "q", 13, (i * 4, 0)))
        chunks.append((2048, 512, "k", 13, ("ak",)))
        chunks.append((2560, 512, "v", 13, ("av",)))
        for i in range(4):
            chunks.append((3072 + i * 512, 512, "q", 13, (16 + i * 4, 1)))
        chunks.append((5120, 512, "raw", 0, (kcr_d,)))
        chunks.append((5632, 512, "raw", 0, (vcr_d,)))
        chunks.append((6144, 512, "k", 0, ("ks",)))
        chunks.append((6656, 512, "v", 0, ("vs",)))
        chunks.append((7168, 512, "k", 12, ("kw",)))
        chunks.append((7680, 512, "v", 12, ("vw",)))
        chunks.append((8192, 48, "g", 13, ()))

        wcnt = [0]
        pscnt = [0]
        st3 = [0]

        def proj_chunk(sti, ch):
            c0, ncol, kind, _, meta = ch
            wi = wcnt[0] % 2
            wcnt[0] += 1
            wk = "w%d" % wi
            w3 = wbuf3[wi]
            if 5120 <= c0 < 7168:
                c4 = (c0 - 5120) // 512
                S.add("sp", lambda e: e.dma_start(out=w3, in_=wkv_d[c4].rearrange("p (k c) -> p k c", c=512)),
                      reads=["wkv%d" % c4], writes=[wk], dma=True)
            else:
                S.add("pool", lambda e: e.dma_start(out=w3[:, :, 0:ncol],
                                                    in_=w_in[:, c0:c0 + ncol].rearrange("(k p) c -> p k c", p=128)),
                      writes=[wk], dma=True)
            if kind in ("q", "k", "raw"):
                for hh in range(4):
                    bi = 2 + pscnt[0] % 3
                    pscnt[0] += 1
                    pb = banks[bi]
                    pk = "bank%d" % bi
                    for kc in range(32):
                        S.add("pe", (lambda kc, hh, pb: lambda e: e.matmul(
                            pb[:, :], lhsT=w3[:, kc, hh * 128:(hh + 1) * 128], rhs=hnT3[:, kc, :],
                            start=(kc == 0), stop=(kc == 31)))(kc, hh, pb),
                            reads=[wk, "hnT"], writes=[pk])
                    si = st3[0] % 3
                    st3[0] += 1
                    sg = stg[si]
                    sk = "stg%d" % si
                    if kind == "raw":
                        evac_copy(sg, pb[:, :], [pk], [sk])
                        dd = meta[0]
                        S.add("act", (lambda dd, hh, sg: lambda e: e.dma_start(
                            out=dd[hh][:, sti * 512:(sti + 1) * 512], in_=sg))(dd, hh, sg), reads=[sk], dma=True)
                        continue
                    j = si % 2
                    S.add("act", (lambda pb, j: lambda e: e.activation(out=sqb[j], in_=pb[:, :], func=AF.Square))(pb, j),
                          reads=[pk], writes=["sqb%d" % j])
                    b2 = 5 + j
                    S.add("pe", (lambda j, b2: lambda e: e.matmul(banks[b2][:, :], lhsT=ones, rhs=sqb[j],
                                                                  start=True, stop=True))(j, b2),
                          reads=["ones", "sqb%d" % j], writes=["bank%d" % b2])
                    rstd_from(rsb[j], banks[b2][:, :], 1.0 / 128, ["bank%d" % b2], ["rsb%d" % j], rtmp[j])
                    if kind == "q":
                        gcol = hgq[:, meta[1]:meta[1] + 1]
                        gk = "hgq%d" % meta[1]
                    else:
                        ci = {"ak": 1, "ks": 4, "kw": 5}[meta[0]]
                        gcol = hgc[:, ci:ci + 1]
                        gk = "hgc"
                    S.add("dve", (lambda pb, j, sg, gcol: lambda e: e.scalar_tensor_tensor(
                        out=sg, in0=pb[:, :], scalar=gcol, in1=rsb[j], op0=ALU.mult, op1=ALU.mult))(pb, j, sg, gcol),
                        reads=[pk, "rsb%d" % j, gk], writes=[sk])
                    if kind == "q":
                        dst = qT_d[meta[0] + hh][:, (sti - 13) * 512:(sti - 12) * 512]
                    elif meta[0] == "ak":
                        dst = akT_d[hh][:, (sti - 13) * 512:(sti - 12) * 512]
                    elif meta[0] == "ks":
                        dst = ksT_d[hh][:, sti * 512:(sti + 1) * 512]
                    else:
                        dst = kwT_d[hh][:, (sti - 12) * 512:(sti - 11) * 512]
                    S.add("act", (lambda dst, sg: lambda e: e.dma_start(out=dst, in_=sg))(dst, sg), reads=[sk], dma=True)
            else:
                for tt in range(4):
                    bi = 2 + pscnt[0] % 3
                    pscnt[0] += 1
                    pb = banks[bi]
                    pk = "bank%d" % bi
                    for kc in range(32):
                        S.add("pe", (lambda kc, tt, pb: lambda e: e.matmul(
                            pb[:, 0:ncol], lhsT=hnT3[:, kc, tt * 128:(tt + 1) * 128], rhs=w3[:, kc, 0:ncol],
                            start=(kc == 0), stop=(kc == 31)))(kc, tt, pb),
                            reads=[wk, "hnT"], writes=[pk])
                    if kind == "g":
                        ti = (sti - 13) * 4 + tt
                        S.add("act", (lambda pb, ti: lambda e: e.activation(
                            out=gates[:, ti * 48:(ti + 1) * 48], in_=pb[:, 0:48], func=AF.Sigmoid))(pb, ti),
                            reads=[pk], writes=["gates%d" % ti])
                        continue
                    si = st3[0] % 3
                    st3[0] += 1
                    sg = stg[si]
                    sk = "stg%d" % si
                    evac_copy(sg, pb[:, :], [pk], [sk])
                    r0 = sti * 512 + tt * 128
                    if meta[0] == "av":
                        dst = av_d[r0 - 13 * 512:r0 - 13 * 512 + 128, :]
                    elif meta[0] == "vs":
                        dst = vs_d[r0:r0 + 128, :]
                    else:
                        dst = vw_d[r0 - 12 * 512:r0 - 12 * 512 + 128, :]
                    S.add("act", (lambda dst, sg: lambda e: e.dma_start(out=dst, in_=sg))(dst, sg), reads=[sk], dma=True)

        def build_norm_T(src_rows, gb, dst3, tcol, xk_i, stat_ap, extra_scale=None):
            xb_ = xbuf[xk_i % 2]
            xk = "xbuf%d" % (xk_i % 2)
            S.add("sp", lambda e: e.dma_start(out=xb_, in_=src_rows), writes=[xk], dma=True)
            S.add("dve", lambda e: e.memset(stat_ap[:, 0:1], 0.0), writes=["stat0"])
            S.add("act", lambda e: e.activation(out=xs, in_=xb_, func=AF.Square, accum_out=stat_ap[:, 0:1]),
                  reads=[xk, "stat0"], writes=["xs", "stat0"])
            rstd_from(stat_ap[:, 1:2], stat_ap[:, 0:1], 1.0 / D, ["stat0"], ["stat1"], stat_ap[:, 2:3])
            S.add("dve", lambda e: e.scalar_tensor_tensor(out=xs, in0=xb_, scalar=stat_ap[:, 1:2], in1=gb,
                                                          op0=ALU.mult, op1=ALU.mult),
                  reads=[xk, "stat1", "gbc"], writes=["xs"])
            for q4 in range(4):
                bi = q4 % 2
                pbf = banks_bf[bi]
                pk = "bank%d" % bi
                for k8 in range(8):
                    kc = q4 * 8 + k8
                    S.add("pe", (lambda kc, k8, pbf: lambda e: e.transpose(
                        pbf[:, k8 * 128:(k8 + 1) * 128], xs[:, kc * 128:(kc + 1) * 128], ident))(kc, k8, pbf),
                        reads=["xs", "ident"], writes=[pk])
                yield q4, pbf, pk

        def norm_tile_to(src_rows, gb, dst3, dkey, tcol, xk_i, ncols=128, src_c0=0):
            for q4, pbf, pk in build_norm_T(src_rows, gb, dst3, tcol, xk_i, stat):
                evac_copy(dst3[:, q4 * 8:(q4 + 1) * 8, tcol:tcol + ncols],
                          pbf.rearrange("p (k t) -> p k t", t=128)[:, :, src_c0:src_c0 + ncols], [pk], [dkey])

        tile_i = 0
        hdbg = nc.dram_tensor("hdbg", [16, 128, 32 * 512], BF16, kind="ExternalOutput").ap() if a1_only else None
        for sti in (st_list if st_list is not None else range(16)):
            for tt in range(4):
                tl = sti * 4 + tt
                norm_tile_to(x[tl * 128:(tl + 1) * 128, :], gbc, hnT3, "hnT", tt * 128, tile_i)
                tile_i += 1
            if a1_only == 1:
                S.add("sp", (lambda sti: lambda e: e.dma_start(out=hdbg[sti], in_=hnT))(sti), reads=["hnT"], dma=True)
            for ch in chunks:
                if sti >= ch[3]:
                    proj_chunk(sti, ch)
        S.fence()
        if a1_only:
            S.emit()
            return nc

        AR.reset(base_mark)
        w1 = AR.bf(32 * 256)
        w13 = w1.rearrange("p (l h) -> p l h", h=256)
        w2 = AR.bf(2 * 128)
        w23 = w2.rearrange("p (c d) -> p c d", d=128)
        posb = AR.bf(32)
        b1c = AR.f32(2)
        c1 = AR.f32(2)
        b2c = AR.f32(2)
        b2r = AR.f32(128)
        rawT = [AR.bf(S_ALL + 16) for _ in range(2)]
        hid = AR.bf(2 * 512)
        hid3 = hid.rearrange("p (c n) -> p c n", n=512)
        u_ = AR.f32(512)
        t_ = AR.f32(512)
        sg_ = AR.f32(512)
        kraw = AR.f32(512)
        sq2 = AR.bf(512)
        rs2 = AR.f32(512)
        rt2 = AR.f32(512)
        kco = AR.bf(512)
        vco = AR.bf(128)
        S.add("sp", lambda e: e.dma_start(out=b2c, in_=cb2c[:, :]), writes=["b2c"], dma=True)
        S.add("sp", lambda e: e.dma_start(out=b2r, in_=cb2r[0].partition_broadcast(128)), writes=["b2r"], dma=True)
        GC = 2.0 * math.sqrt(2.0 / math.pi)
        ri = 0
        for kv in range(2):
            S.add("pool", (lambda kv: lambda e: e.dma_start(
                out=w13, in_=cw1[kv].rearrange("(l p) h -> p l h", p=128)))(kv), writes=["w1"], dma=True)
            S.add("pool", (lambda kv: lambda e: e.dma_start(
                out=w23, in_=cw2[kv].rearrange("(c p) d -> p c d", p=128)))(kv), writes=["w2"], dma=True)
            S.add("pool", (lambda kv: lambda e: e.dma_start(out=posb, in_=posT[kv]))(kv), writes=["posb"], dma=True)
            S.add("sp", (lambda kv: lambda e: e.dma_start(out=b1c, in_=cb1[kv]))(kv), writes=["b1c"], dma=True)
            for hc in range(2):
                for l in range(32):
                    S.add("pe", (lambda hc, l: lambda e: e.matmul(
                        banks[0][:, 0:1], lhsT=w13[:, l, hc * 128:(hc + 1) * 128], rhs=posb[:, l:l + 1],
                        start=(l == 0), stop=(l == 31)))(hc, l), reads=["w1", "posb"], writes=["bank0"])
                S.add("dve", (lambda hc: lambda e: e.tensor_tensor(
                    out=c1[:, hc:hc + 1], in0=banks[0][:, 0:1], in1=b1c[:, hc:hc + 1], op=ALU.add))(hc),
                    reads=["bank0", "b1c"], writes=["c1_%d" % hc])
            for g in range(4):
                rT = rawT[ri % 2]
                rk = "rawT%d" % (ri % 2)
                ri += 1
                src = (kcr_d if kv == 0 else vcr_d)[g]
                S.add("sp", (lambda rT, src: lambda e: e.dma_start(out=rT, in_=src))(rT, src), writes=[rk], dma=True)
                for hc in range(2):
                    pb = banks[1 + hc]
                    pk = "bank%d" % (1 + hc)
                    for l in range(32):
                        S.add("pe", (lambda hc, l, pb, rT: lambda e: e.matmul(
                            pb[:, :], lhsT=w13[:, l, hc * 128:(hc + 1) * 128], rhs=rT[:, l:l + 16 * 511 + 1:16],
                            start=(l == 0), stop=(l == 31)))(hc, l, pb, rT), reads=["w1", rk], writes=[pk])
                    S.add("act", (lambda hc, pb: lambda e: e.activation(
                        out=u_, in_=pb[:, :], func=AF.Identity, bias=c1[:, hc:hc + 1]))(hc, pb),
                        reads=[pk, "c1_%d" % hc], writes=["u_"])
                    S.add("dve", lambda e: e.tensor_tensor(out=t_, in0=u_, in1=u_, op=ALU.mult),
                          reads=["u_"], writes=["t_"])
                    S.add("dve", lambda e: e.tensor_scalar(out=t_, in0=t_, scalar1=0.044715, scalar2=1.0,
                                                           op0=ALU.mult, op1=ALU.add), reads=["t_"], writes=["t_"])
                    S.add("dve", lambda e: e.tensor_tensor(out=t_, in0=t_, in1=u_, op=ALU.mult),
                          reads=["t_", "u_"], writes=["t_"])
                    S.add("act", lambda e: e.activation(out=sg_, in_=t_, func=AF.Sigmoid, scale=GC),
                          reads=["t_"], writes=["sg_"])
                    S.add("dve", (lambda hc: lambda e: e.tensor_tensor(out=hid3[:, hc, :], in0=u_, in1=sg_,
                                                                       op=ALU.mult))(hc),
                          reads=["u_", "sg_"], writes=["hid%d" % hc])
                if kv == 0:
                    for hc in range(2):
                        S.add("pe", (lambda hc: lambda e: e.matmul(
                            banks[3][:, :], lhsT=w23[:, hc, :], rhs=hid3[:, hc, :], start=(hc == 0), stop=(hc == 1)))(hc),
                            reads=["w2", "hid%d" % hc], writes=["bank3"])
                    S.add("act", lambda e: e.activation(out=kraw, in_=banks[3][:, :], func=AF.Identity,
                                                        bias=b2c[:, 0:1]), reads=["bank3", "b2c"], writes=["kraw"])
                    S.add("act", lambda e: e.activation(out=sq2, in_=kraw, func=AF.Square),
                          reads=["kraw"], writes=["sq2"])
                    S.add("pe", lambda e: e.matmul(banks[4][:, :], lhsT=ones, rhs=sq2, start=True, stop=True),
                          reads=["ones", "sq2"], writes=["bank4"])
                    rstd_from(rs2, banks[4][:, :], 1.0 / 128, ["bank4"], ["rs2"], rt2)
                    S.add("dve", lambda e: e.scalar_tensor_tensor(out=kco, in0=kraw, scalar=hgc[:, 3:4], in1=rs2,
                                                                  op0=ALU.mult, op1=ALU.mult),
                          reads=["kraw", "rs2", "hgc"], writes=["kco"])
                    S.add("sp", (lambda g: lambda e: e.dma_start(out=kcT_d[g], in_=kco))(g), reads=["kco"], dma=True)
                else:
                    for t in range(4):
                        for hc in range(2):
                            S.add("pe", (lambda hc, t: lambda e: e.matmul(
                                banks[5][:, 0:128], lhsT=hid3[:, hc, t * 128:(t + 1) * 128], rhs=w23[:, hc, :],
                                start=(hc == 0), stop=(hc == 1)))(hc, t),
                                reads=["w2", "hid%d" % hc], writes=["bank5"])
                        S.add("dve", lambda e: e.tensor_tensor(out=vco, in0=banks[5][:, 0:128], in1=b2r, op=ALU.add),
                              reads=["bank5", "b2r"], writes=["vco"])
                        S.add("sp", (lambda g, t: lambda e: e.dma_start(out=vc_d[g][t * 128:(t + 1) * 128, :],
                                                                        in_=vco))(g, t), reads=["vco"], dma=True)
        S.fence()

        AR.reset(base_mark)
        tA = AR.bf(8 * 512)
        tA3 = tA.rearrange("p (t c) -> p t c", c=512)
        tB = AR.bf(16 * 512)
        tB3 = tB.rearrange("p (t c) -> p t c", c=512)
        em = AR.bf(S_ALL)
        ov = AR.bf(512)
        ov3 = ov.rearrange("p (t b) -> p t b", b=128)
        pdc = AR.f32(4)
        wpd = AR.f32(13)
        esk = AR.f32(16)
        cfp = AR.f32(NQT * 4 * 4)
        sV = AR.f32(NQT * 128)
        sA = AR.f32(NQT * 128)
        sF = AR.f32(NQT * 128)
        qT = AR.bf(4 * 1152)
        qT4 = qT.rearrange("p (q r t) -> p q r t", r=4, t=128)
        kT = AR.bf(S_ALL)
        vv = AR.bf(64 * 128)
        vv3 = vv.rearrange("p (t d) -> p t d", d=128)
        kw = AR.bf(13 * 128)
        vw = AR.bf(13 * 128)
        vw3 = vw.rearrange("p (t d) -> p t d", d=128)
        kc = AR.bf(512)
        vcs = AR.bf(512)
        vcs3 = vcs.rearrange("p (t d) -> p t d", d=128)
        tC = [AR.bf(512) for _ in range(2)]
        PT = [AR.bf(512) for _ in range(3)]
        addT = AR.bf(512)
        ob32 = AR.f32(512)
        obb = [AR.bf(512) for _ in range(2)]
        den = AR.f32(4)
        rden = AR.f32(4)
        sc = AR.f32(4)
        fin = AR.f32(128)
        fin2 = AR.f32(128)
        m8 = AR.f32(16)
        selb = AR.bf(128)
        junk = AR.bf(512)
        oTs = AR.f32(512)
        dns = AR.f32(512)
        idf = AR.f32(128)
        S.add("sp", lambda e: e.dma_start(out=idf, in_=identf[:, :]), writes=["idf"], dma=True)

        S.add("pool", lambda e: e.dma_start(out=tA3, in_=tabA.rearrange("g j k c -> k (g j) c")),
              writes=["tA"], dma=True)
        S.add("pool", lambda e: e.dma_start(out=tB3, in_=tabB.rearrange("g j k c -> k (g j) c")),
              writes=["tB"], dma=True)
        S.add("pool", lambda e: e.dma_start(out=em, in_=emast[:, :]), writes=["em"], dma=True)
        S.add("pool", lambda e: e.dma_start(out=ov, in_=ovl[:, :]), writes=["ov"], dma=True)
        S.add("sp", lambda e: e.dma_start(out=pdc, in_=padc[:, :]), writes=["pdc"], dma=True)
        S.add("sp", lambda e: e.dma_start(out=esk, in_=sinks[0].partition_broadcast(128)), writes=["esk"], dma=True)
        S.add("act", lambda e: e.activation(out=esk, in_=esk, func=AF.Exp), reads=["esk"], writes=["esk"])
        S.add("sp", lambda e: e.dma_start(out=cfp, in_=cfar[:, :]), writes=["cfp"], dma=True)
        S.add("sp", lambda e: e.dma_start(out=wpd, in_=wpad[:, :]), writes=["pdc"], dma=True)
        S.add("dve", lambda e: e.tensor_scalar(out=cfp, in0=cfp, scalar1=-NEGM, scalar2=None, op0=ALU.add),
              reads=["cfp"], writes=["cfp"])
        for nm, dst, src in (("sV", sV, selV), ("sA", sA, selA), ("sF", sF, selF)):
            S.add("sp", (lambda dst, src: lambda e: e.dma_start(out=dst, in_=src[:, :]))(dst, src),
                  writes=[nm], dma=True)

        ps_rr = [0]
        acc_rr = [0]
        pt_rr = [0]
        dcol = [0]
        ob_rr = [0]

        def attend(rhs_q, tiles, pad_bias=None, vs=False):
            ao = 2 + acc_rr[0] % 2
            acc_rr[0] += 1
            dc_ = (dcol[0] % 16) * 4
            dcol[0] += 1
            aok = "bank%d" % ao
            dk = "bank5_%d" % dc_
            n = len(tiles)
            stt = {}

            def s_part(ti):
                kap, kkeys, terms, vap, vkeys, xrhs = tiles[ti]
                bi = ps_rr[0] % 2
                ps_rr[0] += 1
                pb = banks[bi]
                pk = "bank%d" % bi
                S.add("pe", (lambda pb, kap, nt: lambda e: e.matmul(pb[:, :], lhsT=kap, rhs=rhs_q, start=True,
                                                                    stop=(nt == 0)))(pb, kap, len(terms)),
                      reads=list(kkeys) + ["qT"], writes=[pk])
                for i2, (tl, tr, tk) in enumerate(terms):
                    S.add("pe", (lambda pb, tl, tr, last: lambda e: e.matmul(pb[:, :], lhsT=tl, rhs=tr, start=False,
                                                                             stop=last))(pb, tl, tr, i2 == len(terms) - 1),
                          reads=list(tk), writes=[pk])
                pi = pt_rr[0] % 3
                pt_rr[0] += 1
                P = PT[pi]
                ptk = "PT%d" % pi
                if pad_bias is not None and pad_bias[ti] is not None:
                    S.add("act", (lambda P, pb, bb: lambda e: e.activation(out=P, in_=pb[:, :], func=AF.Exp, bias=bb))(
                        P, pb, pad_bias[ti]), reads=[pk, "pdc"], writes=[ptk])
                else:
                    S.add("act", (lambda P, pb: lambda e: e.activation(out=P, in_=pb[:, :], func=AF.Exp))(P, pb),
                          reads=[pk], writes=[ptk])
                stt[ti] = (P, ptk)

            def pv_part(ti):
                kap, kkeys, terms, vap, vkeys, xrhs = tiles[ti]
                P, ptk = stt.pop(ti)
                if vs:
                    S.add("pe", (lambda P, vap, ti: lambda e: e.matmul(banks[7][:, :], lhsT=vap, rhs=P, start=(ti == 0),
                                                                       stop=(ti == n - 1)))(P, vap, ti),
                          reads=[ptk] + list(vkeys), writes=["bank7"])
                    S.add("pe", (lambda P, ti: lambda e: e.matmul(banks[6][0:1, :], lhsT=onec[:, 0:1], rhs=P,
                                                                  start=(ti == 0), stop=(ti == n - 1)))(P, ti),
                          reads=[ptk, "onec"], writes=["bank6"])
                    return
                for r in range(4):
                    Pr = P[:, r * 128:(r + 1) * 128]
                    S.add("pe", (lambda Pr, r, vap, ti: lambda e: e.matmul(
                        banks[ao][:, r * 128:(r + 1) * 128], lhsT=Pr, rhs=vap, start=(ti == 0 and r == 0),
                        stop=(ti == n - 1)))(Pr, r, vap, ti), reads=[ptk] + list(vkeys), writes=[aok])
                    S.add("pe", (lambda Pr, r, ti: lambda e: e.matmul(
                        banks[5][:, dc_ + r:dc_ + r + 1], lhsT=Pr, rhs=onec[:, 0:1], start=(ti == 0 and r == 0),
                        stop=(ti == n - 1)))(Pr, r, ti), reads=[ptk, "onec"], writes=[dk])
                    if xrhs is not None:
                        S.add("pe", (lambda Pr, r, ti, xr: lambda e: e.matmul(
                            banks[4][:, r * 128:(r + 1) * 128], lhsT=Pr, rhs=xr, start=(ti == 0 and r == 0),
                            stop=(ti == n - 1)))(Pr, r, ti, xrhs), reads=[ptk, "ov"], writes=["bank4"])

            s_part(0)
            for ti in range(n):
                if ti + 1 < n:
                    s_part(ti + 1)
                pv_part(ti)
            if vs:
                S.add("act", lambda e: e.copy(out=oTs, in_=banks[7][:, :]), reads=["bank7"], writes=["oTs"])
                S.add("dve", lambda e: e.tensor_copy(out=dns[0:1, :], in_=banks[6][0:1, :]), reads=["bank6"],
                      writes=["dns"])
                for r in range(4):
                    S.add("pe", (lambda r: lambda e: e.transpose(banks[ao][:, r * 128:(r + 1) * 128],
                                                                 oTs[:, r * 128:(r + 1) * 128], idf))(r),
                          reads=["oTs", "idf"], writes=[aok])
                for r in range(4):
                    S.add("pe", (lambda r: lambda e: e.matmul(banks[5][:, dc_ + r:dc_ + r + 1],
                                                              lhsT=dns[0:1, r * 128:(r + 1) * 128], rhs=idf[0:1, 0:1],
                                                              start=True, stop=True))(r),
                          reads=["dns", "idf"], writes=[dk])
            return ao, dc_

        def finish(ao, dc_, gate_cols, first, add_sink=None):
            aok = "bank%d" % ao
            dk = "bank5_%d" % dc_
            if add_sink is not None:
                S.add("dve", lambda e: e.tensor_tensor(out=den, in0=banks[5][:, dc_:dc_ + 4], in1=add_sink, op=ALU.add),
                      reads=[dk, "esk"], writes=["den"])
            else:
                S.add("dve", lambda e: e.tensor_scalar(out=den, in0=banks[5][:, dc_:dc_ + 4], scalar1=1e-30,
                                                       scalar2=None, op0=ALU.max), reads=[dk], writes=["den"])
            S.add("dve", lambda e: e.reciprocal(out=rden, in_=den), reads=["den"], writes=["rden"])
            if gate_cols is not None:
                S.add("dve", lambda e: e.tensor_tensor(out=sc, in0=rden, in1=gate_cols, op=ALU.mult),
                      reads=["rden", "gatesall"], writes=["sc"])
                scal, sck = sc, "sc"
            else:
                scal, sck = rden, "rden"
            for r in range(4):
                o_ = ob32[:, r * 128:(r + 1) * 128]
                a_ = banks[ao][:, r * 128:(r + 1) * 128]
                if first:
                    S.add("dve", (lambda o_, a_, r: lambda e: e.tensor_scalar(
                        out=o_, in0=a_, scalar1=scal[:, r:r + 1], scalar2=None, op0=ALU.mult))(o_, a_, r),
                        reads=[aok, sck], writes=["ob32_%d" % r])
                else:
                    S.add("dve", (lambda o_, a_, r: lambda e: e.scalar_tensor_tensor(
                        out=o_, in0=a_, scalar=scal[:, r:r + 1], in1=o_, op0=ALU.mult, op1=ALU.add))(o_, a_, r),
                        reads=[aok, sck, "ob32_%d" % r], writes=["ob32_%d" % r])

        def emit_o(qidx, colblk, ssq_ap, ssq_key):
            oi = ob_rr[0] % 2
            ob_rr[0] += 1
            ob_ = obb[oi]
            okk = "obb%d" % oi
            S.add("pool", lambda e: e.tensor_copy(out=ob_, in_=ob32), reads=["ob32_%d" % r for r in range(4)],
                  writes=[okk])
            S.add("act", lambda e: e.activation(out=junk, in_=ob_, func=AF.Square, accum_out=ssq_ap),
                  reads=[okk], writes=["junk", ssq_key])
            S.add("sp", lambda e: e.dma_start(out=o_d[qidx * 128:(qidx + 1) * 128, colblk * 512:(colblk + 1) * 512],
                                              in_=ob_), reads=[okk], dma=True)

        for grp in range(8):
            isA = grp < 4
            g = grp % 4
            h0 = (0 if isA else 16) + 4 * g
            for r in range(4):
                S.add("sp", (lambda h0, r: lambda e: e.dma_start(
                    out=qT4[:, :, r, :], in_=qT_d[h0 + r][:, 384:1536].rearrange("d (q t) -> d q t", t=128)))(h0, r),
                    writes=["qT"], dma=True)
            if isA:
                S.add("sp", (lambda g: lambda e: e.dma_start(out=kT[:, 0:1280], in_=akT_d[g][:, 256:1536]))(g),
                      writes=["kT"], dma=True)
                S.add("sp", (lambda g: lambda e: e.dma_start(
                    out=vv3[:, 0:10, :], in_=av_d[256:1536, g * 128:(g + 1) * 128].rearrange("(t p) d -> p t d", p=128)))(g),
                    writes=["vv"], dma=True)
            else:
                S.add("sp", (lambda g: lambda e: e.dma_start(out=kT, in_=ksT_d[g]))(g), writes=["kT"], dma=True)
                S.add("sp", (lambda g: lambda e: e.dma_start(
                    out=vv3, in_=vs_d[:, g * 128:(g + 1) * 128].rearrange("(t p) d -> p t d", p=128)))(g),
                    writes=["vv"], dma=True)
                S.add("sp", (lambda g: lambda e: e.dma_start(out=kw, in_=kwT_d[g][:, 384:2048]))(g),
                      writes=["kw"], dma=True)
                S.add("sp", (lambda g: lambda e: e.dma_start(
                    out=vw3, in_=vw_d[384:2048, g * 128:(g + 1) * 128].rearrange("(t p) d -> p t d", p=128)))(g),
                    writes=["vw"], dma=True)
                S.add("sp", (lambda g: lambda e: e.dma_start(out=kc, in_=kcT_d[g]))(g), writes=["kc"], dma=True)
                S.add("sp", (lambda g: lambda e: e.dma_start(
                    out=vcs3, in_=vc_d[g].rearrange("(t p) d -> p t d", p=128)))(g), writes=["vcs"], dma=True)
            for qx in range(NQT):
                Q = 55 + qx
                rhs_q = qT[:, qx * 512:(qx + 1) * 512]
                if isA:
                    tiles = []
                    for j in (1, 0):
                        kt = Q - j - 54
                        tiles.append((kT[:, kt * 128:(kt + 1) * 128], ["kT"],
                                      [(ident, tA3[:, g * 2 + j, :], ["ident", "tA"])],
                                      vv3[:, kt, :], ["vv"], None))
                    ao, dc_ = attend(rhs_q, tiles, pad_bias=[wpd[:, Q - j - 51:Q - j - 50] for j in (1, 0)])
                    finish(ao, dc_, None, True, add_sink=esk[:, 4 * g:4 * g + 4])
                    emit_o(qx, g, ssqA[:, qx * 4 + g:qx * 4 + g + 1], key("ssqA"))
                    continue
                gt = gates[:, (3 + qx) * 48:(4 + qx) * 48].rearrange("p (h b) -> p h b", b=3)
                tci = (grp * NQT + qx) % 2
                tCc = tC[tci]
                S.add("pool", (lambda tCc, qx, g: lambda e: e.dma_start(out=tCc, in_=tabC[qx][g]))(tCc, qx, g),
                      writes=["tC%d" % tci], dma=True)
                tiles = []
                for t in range(4):
                    if t < 3:
                        term = (ident, tB3[:, g * 4 + 2, :], ["ident", "tB"])
                    else:
                        term = (ident, tCc, ["ident", "tC%d" % tci])
                    tiles.append((kc[:, t * 128:(t + 1) * 128], ["kc"], [term], vcs3[:, t, :], ["vcs"], ov3[:, t, :]))
                ao, dc_ = attend(rhs_q, tiles, pad_bias=[pdc[:, t:t + 1] for t in range(4)])
                finish(ao, dc_, gt[:, 4 * g:4 * g + 4, 0], True)
                for r in range(4):
                    a_ = banks[4][:, r * 128:(r + 1) * 128]
                    if r == 0:
                        S.add("dve", (lambda a_: lambda e: e.tensor_scalar(out=fin, in0=a_, scalar1=rden[:, 0:1],
                                                                           scalar2=None, op0=ALU.mult))(a_),
                              reads=["bank4", "rden"], writes=["fin"])
                    else:
                        S.add("dve", (lambda a_, r: lambda e: e.scalar_tensor_tensor(
                            out=fin, in0=a_, scalar=rden[:, r:r + 1], in1=fin, op0=ALU.mult, op1=ALU.add))(a_, r),
                            reads=["bank4", "rden", "fin"], writes=["fin"])
                S.add("dve", (lambda qx: lambda e: e.tensor_tensor(out=fin, in0=fin, in1=sV[:, qx * 128:(qx + 1) * 128],
                                                                   op=ALU.mult))(qx), reads=["fin", "sV"], writes=["fin"])
                S.add("dve", (lambda qx: lambda e: e.tensor_tensor(out=fin, in0=fin, in1=sA[:, qx * 128:(qx + 1) * 128],
                                                                   op=ALU.add))(qx), reads=["fin", "sA"], writes=["fin"])
                S.add("dve", lambda e: e.max(out=m8[:, 0:8], in_=fin), reads=["fin"], writes=["m8a"])
                S.add("dve", lambda e: e.match_replace(out=fin2, in_to_replace=m8[:, 0:8], in_values=fin,
                                                       imm_value=-1e30), reads=["fin", "m8a"], writes=["fin2"])
                S.add("dve", lambda e: e.max(out=m8[:, 8:16], in_=fin2), reads=["fin2"], writes=["m8b"])
                S.add("dve", lambda e: e.tensor_scalar(out=fin2, in0=fin, scalar1=m8[:, 15:16], scalar2=None,
                                                       op0=ALU.is_ge), reads=["fin", "m8b"], writes=["fin2"])
                S.add("dve", (lambda qx: lambda e: e.tensor_tensor(out=selb, in0=fin2, in1=sF[:, qx * 128:(qx + 1) * 128],
                                                                   op=ALU.mult))(qx), reads=["fin2", "sF"], writes=["selb"])
                S.add("pe", lambda e: e.transpose(banks_bf[6][:, 0:128], selb, ident), reads=["selb", "ident"],
                      writes=["bank6"])
                for r in range(4):
                    cc = (qx * 4 + g) * 4 + r
                    S.add("dve", (lambda r, cc: lambda e: e.tensor_scalar(
                        out=addT[:, r * 128:(r + 1) * 128], in0=banks_bf[6][:, 0:128], scalar1=cfp[:, cc:cc + 1],
                        scalar2=NEGM, op0=ALU.mult, op1=ALU.add))(r, cc), reads=["bank6", "cfp"], writes=["addT"])
                tiles = []
                for kt in range(0, Q + 1):
                    j = Q - kt
                    ek = em[:, kt * 128:(kt + 1) * 128]
                    if j == 0:
                        terms = [(ident, tB3[:, g * 4 + 0, :], ["ident", "tB"])]
                    elif j == 1:
                        terms = [(ek, addT, ["em", "addT"]), (ident, tB3[:, g * 4 + 1, :], ["ident", "tB"])]
                    else:
                        terms = [(ek, addT, ["em", "addT"])]
                    tiles.append((kT[:, kt * 128:(kt + 1) * 128], ["kT"], terms, vv3[:, kt, :], ["vv"], None))
                ao, dc_ = attend(rhs_q, tiles, vs=True)
                finish(ao, dc_, gt[:, 4 * g:4 * g + 4, 1], False)
                tiles = []
                for j in (4, 3, 2, 1, 0):
                    kt = Q - j - 51
                    tb = {0: 0, 1: 1, 2: 2, 3: 2, 4: 3}[j]
                    tiles.append((kw[:, kt * 128:(kt + 1) * 128], ["kw"],
                                  [(ident, tB3[:, g * 4 + tb, :], ["ident", "tB"])], vw3[:, kt, :], ["vw"], None))
                ao, dc_ = attend(rhs_q, tiles, pad_bias=[wpd[:, Q - j - 51:Q - j - 50] for j in (4, 3, 2, 1, 0)])
                finish(ao, dc_, gt[:, 4 * g:4 * g + 4, 2], False)
                emit_o(qx, 4 + g, ssqB[:, qx * 4 + g:qx * 4 + g + 1], key("ssqB"))
        S.fence()

        AR.reset(base_mark)
        gob = AR.f32(D)
        S.add("sp", lambda e: e.dma_start(out=gob, in_=gout[0].partition_broadcast(128)), writes=["gbc"], dma=True)
        onT = AR.bf(32 * 1152)
        onT3 = onT.rearrange("p (k t) -> p k t", t=1152)
        obf = [AR.bf(D) for _ in range(2)]
        xs2 = AR.bf(D)
        wo = [AR.bf(32 * 512) for _ in range(2)]
        wo3 = [w.rearrange("p (k c) -> p k c", c=512) for w in wo]
        xc = [AR.f32(512) for _ in range(3)]
        x1c = [AR.f32(512) for _ in range(3)]
        rA = AR.f32(NQT)
        rB = AR.f32(NQT)
        rtm = AR.f32(NQT)
        rtmB = AR.f32(NQT)
        junk2 = AR.bf(512)
        S.add("dve", lambda e: e.tensor_reduce(out=rA, in_=ssqA.rearrange("p (q g) -> p q g", g=4), axis=AX.X,
                                               op=ALU.add), reads=[], writes=["rA0"])
        S.add("dve", lambda e: e.tensor_reduce(out=rB, in_=ssqB.rearrange("p (q g) -> p q g", g=4), axis=AX.X,
                                               op=ALU.add), reads=[], writes=["rB0"])
        rstd_from(rA, rA, 1.0 / 2048, ["rA0"], ["rA"], rtm)
        rstd_from(rB, rB, 1.0 / 2048, ["rB0"], ["rB"], rtmB)
        for qx in range(NQT):
            ob_ = obf[qx % 2]
            okk = "obf%d" % (qx % 2)
            S.add("sp", (lambda ob_, qx: lambda e: e.dma_start(out=ob_, in_=o_d[qx * 128:(qx + 1) * 128, :]))(ob_, qx),
                  writes=[okk], dma=True)
            S.add("dve", (lambda ob_, qx: lambda e: e.scalar_tensor_tensor(
                out=xs2[:, 0:2048], in0=ob_[:, 0:2048], scalar=rA[:, qx:qx + 1], in1=gob[:, 0:2048],
                op0=ALU.mult, op1=ALU.mult))(ob_, qx), reads=[okk, "rA", "gbc"], writes=["xs2a"])
            S.add("dve", (lambda ob_, qx: lambda e: e.scalar_tensor_tensor(
                out=xs2[:, 2048:4096], in0=ob_[:, 2048:4096], scalar=rB[:, qx:qx + 1], in1=gob[:, 2048:4096],
                op0=ALU.mult, op1=ALU.mult))(ob_, qx), reads=[okk, "rB", "gbc"], writes=["xs2b"])
            for q4 in range(4):
                bi = q4 % 2
                pbf = banks_bf[bi]
                pk = "bank%d" % bi
                for k8 in range(8):
                    kc_ = q4 * 8 + k8
                    S.add("pe", (lambda kc_, k8, pbf: lambda e: e.transpose(
                        pbf[:, k8 * 128:(k8 + 1) * 128], xs2[:, kc_ * 128:(kc_ + 1) * 128], ident))(kc_, k8, pbf),
                        reads=["xs2a", "xs2b", "ident"], writes=[pk])
                evac_copy(onT3[:, q4 * 8:(q4 + 1) * 8, qx * 128:(qx + 1) * 128],
                          pbf.rearrange("p (k t) -> p k t", t=128), [pk], ["onT"])
        xi = 0
        for dc in range(8):
            wi = dc % 2
            S.add("pool", (lambda wi, dc: lambda e: e.dma_start(
                out=wo3[wi], in_=w_out[:, dc * 512:(dc + 1) * 512].rearrange("(k p) c -> p k c", p=128)))(wi, dc),
                writes=["wo%d" % wi], dma=True)
            for qx in range(NQT):
                bi = 2 + (dc * NQT + qx) % 3
                pb = banks[bi]
                pk = "bank%d" % bi
                for kc_ in range(32):
                    S.add("pe", (lambda kc_, pb, wi, qx: lambda e: e.matmul(
                        pb[:, :], lhsT=onT3[:, kc_, qx * 128:(qx + 1) * 128], rhs=wo3[wi][:, kc_, :],
                        start=(kc_ == 0), stop=(kc_ == 31)))(kc_, pb, wi, qx), reads=["onT", "wo%d" % wi], writes=[pk])
                j = xi % 3
                xi += 1
                S.add("sp", (lambda j, qx, dc: lambda e: e.dma_start(
                    out=xc[j], in_=x[(55 + qx) * 128:(56 + qx) * 128, dc * 512:(dc + 1) * 512]))(j, qx, dc),
                    writes=["xc%d" % j], dma=True)
                S.add("dve", (lambda j, pb: lambda e: e.tensor_tensor(out=x1c[j], in0=pb[:, :], in1=xc[j],
                                                                      op=ALU.add))(j, pb),
                      reads=[pk, "xc%d" % j], writes=["x1c%d" % j])
                S.add("act", (lambda j, qx, dc: lambda e: e.activation(
                    out=junk2, in_=x1c[j], func=AF.Square, accum_out=ssqF[:, qx * 8 + dc:qx * 8 + dc + 1]))(j, qx, dc),
                    reads=["x1c%d" % j], writes=["junk2", key("ssqF")])
                S.add("sp", (lambda j, qx, dc: lambda e: e.dma_start(
                    out=x1_d[qx * 128:(qx + 1) * 128, dc * 512:(dc + 1) * 512], in_=x1c[j]))(j, qx, dc),
                    reads=["x1c%d" % j], dma=True)
        S.fence()

        AR.reset(base_mark)
        gfb = AR.f32(D)
        S.add("sp", lambda e: e.dma_start(out=gfb, in_=gffn[0].partition_broadcast(128)), writes=["gbc"], dma=True)
        hfT = AR.bf(32 * 1026)
        hfT3 = hfT.rearrange("p (k t) -> p k t", t=1026)
        ffn_mark = AR.mark()
        xbuf = [AR.f32(D) for _ in range(2)]
        xs = AR.bf(D)
        rF = AR.f32(NQT)
        rtm2 = AR.f32(NQT)
        S.add("dve", lambda e: e.tensor_reduce(out=rF, in_=ssqF.rearrange("p (q g) -> p q g", g=8), axis=AX.X,
                                               op=ALU.add), reads=[], writes=["rF0"])
        rstd_from(rF, rF, 1.0 / D, ["rF0"], ["rF"], rtm2)
        for qx in range(NQT):
            xb_ = xbuf[qx % 2]
            xk = "xbuf%d" % (qx % 2)
            S.add("sp", (lambda xb_, qx: lambda e: e.dma_start(out=xb_, in_=x1_d[qx * 128:(qx + 1) * 128, :]))(xb_, qx),
                  writes=[xk], dma=True)
            S.add("dve", (lambda xb_, qx: lambda e: e.scalar_tensor_tensor(
                out=xs, in0=xb_, scalar=rF[:, qx:qx + 1], in1=gfb, op0=ALU.mult, op1=ALU.mult))(xb_, qx),
                reads=[xk, "rF", "gbc"], writes=["xs"])
            for q4 in range(4):
                bi = q4 % 2
                pbf = banks_bf[bi]
                pk = "bank%d" % bi
                for k8 in range(8):
                    kc_ = q4 * 8 + k8
                    S.add("pe", (lambda kc_, k8, pbf: lambda e: e.transpose(
                        pbf[:, k8 * 128:(k8 + 1) * 128], xs[:, kc_ * 128:(kc_ + 1) * 128], ident))(kc_, k8, pbf),
                        reads=["xs", "ident"], writes=[pk])
                src = pbf.rearrange("p (k t) -> p k t", t=128)
                if qx == 0:
                    evac_copy(hfT3[:, q4 * 8:(q4 + 1) * 8, 0:2], src[:, :, 126:128], [pk], ["hfT"])
                else:
                    c0 = 2 + (qx - 1) * 128
                    evac_copy(hfT3[:, q4 * 8:(q4 + 1) * 8, c0:c0 + 128], src, [pk], ["hfT"])
        S.fence()

        AR.reset(ffn_mark)
        cwc = AR.f32(NFC * 3)
        cbc = AR.f32(NFC)
        hfl = AR.f32(2)
        S.add("sp", lambda e: e.dma_start(out=hfl, in_=hflag[:, :]), writes=["hfl"], dma=True)
        S.add("sp", lambda e: e.dma_start(out=cwc, in_=convw[:, :]), writes=["cwc"], dma=True)
        S.add("sp", lambda e: e.dma_start(out=cbc, in_=convb[:, :]), writes=["cbc"], dma=True)
        wg = [AR.bf(32 * 256) for _ in range(2)]
        wu = [AR.bf(32 * 256) for _ in range(2)]
        wg3 = [w.rearrange("p (k c) -> p k c", c=256) for w in wg]
        wu3 = [w.rearrange("p (k c) -> p k c", c=256) for w in wu]
        gsb = [AR.f32(514) for _ in range(2)]
        tcv = [AR.f32(512) for _ in range(2)]
        ssb = [AR.f32(512) for _ in range(2)]
        hst = [AR.bf(1024) for _ in range(2)]
        u2 = 0
        for c2 in range(43):
            wi = c2 % 2
            S.add("pool", (lambda wi, c2: lambda e: e.dma_start(
                out=wg3[wi], in_=w_gate[:, c2 * 256:(c2 + 1) * 256].rearrange("(k p) c -> p k c", p=128)))(wi, c2),
                writes=["wg%d" % wi], dma=True)
            S.add("pool", (lambda wi, c2: lambda e: e.dma_start(
                out=wu3[wi], in_=w_up[:, c2 * 256:(c2 + 1) * 256].rearrange("(k p) c -> p k c", p=128)))(wi, c2),
                writes=["wu%d" % wi], dma=True)
            for sub in range(2):
                fc = c2 * 2 + sub
                hs = hst[fc % 2]
                hk = "hst%d" % (fc % 2)
                for half in range(2):
                    b0 = half * 512
                    u = u2 % 2
                    u2 += 1
                    bG, bU = banks[u * 3], banks[u * 3 + 1]
                    bH = banks[6 + u]
                    kG, kU, kH = "bank%d" % (u * 3), "bank%d" % (u * 3 + 1), "bank%d" % (6 + u)
                    for kc_ in range(32):
                        lg = wg3[wi][:, kc_, sub * 128:(sub + 1) * 128]
                        S.add("pe", (lambda kc_, lg, bG, b0: lambda e: e.matmul(
                            bG[:, :], lhsT=lg, rhs=hfT3[:, kc_, b0 + 2:b0 + 514], start=(kc_ == 0), stop=(kc_ == 31)))(
                            kc_, lg, bG, b0), reads=["wg%d" % wi, "hfT"], writes=[kG])
                        S.add("pe", (lambda kc_, lg, bH, b0: lambda e: e.matmul(
                            bH[:, 0:2], lhsT=lg, rhs=hfT3[:, kc_, b0:b0 + 2], start=(kc_ == 0), stop=(kc_ == 31)))(
                            kc_, lg, bH, b0), reads=["wg%d" % wi, "hfT"], writes=[kH])
                    for kc_ in range(32):
                        lu = wu3[wi][:, kc_, sub * 128:(sub + 1) * 128]
                        S.add("pe", (lambda kc_, lu, bU, b0: lambda e: e.matmul(
                            bU[:, :], lhsT=lu, rhs=hfT3[:, kc_, b0 + 2:b0 + 514], start=(kc_ == 0), stop=(kc_ == 31)))(
                            kc_, lu, bU, b0), reads=["wu%d" % wi, "hfT"], writes=[kU])
                    gs, tc_, ss = gsb[u], tcv[u], ssb[u]
                    S.add("act", (lambda gs, bH, half: lambda e: e.mul(out=gs[:, 0:2], in_=bH[:, 0:2],
                                                                       mul=hfl[:, half:half + 1]))(gs, bH, half),
                          reads=[kH, "hfl"], writes=["gsbh%d" % u])
                    S.add("act", (lambda gs, bG: lambda e: e.copy(out=gs[:, 2:514], in_=bG[:, :]))(gs, bG),
                          reads=[kG], writes=["gsbm%d" % u])
                    gk = ["gsbh%d" % u, "gsbm%d" % u]
                    S.add("dve", (lambda gs, tc_, fc: lambda e: e.tensor_scalar(
                        out=tc_, in0=gs[:, 2:514], scalar1=cwc[:, fc * 3 + 2:fc * 3 + 3], scalar2=None, op0=ALU.mult))(
                        gs, tc_, fc), reads=gk + ["cwc"], writes=["tcv%d" % u])
                    S.add("dve", (lambda gs, tc_, fc: lambda e: e.scalar_tensor_tensor(
                        out=tc_, in0=gs[:, 1:513], scalar=cwc[:, fc * 3 + 1:fc * 3 + 2], in1=tc_, op0=ALU.mult,
                        op1=ALU.add))(gs, tc_, fc), reads=gk + ["cwc", "tcv%d" % u], writes=["tcv%d" % u])
                    S.add("dve", (lambda gs, tc_, fc: lambda e: e.scalar_tensor_tensor(
                        out=tc_, in0=gs[:, 0:512], scalar=cwc[:, fc * 3:fc * 3 + 1], in1=tc_, op0=ALU.mult,
                        op1=ALU.add))(gs, tc_, fc), reads=gk + ["cwc", "tcv%d" % u], writes=["tcv%d" % u])
                    S.add("act", (lambda tc_, ss, fc: lambda e: e.activation(
                        out=ss, in_=tc_, func=AF.Silu, bias=cbc[:, fc:fc + 1]))(tc_, ss, fc),
                        reads=["tcv%d" % u, "cbc"], writes=["ssb%d" % u])
                    S.add("dve", (lambda ss, bU, hs, b0: lambda e: e.tensor_tensor(
                        out=hs[:, b0:b0 + 512], in0=bU[:, :], in1=ss, op=ALU.mult))(ss, bU, hs, b0),
                        reads=["ssb%d" % u, kU], writes=[hk + "_%d" % half])
                S.add("sp", (lambda hs, fc: lambda e: e.dma_start(out=hT_d[fc], in_=hs))(hs, fc),
                      reads=[hk + "_0", hk + "_1"], dma=True)
        S.fence()

        AR.reset(base_mark)
        wd = [AR.bf(4 * 512) for _ in range(2)]
        wd3 = [w.rearrange("p (f c) -> p f c", c=512) for w in wd]
        hb = [AR.bf(4 * 1024) for _ in range(2)]
        hb3 = [h.rearrange("p (f t) -> p f t", t=1024) for h in hb]
        x1b = [AR.f32(512) for _ in range(3)]
        ob = [AR.f32(512) for _ in range(3)]
        li = 0
        oi = 0
        for dc in range(8):
            for fg in range(22):
                nf = 4 if fg < 21 else 2
                i2 = li % 2
                li += 1
                S.add("pool", (lambda i2, fg, nf, dc: lambda e: e.dma_start(
                    out=wd3[i2][:, 0:nf, :],
                    in_=w_down[fg * 512:fg * 512 + nf * 128, dc * 512:(dc + 1) * 512].rearrange("(f p) c -> p f c", p=128)))(
                    i2, fg, nf, dc), writes=["wd%d" % i2], dma=True)
                S.add("sp", (lambda i2, fg, nf: lambda e: e.dma_start(
                    out=hb3[i2][:, 0:nf, :], in_=hT_d[fg * 4:fg * 4 + nf].rearrange("f p t -> p f t")))(i2, fg, nf),
                    writes=["hb%d" % i2], dma=True)
                for f in range(nf):
                    fc = fg * 4 + f
                    for tl in range(8):
                        S.add("pe", (lambda i2, f, tl, fc: lambda e: e.matmul(
                            banks[tl][:, :], lhsT=hb3[i2][:, f, tl * 128:(tl + 1) * 128], rhs=wd3[i2][:, f, :],
                            start=(fc == 0), stop=(fc == NFC - 1)))(i2, f, tl, fc),
                            reads=["wd%d" % i2, "hb%d" % i2], writes=["bank%d" % tl])
            for tl in range(8):
                j = oi % 3
                oi += 1
                S.add("sp", (lambda j, tl, dc: lambda e: e.dma_start(
                    out=x1b[j], in_=x1_d[(tl + 1) * 128:(tl + 2) * 128, dc * 512:(dc + 1) * 512]))(j, tl, dc),
                    writes=["x1b%d" % j], dma=True)
                S.add("dve", (lambda j, tl: lambda e: e.tensor_tensor(out=ob[j], in0=banks[tl][:, :], in1=x1b[j],
                                                                      op=ALU.add))(j, tl),
                      reads=["bank%d" % tl, "x1b%d" % j], writes=["ob%d" % j])
                S.add("sp", (lambda j, tl, dc: lambda e: e.dma_start(
                    out=y[tl * 128:(tl + 1) * 128, dc * 512:(dc + 1) * 512], in_=ob[j]))(j, tl, dc),
                    reads=["ob%d" % j], dma=True)
        S.emit()
    return nc


def _bucket(dist):
    n = np.maximum(dist, 0)
    nf = np.maximum(n, 1).astype(np.float32)
    large = 16 + (np.log(nf / np.float32(16)) / np.float32(math.log(128 / 16)) * np.float32(16)).astype(np.int32)
    large = np.minimum(large, 31)
    return np.where(n < 16, n, large)


def _host_tables(rel_bias, c):
    rb = np.asarray(rel_bias, np.float32)
    k = np.arange(128)[:, None]
    q = np.arange(128)[None, :]
    tabA = np.full((4, 2, 128, 4, 128), NEGM, np.float32)
    tabB = np.full((4, 4, 128, 4, 128), NEGM, np.float32)
    for g in range(4):
        for r in range(4):
            hA = 4 * g + r
            hB = 16 + 4 * g + r
            for j in range(2):
                dist = q - k + 128 * j
                ok = (dist >= 0) & (dist < 128)
                tabA[g, j, :, r, :] = np.where(ok, rb[_bucket(dist), hA], NEGM)
            dist = q - k
            tabB[g, 0, :, r, :] = np.where(dist >= 0, rb[_bucket(dist), hB], NEGM)
            dist = q - k + 128
            tabB[g, 1, :, r, :] = rb[_bucket(dist), hB]
            tabB[g, 2, :, r, :] = rb[31, hB]
            tabB[g, 3, :, r, :] = np.where(k > q, rb[31, hB], NEGM)
    tabC = np.full((NQT, 4, 128, 4, 128), NEGM, np.float32)
    cfar = np.zeros((NQT, 4, 128, 4), np.float32)
    npad_blk = (7 - c) * 16
    selV = np.zeros((NQT, 128, 128), np.float32)
    selA = np.zeros((NQT, 128, 128), np.float32)
    selF = np.zeros((NQT, 128, 128), np.float32)
    blk = np.arange(128)[None, :]
    for qx in range(NQT):
        Q = 55 + qx
        qs = Q * 128 + np.arange(128)
        nn = 384 + np.arange(128)
        dist = qs[None, :] - (16 * nn[:, None] + 31)
        for g in range(4):
            for r in range(4):
                hB = 16 + 4 * g + r
                tabC[qx, g, :, r, :] = np.where(dist >= 0, rb[_bucket(dist), hB], NEGM)
                cfar[qx, g, :, r] = np.where(np.arange(128) < 2 * Q - 2, rb[31, hB], 0.0)
        cur = (qs // 64)[:, None]
        valid = (blk <= cur) & (blk >= npad_blk)
        forced = ((blk == npad_blk) | (blk == cur) | (blk == cur - 1)) & valid
        selF[qx] = valid
        selV[qx] = valid & ~forced
        selA[qx] = np.where(forced, 1e6 + blk * 16.0, np.where(valid, 0.0, -1.0))
    padc = np.zeros((128, 4), np.float32)
    for t in range(4):
        n_ = t * 128 + np.arange(128)
        padc[:, t] = np.where(n_ >= (7 - c) * 64, 0.0, NEGM)
    wpad = np.zeros((128, 13), np.float32)
    for i in range(13):
        wpad[:, i] = 0.0 if (51 + i) * 128 >= (7 - c) * TOK else NEGM
    hflag = np.ones((128, 2), np.float32)
    if c == 0:
        hflag[:, 0] = 0.0
    lay = lambda a: np.ascontiguousarray(a.transpose(1, 0, 2).reshape(128, -1))
    cf = np.ascontiguousarray(cfar.transpose(2, 0, 1, 3).reshape(128, NQT * 16))
    return dict(tabA=tabA.reshape(4, 2, 128, 512), tabB=tabB.reshape(4, 4, 128, 512),
                tabC=tabC.reshape(NQT, 4, 128, 512), cfar=cf, selV=lay(selV), selA=lay(selA), selF=lay(selF),
                padc=padc, wpad=wpad, hflag=hflag)


_NC_CACHE = {}


def kernel(x, rel_bias, norm_mix_g, w_in, a_q_norm_g, a_k_norm_g, a_sinks, b_q_norm_g, b_k_norm_g,
           cmp_pos_emb, cmp_w1, cmp_b1, cmp_w2, cmp_b2, out_norm_g, w_out, norm_ffn_g, w_gate, w_up,
           conv_w, conv_b, w_down):
    if "nc" not in _NC_CACHE:
        _NC_CACHE["nc"] = build_nc()
    nc = _NC_CACHE["nc"]
    in_maps = _prep(x, rel_bias, norm_mix_g, w_in, a_q_norm_g, a_k_norm_g, a_sinks, b_q_norm_g, b_k_norm_g,
                    cmp_pos_emb, cmp_w1, cmp_b1, cmp_w2, cmp_b2, out_norm_g, w_out, norm_ffn_g, w_gate, w_up,
                    conv_w, conv_b, w_down)
    res = run_bass_kernel_spmd(nc, in_maps, core_ids=list(range(NCORES)))
    out = np.concatenate([np.asarray(r["y"], np.float32) for r in res.results], axis=0)
    return out[None]


def _prep(x, rel_bias, norm_mix_g, w_in, a_q_norm_g, a_k_norm_g, a_sinks, b_q_norm_g, b_k_norm_g,
          cmp_pos_emb, cmp_w1, cmp_b1, cmp_w2, cmp_b2, out_norm_g, w_out, norm_ffn_g, w_gate, w_up,
          conv_w, conv_b, w_down):
    f = lambda a: np.ascontiguousarray(np.asarray(a, np.float32))
    x = f(x)[0]
    hg = np.zeros((128, 8), np.float32)
    hg[:, 0] = f(a_q_norm_g)[0]
    hg[:, 1] = f(a_k_norm_g)[0]
    hg[:, 2] = f(b_q_norm_g)[0]
    hg[:, 3:6] = f(b_k_norm_g)[0].T
    kk = np.arange(S_ALL)
    emast = (kk[None, :] // 64 == np.arange(128)[:, None]).astype(np.float32)
    nn = np.arange(512)[:, None]
    bb = np.arange(128)[None, :]
    ovl = ((nn * 16 <= bb * 64 + 63) & (nn * 16 + 31 >= bb * 64)).astype(np.float32)
    ovl = ovl.reshape(4, 128, 128).transpose(1, 0, 2).reshape(128, 512)
    shared = dict(
        w_in=f(w_in)[0], w_out=f(w_out)[0], w_gate=f(w_gate)[0], w_up=f(w_up)[0], w_down=f(w_down)[0],
        gmix=f(norm_mix_g), gout=f(out_norm_g), gffn=f(norm_ffn_g), hg=hg, sinks=f(a_sinks),
        posT=np.ascontiguousarray(f(cmp_pos_emb)[0].transpose(0, 2, 1)),
        cw1=f(cmp_w1)[0], cb1=np.ascontiguousarray(f(cmp_b1)[0].reshape(2, 2, 128).transpose(0, 2, 1)),
        cw2=f(cmp_w2)[0],
        cb2c=np.ascontiguousarray(np.stack([f(cmp_b2)[0, 0], f(cmp_b2)[0, 0]], axis=1)),
        cb2r=f(cmp_b2)[0, 1:2],
        convw=np.ascontiguousarray(f(conv_w)[0].T.reshape(NFC, 128, 3).transpose(1, 0, 2).reshape(128, NFC * 3)),
        convb=np.ascontiguousarray(f(conv_b)[0].reshape(NFC, 128).T),
        emast=emast, ovl=np.ascontiguousarray(ovl), identf=np.eye(128, dtype=np.float32),
    )
    in_maps = []
    for c in range(NCORES):
        pad = (7 - c) * TOK
        xr = np.zeros((S_ALL, D), np.float32)
        xr[pad:] = x[:S_ALL - pad]
        m = dict(shared)
        m["x"] = xr
        m.update(_host_tables(rel_bias, c))
        in_maps.append(m)
    return in_maps
```
